# Optimizing a Trainium2 kernel written in Bass

```python
import math
import jax, jax.numpy as jnp
from jax import lax
import numpy as np

D_MODEL = 1024
BATCH = 1
SEQ = 16384
DEPTH = 1
DEC_BATCH = 16
DEC_SEQ = 2048
PAST_LEN = 128

HEAD_DIM_A = 64
V_DIM_A = 2 * HEAD_DIM_A
N_HEADS_A = D_MODEL // V_DIM_A
D_A = N_HEADS_A * V_DIM_A
QK_A = N_HEADS_A * 2 * HEAD_DIM_A
Q_BLOCK = 128
N_HEADS_B = 4
DK_B = D_MODEL // 2
DV_B = D_MODEL
KEY_DIM_B = DK_B // N_HEADS_B
V_DIM_B = DV_B // N_HEADS_B
GATE_RANK = 16
GATE_NORM = 16.0
CHUNK = 64
N_BUCKETS = 32
MAX_DISTANCE = 128
D_FF = 4 * D_MODEL
EPS = 1e-6

IN_SPLIT = [QK_A, QK_A, D_A, DK_B, DK_B, DV_B, DV_B, GATE_RANK, GATE_RANK, D_MODEL, D_MODEL]
IN_OFFSETS = [int(v) for v in np.cumsum(IN_SPLIT)[:-1]]
D_IN = int(sum(IN_SPLIT))

kernel_name = "hybrid_diffattn_gla_encoder"


def rms_norm(x, g):
    xf = x.astype(jnp.float32)
    y = xf * lax.rsqrt(jnp.mean(xf * xf, axis=-1, keepdims=True) + EPS)
    return (y * g.astype(jnp.float32)).astype(x.dtype)


def t5_bucket(rel):
    nb = N_BUCKETS // 2
    max_exact = nb // 2
    ret = (rel > 0).astype(jnp.int32) * nb
    n = jnp.abs(rel)
    nf = jnp.maximum(n, 1).astype(jnp.float32)
    large = max_exact + (jnp.log(nf / max_exact) / math.log(MAX_DISTANCE / max_exact)
                         * (nb - max_exact)).astype(jnp.int32)
    large = jnp.minimum(large, nb - 1)
    return ret + jnp.where(n < max_exact, n, large)


def diff_attention(q, k, v, lam, rel_bias):
    B, S = q.shape[0], q.shape[1]
    nblk = S // Q_BLOCK
    scale = HEAD_DIM_A ** -0.5
    qb = q.reshape(B, nblk, Q_BLOCK, N_HEADS_A, 2, HEAD_DIM_A).transpose(1, 0, 2, 3, 4, 5)
    kpos = jnp.arange(S, dtype=jnp.int32)
    vf = v.astype(jnp.float32)

    def one_block(args):
        q_blk, i = args
        qpos = i * Q_BLOCK + jnp.arange(Q_BLOCK, dtype=jnp.int32)
        bias = rel_bias[t5_bucket(kpos[None, :] - qpos[:, None])]
        bias = bias.transpose(2, 0, 1).astype(jnp.float32)
        s = jnp.einsum('bqhmd,bkhmd->bmhqk', q_blk, k).astype(jnp.float32) * scale + bias
        p = jax.nn.softmax(s, axis=-1)
        w = p[:, 0] - lam * p[:, 1]
        return jnp.einsum('bhqk,bkhd->bqhd', w, vf)

    o = lax.map(one_block, (qb, jnp.arange(nblk, dtype=jnp.int32)))
    return o.transpose(1, 0, 2, 3, 4).reshape(B, S, N_HEADS_A, V_DIM_A)


def gla_scan(q, k, v, g):
    B, S, H, dk = q.shape
    dv = v.shape[-1]
    nc = S // CHUNK

    def to_chunks(t):
        return t.reshape(B, nc, CHUNK, H, t.shape[-1]).transpose(1, 0, 3, 2, 4)

    mask = jnp.tril(jnp.ones((CHUNK, CHUNK), dtype=bool))[:, :, None]

    def step(state, inp):
        qc, kc, vc, gc = inp
        b = jnp.cumsum(gc, axis=2)
        o_inter = jnp.einsum('bhcd,bhde->bhce', qc * jnp.exp(b), state)
        diff = b[:, :, :, None, :] - b[:, :, None, :, :]
        decay = jnp.where(mask, jnp.exp(jnp.minimum(diff, 0.0)), 0.0)
        a = jnp.einsum('bhid,bhjd,bhijd->bhij', qc, kc, decay)
        o = o_inter + jnp.einsum('bhij,bhje->bhie', a, vc)
        b_last = b[:, :, -1:, :]
        state = (jnp.exp(b_last[:, :, 0, :, None]) * state
                 + jnp.einsum('bhjd,bhje->bhde', kc * jnp.exp(b_last - b), vc))
        return state, o

    s0 = jnp.zeros((B, H, dk, dv), jnp.float32)
    _, o = lax.scan(step, s0, (to_chunks(q), to_chunks(k), to_chunks(v), to_chunks(g)))
    return o.transpose(1, 0, 3, 2, 4).reshape(B, S, H, dv)


def bidirectional_gla(q, k, v, g_fwd, g_bwd):
    o_f = gla_scan(q, k, v, g_fwd)
    flip = lambda t: jnp.flip(t, axis=1)
    o_b = flip(gla_scan(flip(q), flip(k), flip(v), flip(g_bwd)))
    return o_f + o_b


def encoder_layer(x, c, layer_idx, rel_bias, w_ada, b_ada, norm1_g, w_in, q_norm_g, k_norm_g,
                  lam_q1, lam_k1, lam_q2, lam_k2, subln_g, w_gate_f, b_gate_f, w_gate_b, b_gate_b,
                  gla_norm_g, w_branch_a, w_branch_b, w_out, norm2_g, w_up, w_down):
    B, S, _ = x.shape
    mod = jax.nn.silu(c) @ w_ada + b_ada
    shift1, scale1, gate1, shift2, scale2, gate2 = jnp.split(mod, 6, axis=-1)

    h = rms_norm(x, norm1_g) * (1.0 + scale1[:, None]) + shift1[:, None]
    proj = h @ w_in
    qa, ka, va, qg, kg, vg, og, lr_f, lr_b, ga, gb = jnp.split(proj, IN_OFFSETS, axis=-1)

    lambda_init = 0.8 - 0.6 * math.exp(-0.3 * layer_idx)
    lam = (jnp.exp(jnp.sum(lam_q1.astype(jnp.float32) * lam_k1.astype(jnp.float32)))
           - jnp.exp(jnp.sum(lam_q2.astype(jnp.float32) * lam_k2.astype(jnp.float32))) + lambda_init)
    qa = rms_norm(qa.reshape(B, S, N_HEADS_A, 2, HEAD_DIM_A), q_norm_g)
    ka = rms_norm(ka.reshape(B, S, N_HEADS_A, 2, HEAD_DIM_A), k_norm_g)
    oa = diff_attention(qa, ka, va.reshape(B, S, N_HEADS_A, V_DIM_A), lam, rel_bias)
    oa = (rms_norm(oa, subln_g) * (1.0 - lambda_init)).astype(x.dtype)
    ya = oa.reshape(B, S, D_A) @ w_branch_a

    f32 = jnp.float32
    g_f = jax.nn.log_sigmoid((lr_f @ w_gate_f + b_gate_f).astype(f32)) / GATE_NORM
    g_b = jax.nn.log_sigmoid((lr_b @ w_gate_b + b_gate_b).astype(f32)) / GATE_NORM
    hk = lambda t: t.reshape(B, S, N_HEADS_B, KEY_DIM_B)
    ob = bidirectional_gla(hk(qg.astype(f32)) * KEY_DIM_B ** -0.5, hk(kg.astype(f32)),
                           vg.astype(f32).reshape(B, S, N_HEADS_B, V_DIM_B), hk(g_f), hk(g_b))
    ob = rms_norm(ob, gla_norm_g) * jax.nn.silu(og.astype(f32)).reshape(B, S, N_HEADS_B, V_DIM_B)
    yb = ob.astype(x.dtype).reshape(B, S, DV_B) @ w_branch_b

    merged = jax.nn.sigmoid(ga) * ya + jax.nn.sigmoid(gb) * yb
    x = x + gate1[:, None] * (merged @ w_out)

    h2 = rms_norm(x, norm2_g) * (1.0 + scale2[:, None]) + shift2[:, None]
    u = jax.nn.relu(h2 @ w_up)
    x = x + gate2[:, None] * ((u * u) @ w_down)
    return x


def setup_inputs(seed: int = 0) -> dict:
    key = jax.random.key(seed)
    ks = jax.random.split(key, 32)
    nrm = lambda k, shape, s: jax.random.normal(k, shape, jnp.float32) * s
    L = DEPTH
    return {
        "x_prompt": nrm(ks[0], (BATCH, SEQ, D_MODEL), 1.0),
        "x_sample": nrm(ks[1], (DEC_BATCH, DEC_SEQ, D_MODEL), 1.0),
        "c_prompt": nrm(ks[2], (BATCH, D_MODEL), 1.0),
        "c_sample": nrm(ks[3], (DEC_BATCH, D_MODEL), 1.0),
        "rel_bias": nrm(ks[4], (N_BUCKETS, N_HEADS_A), 0.5),
        "w_ada": nrm(ks[5], (L, D_MODEL, 6 * D_MODEL), 0.2 * D_MODEL ** -0.5),
        "b_ada": nrm(ks[6], (L, 6 * D_MODEL), 0.02),
        "norm1_g": 1.0 + nrm(ks[7], (L, D_MODEL), 0.02),
        "w_in": nrm(ks[8], (L, D_MODEL, D_IN), D_MODEL ** -0.5),
        "q_norm_g": 1.0 + nrm(ks[9], (L, HEAD_DIM_A), 0.02),
        "k_norm_g": 1.0 + nrm(ks[10], (L, HEAD_DIM_A), 0.02),
        "lam_q1": nrm(ks[11], (L, HEAD_DIM_A), 0.1),
        "lam_k1": nrm(ks[12], (L, HEAD_DIM_A), 0.1),
        "lam_q2": nrm(ks[13], (L, HEAD_DIM_A), 0.1),
        "lam_k2": nrm(ks[14], (L, HEAD_DIM_A), 0.1),
        "subln_g": 1.0 + nrm(ks[15], (L, V_DIM_A), 0.02),
        "w_gate_f": nrm(ks[16], (L, GATE_RANK, DK_B), GATE_RANK ** -0.5),
        "b_gate_f": nrm(ks[17], (L, DK_B), 0.1),
        "w_gate_b": nrm(ks[18], (L, GATE_RANK, DK_B), GATE_RANK ** -0.5),
        "b_gate_b": nrm(ks[19], (L, DK_B), 0.1),
        "gla_norm_g": 1.0 + nrm(ks[20], (L, V_DIM_B), 0.02),
        "w_branch_a": nrm(ks[21], (L, D_A, D_MODEL), D_A ** -0.5),
        "w_branch_b": nrm(ks[22], (L, DV_B, D_MODEL), DV_B ** -0.5),
        "w_out": nrm(ks[23], (L, D_MODEL, D_MODEL), D_MODEL ** -0.5),
        "norm2_g": 1.0 + nrm(ks[24], (L, D_MODEL), 0.02),
        "w_up": nrm(ks[25], (L, D_MODEL, D_FF), D_MODEL ** -0.5),
        "w_down": nrm(ks[26], (L, D_FF, D_MODEL), D_FF ** -0.5),
    }


def reference(x_prompt, x_sample, c_prompt, c_sample, rel_bias, w_ada, b_ada, norm1_g, w_in,
              q_norm_g, k_norm_g, lam_q1, lam_k1, lam_q2, lam_k2, subln_g, w_gate_f, b_gate_f,
              w_gate_b, b_gate_b, gla_norm_g, w_branch_a, w_branch_b, w_out, norm2_g, w_up, w_down):
    def run_trunk(x, c):
        for l in range(DEPTH):
            x = encoder_layer(x, c, l, rel_bias, w_ada[l], b_ada[l], norm1_g[l], w_in[l],
                              q_norm_g[l], k_norm_g[l], lam_q1[l], lam_k1[l], lam_q2[l], lam_k2[l],
                              subln_g[l], w_gate_f[l], b_gate_f[l], w_gate_b[l], b_gate_b[l],
                              gla_norm_g[l], w_branch_a[l], w_branch_b[l], w_out[l], norm2_g[l],
                              w_up[l], w_down[l])
        return x

    y_prompt = run_trunk(x_prompt, c_prompt)
    y_sample = run_trunk(x_sample, c_sample)
    return (y_prompt, y_sample)
```

```python
import math
import numpy as np
import concourse.bass as bass
import concourse.mybir as mybir
from concourse.bass_utils import run_bass_kernel_spmd

F32 = mybir.dt.float32
BF16 = mybir.dt.bfloat16
AF = mybir.ActivationFunctionType
ALU = mybir.AluOpType

D = 1024
NH = 8
HB = 4
DIN = 8224
DFF = 4096
EPS = 1e-6
C_QA, C_KA, C_VA, C_QG, C_KG, C_VG, C_OG, C_LR, C_GA, C_GB = 0, 1024, 2048, 3072, 3584, 4096, 5120, 6144, 6176, 7200
NEG = -30000.0


class Buf:
    __slots__ = ("name", "w", "r", "pw", "pr", "excl")

    def __init__(self, name, excl=False):
        self.name = name
        self.excl = excl
        self.w = []
        self.r = []
        self.pw = []
        self.pr = []


class Op:
    __slots__ = ("eng", "fn", "deps", "needed", "val", "dma", "dsem", "idx")

    def __init__(self, eng, fn, dma):
        self.eng = eng
        self.fn = fn
        self.deps = []
        self.needed = False
        self.val = 0
        self.dma = dma
        self.dsem = -1
        self.idx = 0


ENGS = ("pe", "act", "dve", "pool", "sp")
NDSEM = 6
import os
STORE_Q = os.environ.get('STORE_Q', 'pool')


class Sched:
    def __init__(self):
        self.ops = {e: [] for e in ENGS}
        self.dma_ops = []
        self.out_dmas = []
        self.last_barrier_idx = {e: 0 for e in ENGS}

    def op(self, eng, fn, r=(), w=(), dma=False, pw=()):
        o = Op(eng, fn, dma)
        deps = []
        for b in r:
            deps.extend(b.w)
            if b.excl:
                deps.extend(d for d in b.r if d.eng != eng)
        for b in w:
            deps.extend(b.w)
            deps.extend(b.r)
        for b in pw:
            deps.extend(b.pw)
            deps.extend(b.pr)
        seen = set()
        for d in deps:
            if d is o or id(d) in seen:
                continue
            seen.add(id(d))
            if d.eng == "pe" and eng == "pe" and not d.dma:
                continue
            d.needed = True
            o.deps.append(d)
        for b in r:
            b.r.append(o)
        for b in w:
            b.pw = b.w
            b.pr = b.r
            b.w = [o]
            b.r = []
        for b in pw:
            b.w.append(o)
        o.idx = len(self.ops[eng])
        self.ops[eng].append(o)
        if dma:
            self.dma_ops.append(o)
        return o

    def barrier(self):
        lasts = []
        for e in ENGS:
            for o in reversed(self.ops[e]):
                if not o.dma:
                    lasts.append(o)
                    break
        dmas = list(self.dma_ops)
        self.dma_ops = []
        for e in ENGS:
            o = Op(e, None, False)
            for d in lasts + dmas:
                if d.eng == e and not d.dma:
                    continue
                d.needed = True
                o.deps.append(d)
            o.idx = len(self.ops[e])
            self.ops[e].append(o)

    def emit(self, nc, stack):
        sems = {e: stack.enter_context(nc.semaphore("sem_" + e)) for e in ENGS}
        dsems = {}
        for e in ("sp", "pool", "act"):
            dsems[e] = [stack.enter_context(nc.semaphore(f"dsem_{e}_{i}")) for i in range(NDSEM)]
        for e in ENGS:
            cnt = 0
            dcnt = [0] * NDSEM
            k = 0
            for o in self.ops[e]:
                if o.dma:
                    o.dsem = k % NDSEM
                    dcnt[o.dsem] += 16
                    o.val = dcnt[o.dsem]
                    k += 1
                elif o.fn is not None and o.needed:
                    cnt += 1
                    o.val = cnt
        engobj = {"pe": "tensor", "act": "scalar", "dve": "vector", "pool": "gpsimd", "sp": "sync"}
        block = stack.enter_context(nc.Block())

        def run(ename):
            def body(e):
                waited = {}
                for o in self.ops[ename]:
                    if o.dma:
                        if o.val > 16:
                            key = ("d", ename, o.dsem)
                            if waited.get(key, 0) < o.val - 16:
                                e.wait_ge(dsems[ename][o.dsem], o.val - 16)
                                waited[key] = o.val - 16
                    for d in o.deps:
                        if d.dma:
                            key = ("d", d.eng, d.dsem)
                            s = dsems[d.eng][d.dsem]
                        else:
                            key = ("c", d.eng)
                            s = sems[d.eng]
                        if waited.get(key, 0) < d.val:
                            e.wait_ge(s, d.val)
                            waited[key] = d.val
                    if o.fn is None:
                        continue
                    ins = o.fn(e)
                    if o.dma:
                        ins.then_inc(dsems[ename][o.dsem], 16)
                    elif o.needed:
                        ins.then_inc(sems[ename], 1)
            return body

        block.sync(run("sp"))
        block.gpsimd(run("pool"))
        block.scalar(run("act"))
        block.vector(run("dve"))
        block.tensor(run("pe"))


class Mem:
    def __init__(self, nc, lo=16384, hi=212000):
        self.nc = nc
        self.lo = lo
        self.hi = hi
        self.cur = lo
        self.n = 0
        self.pers = lo

    def alloc(self, shape, dt, nbuf=1, name=None):
        nb = int(np.prod(shape[1:])) * (4 if dt == F32 else 2)
        nb = (nb + 63) // 64 * 64
        self.n += 1
        t = self.nc.alloc_sbuf_tensor_at(f"{name or 't'}_{self.n}", list(shape), dt, offset=self.cur)
        self.cur += nb
        assert self.cur <= self.hi, f"SBUF overflow {self.cur} {name}"
        return t

    def persist(self):
        self.pers = self.cur

    def reset(self):
        self.cur = self.pers


class Tl:
    __slots__ = ("h", "b")

    def __init__(self, h, name, excl=False):
        self.h = h
        self.b = Buf(name, excl)


class Rot:
    def __init__(self, tiles):
        self.t = tiles
        self.i = 0

    def next(self):
        t = self.t[self.i % len(self.t)]
        self.i += 1
        return t


class K:
    def __init__(self, L, debug=False, nphase=99):
        self.nphase = nphase
        assert L % 512 == 0
        self.L = L
        self.T = L // 128
        self.NOT = 7 * self.T + 1
        self.NO = self.NOT * 128
        self.NF = 3 * L
        self.NTK = self.NF + self.NO
        self.debug = debug
        self.nc = bass.Bass("TRN2", target_bir_lowering=False)
        self.S = Sched()
        self.mem = Mem(self.nc)
        self.ntl = 0

    def tile(self, shape, dt, name="t"):
        self.ntl += 1
        return Tl(self.mem.alloc(shape, dt, name=name), f"{name}{self.ntl}")

    def tiles(self, n, shape, dt, name="t"):
        return Rot([self.tile(shape, dt, name) for _ in range(n)])

    def dram_in(self, name, shape, dt=F32):
        return self.nc.dram_tensor(name, list(shape), dt, kind="ExternalInput").ap()

    def dram_scr(self, name, shape, dt):
        kind = "ExternalOutput" if self.debug else "Internal"
        if self.debug:
            self.dbg_names.append(name)
        return self.nc.dram_tensor(name, list(shape), dt, kind=kind).ap()

    def mm(self, out, lhsT, rhs, start, stop, r, w, pw=()):
        self.S.op("pe", lambda e: e.matmul(out, lhsT=lhsT, rhs=rhs, start=start, stop=stop), r, w, pw=pw)

    def tr(self, out, in_, ident, r, w, pw=()):
        self.S.op("pe", lambda e: e.transpose(out, in_, ident), r, w, pw=pw)

    def act(self, out, in_, func, r, w, scale=1.0, bias=0.0, accum=None, pw=()):
        if accum is None:
            self.S.op("act", lambda e: e.activation(out=out, in_=in_, func=func, scale=scale, bias=bias), r, w, pw=pw)
        else:
            self.S.op("act", lambda e: e.activation(out=out, in_=in_, func=func, scale=scale, bias=bias,
                                                    accum_out=accum), r, w, pw=pw)

    def ts(self, eng, out, in0, s1, op0, r, w, s2=None, op1=None, pw=()):
        if s2 is None:
            self.S.op(eng, lambda e: e.tensor_scalar(out=out, in0=in0, scalar1=s1, scalar2=None, op0=op0), r, w, pw=pw)
        else:
            self.S.op(eng, lambda e: e.tensor_scalar(out=out, in0=in0, scalar1=s1, scalar2=s2, op0=op0, op1=op1), r, w, pw=pw)

    def tt(self, eng, out, in0, in1, op, r, w, pw=()):
        self.S.op(eng, lambda e: e.tensor_tensor(out=out, in0=in0, in1=in1, op=op), r, w, pw=pw)

    def stt(self, out, in0, scalar, in1, op0, op1, r, w, pw=()):
        self.S.op("dve", lambda e: e.scalar_tensor_tensor(out=out, in0=in0, scalar=scalar, in1=in1, op0=op0, op1=op1), r, w, pw=pw)

    def cp(self, eng, out, in_, r, w, pw=()):
        if eng == "act":
            self.S.op("act", lambda e: e.copy(out=out, in_=in_), r, w, pw=pw)
        else:
            self.S.op(eng, lambda e: e.tensor_copy(out=out, in_=in_), r, w, pw=pw)

    def dma(self, q, out, in_, r, w, pw=()):
        return self.S.op(q, lambda e: e.dma_start(out=out, in_=in_), r, w, dma=True, pw=pw)

    def load(self, out, in_, w, r=(), pw=()):
        return self.dma("sp", out, in_, r, w, pw=pw)

    def store(self, out, in_, r, w=()):
        return self.dma(STORE_Q, out, in_, r, w)

    def memset(self, eng, ap, val, w):
        self.S.op(eng, lambda e: e.memset(ap, val), (), w)

    def hilo(self, src, srcb, hi, hib, lo, lob, eng="dve"):
        self.cp(eng, hi, src, [srcb], [hib])
        self.tt("dve", lo, src, hi, ALU.subtract, [srcb, hib], [lob])

    def wload(self, dst, src_v, nk, c0, c1, l0=None):
        if l0 is None:
            l0 = c0
        first = not dst.b.w
        for kc in range(nk):
            for a in range(c0, c1, 2048):
                b = min(c1, a + 2048)
                la = l0 + (a - c0)
                if first:
                    self.dma("pool", dst.h[:, kc, la:la + (b - a)], src_v[:, kc, a:b], (), [dst.b])
                    first = False
                else:
                    self.dma("pool", dst.h[:, kc, la:la + (b - a)], src_v[:, kc, a:b], (), (), pw=[dst.b])

    def build(self):
        nc, L, T, NOT, NO, NF, NTK = self.nc, self.L, self.T, self.NOT, self.NO, self.NF, self.NTK
        self.dbg_names = []
        I = {}
        I["xf"] = self.dram_in("xf", [NF, D])
        I["xo"] = self.dram_in("xo", [NO, D])
        I["cT"] = self.dram_in("cT", [128, 8 * 4])
        I["flg"] = self.dram_in("flg", [128, 3 * NOT])
        I["w_ada"] = self.dram_in("w_ada", [D, 6 * D])
        I["b_adaT"] = self.dram_in("b_adaT", [128, 48])
        I["b_ada"] = self.dram_in("b_ada", [1, 6 * D])
        I["n1T"] = self.dram_in("n1T", [128, 8])
        I["n2T"] = self.dram_in("n2T", [128, 8])
        I["w_in"] = self.dram_in("w_in", [D, DIN])
        I["qkg"] = self.dram_in("qkg", [128, 2])
        I["lamv"] = self.dram_in("lamv", [1, 256])
        I["subg"] = self.dram_in("subg", [128, 1])
        I["wgf"] = self.dram_in("wgf", [33, 512])
        I["wgb"] = self.dram_in("wgb", [33, 512])
        I["glng"] = self.dram_in("glng", [128, 2])
        I["w_ba"] = self.dram_in("w_ba", [D, D])
        I["w_bb"] = self.dram_in("w_bb", [D, D])
        I["w_out"] = self.dram_in("w_out", [D, D])
        I["w_up"] = self.dram_in("w_up", [D, DFF])
        I["w_dn"] = self.dram_in("w_dn", [DFF, D])
        I["rb"] = self.dram_in("rb", [32, 8])
        I["rbrow"] = self.dram_in("rbrow", [1, 256])
        I["oh"] = self.dram_in("oh", [32, 1280])
        I["cst"] = self.dram_in("cst", [128, 6 * 128])
        self.I = I
        self.y = nc.dram_tensor("y", [NF, D], F32, kind="ExternalOutput").ap()

        Sc = {}
        Sc["KT"] = self.dram_scr("KT", [8, 128, NTK], BF16)
        Sc["VH"] = self.dram_scr("VH", [8, 128, NTK // 128, 128], BF16)
        Sc["QT"] = self.dram_scr("QT", [8, 128, NF], BF16)
        Sc["GQT"] = self.dram_scr("GQT", [4, 128, NF], BF16)
        Sc["GKT"] = self.dram_scr("GKT", [4, 128, NF], BF16)
        Sc["GK"] = self.dram_scr("GK", [NF, 512], BF16)
        Sc["GV"] = self.dram_scr("GV", [NF, 1024], BF16)
        for nm in ("GGFH", "GGFL", "GGBH", "GGBL"):
            Sc[nm] = self.dram_scr(nm, [NF, 512], BF16)
        Sc["OGT"] = self.dram_scr("OGT", [8, 128, NF], BF16)
        Sc["GAT"] = self.dram_scr("GAT", [8, 128, NF], BF16)
        Sc["GBT"] = self.dram_scr("GBT", [8, 128, NF], BF16)
        Sc["OAT"] = self.dram_scr("OAT", [8, 128, NF], BF16)
        Sc["OBT"] = self.dram_scr("OBT", [8, 128, NF], BF16)
        Sc["X1"] = self.dram_scr("X1", [NF, D], F32)
        Sc["H2T"] = self.dram_scr("H2T", [8, 128, NF], BF16)
        Sc["GBC"] = self.dram_scr("GBC", [6, 128, D], F32)
        Sc["HT"] = self.dram_scr("HT", [8, 128, NTK], BF16)
        self.frev_t = nc.dram_tensor("FREV", [8, 1280], F32, kind="ExternalOutput" if self.debug else "Internal")
        if self.debug:
            self.dbg_names.append("FREV")
        Sc["SIN"] = self.dram_scr("SIN", [8, 128, 256], F32)
        self.Sc = Sc

        self.pspair = [Tl(nc.alloc_psum_tensor(f"pp{i}", [128, 1024], F32), f"pp{i}", excl=True) for i in range(4)]
        self.psum = [Tl(self.pspair[i // 2].h[:, (i % 2) * 512:(i % 2 + 1) * 512], f"ps{i}", excl=True) for i in range(8)]
        self.prot = Rot(self.psum)

        phases = [self.phase0, lambda: self.phase1(0), lambda: self.phase1(1), lambda: self.phase1(2),
                  self.phaseB, self.phaseC, self.phaseD1, self.phaseD2]
        for ph in phases[:self.nphase]:
            ph()
            self.S.barrier()
        return nc

    def phase0(self):
        nc, I, Sc, NOT = self.nc, self.I, self.Sc, self.NOT
        P = self
        cst = self.tile([128, 6 * 128], F32, "cst")
        self.load(cst.h[:], I["cst"][:, :], [cst.b])
        self.cst = cst
        self.ident = cst.h[:, 0:128]
        self.J32 = cst.h[:, 128:256]
        self.triF = cst.h[:, 256:384]
        self.triB = cst.h[:, 384:512]
        self.blk64 = cst.h[:, 512:640]
        self.ones32 = cst.h[:, 640:768]
        cstb = self.tile([128, 6 * 128], BF16, "cstb")
        self.cstb = cstb
        self.cp("dve", cstb.h[:], cst.h[:], [cst.b], [cstb.b])
        self.identb = cstb.h[:, 0:128]
        self.Jb = cstb.h[:, 128:256]
        self.triFb = cstb.h[:, 256:384]
        self.triBb = cstb.h[:, 384:512]
        self.blk64b = cstb.h[:, 512:640]
        self.onesb = cstb.h[:, 640:768]
        sm = self.tile([128, 64], F32, "sm")
        self.sm = sm
        self.load(sm.h[:, 0:8], I["n1T"][:, :], [sm.b])
        self.load(sm.h[:, 8:16], I["n2T"][:, :], [sm.b])
        self.load(sm.h[:, 16:18], I["qkg"][:, :], [sm.b])
        self.load(sm.h[:, 18:19], I["subg"][:, :], [sm.b])
        self.load(sm.h[:, 19:21], I["glng"][:, :], [sm.b])
        self.ts("dve", sm.h[:, 21:22], sm.h[:, 18:19], 0.8, ALU.mult, [sm.b], [sm.b])
        self.memset("dve", sm.h[:, 22:23], 0.0, [sm.b])
        amod = self.tile([128, 4 * 32], F32, "amod")
        self.amod = amod
        self.pers_late = self.mem.cur
        rbb = self.tile([128, 256], F32, "rbb")
        self.rbb = rbb
        self.load(rbb.h[:], I["rbrow"][0:1, :].partition_broadcast(128), [rbb.b])
        flg = self.tile([128, 3 * NOT], F32, "flg")
        self.flg = flg
        self.load(flg.h[:], I["flg"][:, :], [flg.b])
        nfl = self.tile([128, 2 * NOT], F32, "nfl")
        self.nfl = nfl
        self.ts("dve", nfl.h[:], flg.h[:, 0:2 * NOT], -1.0 / 16.0, ALU.mult, [flg.b], [nfl.b])
        fb = self.tile([128, 8 * NOT], F32, "fb")
        self.fb = fb
        for h in range(8):
            o = fb.h[:, h * NOT:(h + 1) * NOT]
            self.ts("dve", o, flg.h[:, 0:NOT], rbb.h[:, 15 * 8 + h:15 * 8 + h + 1], ALU.mult, [flg.b, rbb.b], [fb.b])
            self.stt(o, flg.h[:, NOT:2 * NOT], rbb.h[:, 31 * 8 + h:31 * 8 + h + 1], o, ALU.mult, ALU.add, [flg.b, rbb.b, fb.b], [fb.b])
            self.stt(o, flg.h[:, 2 * NOT:3 * NOT], NEG, o, ALU.mult, ALU.add, [flg.b, fb.b], [fb.b])
        wgh = self.tile([33, 1024], BF16, "wgh")
        wgl = self.tile([33, 1024], BF16, "wgl")
        self.wgh, self.wgl = wgh, wgl
        self.mem.persist()
        modT = self.tile([128, 6 * 8 * 4], F32, "modT")
        self.modT = modT
        lamt = self.tile([128, 256 + 8], F32, "lamt")
        self.load(lamt.h[:, 0:256], I["lamv"][0:1, :].partition_broadcast(128), [lamt.b])
        self.tt("dve", lamt.h[:, 0:64], lamt.h[:, 0:64], lamt.h[:, 64:128], ALU.mult, [lamt.b], [lamt.b])
        self.tt("dve", lamt.h[:, 128:192], lamt.h[:, 128:192], lamt.h[:, 192:256], ALU.mult, [lamt.b], [lamt.b])
        self.S.op("dve", lambda e: e.reduce_sum(out=lamt.h[:, 256:257], in_=lamt.h[:, 0:64], axis=mybir.AxisListType.X), [lamt.b], [lamt.b])
        self.S.op("dve", lambda e: e.reduce_sum(out=lamt.h[:, 257:258], in_=lamt.h[:, 128:192], axis=mybir.AxisListType.X), [lamt.b], [lamt.b])
        self.act(lamt.h[:, 258:260], lamt.h[:, 256:258], AF.Exp, [lamt.b], [lamt.b])
        self.tt("dve", lamt.h[:, 260:261], lamt.h[:, 259:260], lamt.h[:, 258:259], ALU.subtract, [lamt.b], [lamt.b])
        self.ts("dve", sm.h[:, 23:24], lamt.h[:, 260:261], -0.2, ALU.add, [lamt.b], [sm.b])
        self.neglam = sm.h[:, 23:24]

        wg = self.tile([33, 1024], F32, "wg")
        self.load(wg.h[:, 0:512], I["wgf"][:, :], [wg.b])
        self.load(wg.h[:, 512:1024], I["wgb"][:, :], [wg.b])
        self.hilo(wg.h[:], wg.b, wgh.h[:], wgh.b, wgl.h[:], wgl.b)
        cT = self.tile([128, 32], F32, "cT")
        self.load(cT.h[:], I["cT"][:, :], [cT.b])
        scT = self.tile([128, 32], F32, "scT")
        self.act(scT.h[:], cT.h[:], AF.Silu, [cT.b], [scT.b])
        badT = self.tile([128, 48], F32, "badT")
        self.load(badT.h[:], I["b_adaT"][:, :], [badT.b])
        screp = self.tile([128, 3 * 8 * 128], F32, "screp")
        for s in range(3):
            for kc in range(8):
                self.cp("dve", screp.h[:, (s * 8 + kc) * 128:(s * 8 + kc + 1) * 128],
                        scT.h[:, kc * 4 + s:kc * 4 + s + 1].to_broadcast([128, 128]), [scT.b], [screp.b])
        wa = self.tiles(2, [128, 8, 1024], F32, "wa")
        w_ada_v = I["w_ada"].rearrange("(kc p) c -> p kc c", p=128)
        bbc = self.tile([128, 1024], F32, "bbc")
        gst = self.tiles(2, [128, 1024], F32, "gst")
        for blk in range(6):
            w = wa.next()
            for kc in range(8):
                self.load(w.h[:, kc, :], w_ada_v[:, kc, blk * 1024:(blk + 1) * 1024], [w.b])
            if blk in (2, 5):
                self.load(bbc.h[:], I["b_ada"][0:1, blk * 1024:(blk + 1) * 1024].partition_broadcast(128), [bbc.b])
                for s in range(3):
                    g = gst.next()
                    for half in range(2):
                        ps = self.prot.next()
                        for kc in range(8):
                            self.mm(ps.h[:, :], screp.h[:, (s * 8 + kc) * 128:(s * 8 + kc + 1) * 128],
                                    w.h[:, kc, half * 512:(half + 1) * 512], kc == 0, kc == 7, [screp.b, w.b], [ps.b])
                        self.tt("dve", g.h[:, half * 512:(half + 1) * 512], ps.h[:, :], bbc.h[:, half * 512:(half + 1) * 512],
                                ALU.add, [ps.b, bbc.b], [g.b])
                    self.store(Sc["GBC"][(0 if blk == 2 else 3) + s, :, :], g.h[:], [g.b])
            else:
                for j in range(8):
                    ps = self.prot.next()
                    for kc in range(8):
                        self.mm(ps.h[:, 0:4], w.h[:, kc, j * 128:(j + 1) * 128], scT.h[:, kc * 4:kc * 4 + 4],
                                kc == 0, kc == 7, [w.b, scT.b], [ps.b])
                    self.ts("dve", modT.h[:, (blk * 8 + j) * 4:(blk * 8 + j) * 4 + 4], ps.h[:, 0:4],
                            badT.h[:, blk * 8 + j:blk * 8 + j + 1], ALU.add, [ps.b, badT.b], [modT.b])
        for j in range(8):
            for (dst, nrm, sblk, hblk) in ((0, 0, 1, 0), (2, 8, 4, 3)):
                self.ts("dve", amod.h[:, (dst * 8 + j) * 4:(dst * 8 + j) * 4 + 4], modT.h[:, (sblk * 8 + j) * 4:(sblk * 8 + j) * 4 + 4],
                        1.0, ALU.add, [modT.b], [amod.b], s2=self.sm.h[:, nrm + j:nrm + j + 1], op1=ALU.mult)
                self.cp("dve", amod.h[:, ((dst + 1) * 8 + j) * 4:((dst + 1) * 8 + j) * 4 + 4],
                        modT.h[:, (hblk * 8 + j) * 4:(hblk * 8 + j) * 4 + 4], [modT.b], [amod.b])
        rb = self.tile([32, 8], F32, "rb")
        self.load(rb.h[:], I["rb"][:, :], [rb.b])
        oh = self.tile([32, 1280], F32, "oh")
        self.load(oh.h[:], I["oh"][:, :], [oh.b])
        fr = self.tile([8, 1280], F32, "fr")
        for c0 in (0, 512, 1024):
            n = min(512, 1280 - c0)
            ps = self.prot.next()
            self.mm(ps.h[0:8, 0:n], rb.h[:, :], oh.h[:, c0:c0 + n], True, True, [rb.b, oh.b], [ps.b])
            self.cp("dve", fr.h[:, c0:c0 + n], ps.h[0:8, 0:n], [ps.b], [fr.b])
        self.store(self.frev_t.ap()[:, :], fr.h[:], [fr.b])
        self.mem.reset()

    def amod_ap(self, which, j, s):
        c = (which * 8 + j) * 4 + s
        return self.amod.h[:, c:c + 1]

    def norm_group(self, xts, seq, which, hT, scr, xn_rot, stat_rot):
        n = len(xts)
        xns = []
        for i, xt in enumerate(xts):
            st = stat_rot.next()
            self.act(scr.h[:], xt.h[:], AF.Square, [xt.b], [scr.b, st.b], accum=st.h[:, 0:1])
            self.act(st.h[:, 1:2], st.h[:, 0:1], AF.Ln, [st.b, self.epsT.b], [st.b], scale=1.0 / D, bias=self.eps_ap)
            self.act(st.h[:, 2:3], st.h[:, 1:2], AF.Exp, [st.b], [st.b], scale=-0.5)
            xn = xn_rot.next()
            self.ts("dve", xn.h[:], xt.h[:], st.h[:, 2:3], ALU.mult, [xt.b, st.b], [xn.b])
            xns.append(xn)
        for j in range(8):
            ps = self.prot.next()
            for i, xn in enumerate(xns):
                self.tr(ps.h[:, i * 128:(i + 1) * 128], xn.h[:, j * 128:(j + 1) * 128], self.ident, [xn.b, self.cst.b], [ps.b])
            w, pw = ([hT.b], ()) if j == 0 else ((), [hT.b])
            if j % 2 == 0:
                self.ts("dve", hT.h[:, j, 0:n * 128], ps.h[:, 0:n * 128], self.amod_ap(which, j, seq), ALU.mult,
                        [ps.b, self.amod.b], w, s2=self.amod_ap(which + 1, j, seq), op1=ALU.add, pw=pw)
            else:
                self.act(hT.h[:, j, 0:n * 128], ps.h[:, 0:n * 128], AF.Identity, [ps.b, self.amod.b], w,
                         scale=self.amod_ap(which, j, seq), bias=self.amod_ap(which + 1, j, seq), pw=pw)

    def mk_eps(self):
        epsT = self.tile([128, 1], F32, "eps")
        self.memset("dve", epsT.h[:], EPS, [epsT.b])
        self.eps_ap = epsT.h[:, 0:1]
        self.epsT = epsT

    def phase1(self, pss):
        nc, I, Sc, L, T, NOT = self.nc, self.I, self.Sc, self.L, self.T, self.NOT
        w_in_v = I["w_in"].rearrange("(kc p) c -> p kc c", p=128)
        if pss == 0:
            rr = [(C_KA, 1024), (C_QA, 1024), (C_VA, 1024)]
        elif pss == 1:
            rr = [(C_LR, 32), (C_KG, 512), (C_VG, 1024), (C_QG, 512)]
        else:
            rr = [(C_OG, 1024), (C_GA, 1024), (C_GB, 1024)]
        ranges = []
        for (a, w) in rr:
            wt = self.tile([128, 8, w], BF16, "win")
            self.wload(wt, w_in_v, 8, a, a + w, 0)
            ranges.append((a, a + w, wt))
        ncol = sum(b - a for a, b, _ in ranges)
        win = ranges[0][2]

        def lc(c):
            for (a, b, wt) in ranges:
                if a <= c < b:
                    return wt, c - a
            raise AssertionError(c)

        if os.environ.get("P1_WLOAD_ONLY"):
            o = self.tile([128, 1024], BF16, "dbgo")
            self.cp("dve", o.h[:], win.h[:, 7, 0:1024], [win.b], [o.b])
            self.store(Sc["GV"][0:128, :], o.h[:], [o.b])
            self.mem.reset()
            return
        self.mk_eps()
        if pss == 0:
            xrot = self.tiles(8, [128, 1024], F32, "x")
            xnrot = self.tiles(4, [128, 1024], F32, "xn")
            strot = self.tiles(4, [128, 4], F32, "st")
            scr = self.tile([128, 1024], BF16, "sqscr")
        hTs = self.tiles(2 if pss == 0 else 3, [128, 8, 512], BF16, "hT")
        f32s = self.tiles(4, [128, 512], F32, "f32s")
        bfs = self.tiles(4, [128, 512], BF16, "bfs")
        hls = self.tiles(4, [128, 512], BF16, "hls")
        if pss == 0:
            vst = self.tiles(3, [128, 1024], BF16, "vst")
        if pss == 1:
            kt_t = [[self.tile([128, 512], BF16, "ktm") for _ in range(4)] for _ in range(2)]
            gv_t = [[self.tile([128, 1024], BF16, "gvt") for _ in range(4)] for _ in range(2)]
            g_t = [[[(self.tile([128, 512], BF16, "gh"), self.tile([128, 512], BF16, "gl")) for _ in range(2)]
                    for _ in range(4)] for _ in range(2)]
            ke_t = [[[self.tile([128, 512], BF16, "ke") for _ in range(2)] for _ in range(4)] for _ in range(2)]
            lrT = self.tile([33, 512], BF16, "lrT")
            self.memset("dve", lrT.h[:], 1.0, [lrT.b])
            gst = self.tiles(4, [128, 512], F32, "gst")
            est = self.tiles(3, [128, 512], F32, "est")
            Et = self.tiles(8, [128, 16], F32, "Et")
            tmpS = self.tiles(2, [128, 256], F32, "tmpS")
            sin = self.tile([128, 8, 256], F32, "sin")
            self.memset("dve", sin.h[:], 0.0, [sin.b])

        groups = []
        KB = [0, L + self.NO, 2 * L + self.NO]
        for s in range(3):
            for q in range(L // 512):
                groups.append(("full", s, self.I["xf"], s * L + q * 512, 4, KB[s] + q * 512, s * L + q * 512, None))
        if pss < 2:
            u = 0
            while u < NOT:
                n = min(4, NOT - u)
                groups.append(("other", 0, self.I["xo"], u * 128, n, L + u * 128, None, u))
                u += n

        def load_x(g):
            kind, s, src, r0, n, kb, qb, u0 = g
            xts = []
            for i in range(n):
                xt = xrot.next()
                self.load(xt.h[:], src[r0 + i * 128:r0 + (i + 1) * 128, :], [xt.b])
                xts.append(xt)
            return xts

        def fm_proj(c0, hT, n, m=128):
            ps = self.prot.next()
            wt, l = lc(c0)
            for kc in range(8):
                self.mm(ps.h[0:m, 0:n], wt.h[:, kc, l:l + m], hT.h[:, kc, 0:n], kc == 0, kc == 7, [wt.b, hT.b], [ps.b])
            return ps

        def tm_proj(c0, hT, i):
            ps = self.prot.next()
            wt, l = lc(c0)
            for kc in range(8):
                self.mm(ps.h[:, :], hT.h[:, kc, i * 128:(i + 1) * 128], wt.h[:, kc, l:l + 512], kc == 0, kc == 7,
                        [wt.b, hT.b], [ps.b])
            return ps

        def qk_norm(ps, n, gcol, dst):
            if os.environ.get("QKN") == "0":
                o = bfs.next()
                self.cp("dve", o.h[:, 0:n], ps.h[:, 0:n], [ps.b], [o.b])
                self.store(dst, o.h[:, 0:n], [o.b])
                return
            sh = hls.next()
            self.act(sh.h[:, 0:n], ps.h[:, 0:n], AF.Square, [ps.b], [sh.b])
            raw = f32s.next()
            self.cp("dve", raw.h[:, 0:n], ps.h[:, 0:n], [ps.b], [raw.b])
            p2 = self.prot.next()
            self.mm(p2.h[:, 0:n], self.blk64b, sh.h[:, 0:n], True, True, [self.cstb.b, sh.b], [p2.b])
            ln = f32s.next()
            self.act(ln.h[:, 0:n], p2.h[:, 0:n], AF.Ln, [p2.b, self.epsT.b], [ln.b], scale=1.0 / 64, bias=self.eps_ap)
            self.act(ln.h[:, 0:n], ln.h[:, 0:n], AF.Exp, [ln.b], [ln.b], scale=-0.5)
            o = bfs.next()
            self.stt(o.h[:, 0:n], raw.h[:, 0:n], self.sm.h[:, gcol:gcol + 1], ln.h[:, 0:n], ALU.mult, ALU.mult,
                     [raw.b, ln.b, self.sm.b], [o.b])
            self.store(dst, o.h[:, 0:n], [o.b])

        def load_h(g):
            kind, s, src, r0, n, kb, qb, u0 = g
            hT_ = hTs.next()
            self.load(hT_.h[:, :, 0:n * 128], Sc["HT"][:, :, kb:kb + n * 128].rearrange("j p t -> p j t"), [hT_.b])
            return hT_

        nxt = load_x(groups[0]) if pss == 0 else load_h(groups[0])
        for gi, g in enumerate(groups):
            kind, s, src, r0, n, kb, qb, u0 = g
            N = n * 128
            if pss == 0:
                xts = nxt
                if gi + 1 < len(groups):
                    nxt = load_x(groups[gi + 1])
                hT = hTs.next()
                self.norm_group(xts, s, 0, hT, scr, xnrot, strot)
                self.store(Sc["HT"][:, :, kb:kb + N].rearrange("j p t -> p j t"), hT.h[:, :, 0:N], [hT.b])
            else:
                hT = nxt
                if gi + 1 < len(groups):
                    nxt = load_h(groups[gi + 1])
            if pss == 0:
                blocks = [(C_KA + h * 128, 17, Sc["KT"][h, :, kb:kb + N]) for h in range(8)]
                if kind == "full":
                    blocks += [(C_QA + h * 128, 16, Sc["QT"][h, :, qb:qb + N]) for h in range(8)]
                ps_n = fm_proj(blocks[0][0], hT, N)
                for bi, (c0, gcol, dst) in enumerate(blocks):
                    ps = ps_n
                    if bi + 1 < len(blocks):
                        ps_n = fm_proj(blocks[bi + 1][0], hT, N)
                    qk_norm(ps, N, gcol, dst)
                for i in range(n):
                    v = vst.next()
                    for half in range(2):
                        ps = tm_proj(C_VA + half * 512, hT, i)
                        if half == 0:
                            self.cp("dve", v.h[:, 0:512], ps.h[:, :], [ps.b], [v.b])
                        else:
                            self.cp("act", v.h[:, 512:1024], ps.h[:, :], [ps.b], (), pw=[v.b])
                    kt = (kb + i * 128) // 128
                    self.store(Sc["VH"][:, :, kt, :].rearrange("h p d -> p h d"),
                               v.h[:].rearrange("p (h d) -> p h d", h=8), [v.b])
            elif pss == 1:
                par = gi % 2
                ps = fm_proj(C_LR, hT, N, m=32)
                self.cp("dve", lrT.h[0:32, 0:N], ps.h[0:32, 0:N], [ps.b], [lrT.b])
                ktms, gvs = [], []
                for i in range(n):
                    ps = tm_proj(C_KG, hT, i)
                    ktm = kt_t[par][i]
                    self.cp("act", ktm.h[:], ps.h[:, :], [ps.b], [ktm.b])
                    gv = gv_t[par][i]
                    for half in range(2):
                        ps = tm_proj(C_VG + half * 512, hT, i)
                        if half == 0:
                            self.cp("dve", gv.h[:, 0:512], ps.h[:, :], [ps.b], [gv.b])
                        else:
                            self.cp("act", gv.h[:, 512:1024], ps.h[:, :], [ps.b], (), pw=[gv.b])
                    ktms.append(ktm)
                    gvs.append(gv)
                    if kind == "full":
                        self.store(Sc["GK"][qb + i * 128:qb + (i + 1) * 128, :], ktm.h[:], [ktm.b])
                        self.store(Sc["GV"][qb + i * 128:qb + (i + 1) * 128, :], gv.h[:], [gv.b])
                gfbs = []
                for i in range(n):
                    gfb = []
                    for d in range(2):
                        pz = self.prot.next()
                        self.mm(pz.h[:, :], lrT.h[0:33, i * 128:(i + 1) * 128], self.wgh.h[0:33, d * 512:(d + 1) * 512], True, False,
                                [lrT.b, self.wgh.b], [pz.b])
                        self.mm(pz.h[:, :], lrT.h[0:33, i * 128:(i + 1) * 128], self.wgl.h[0:33, d * 512:(d + 1) * 512], False, True,
                                [lrT.b, self.wgl.b], [pz.b])
                        e1 = est.next()
                        self.act(e1.h[:], pz.h[:, :], AF.Exp, [pz.b], [e1.b], scale=-1.0)
                        self.act(e1.h[:], e1.h[:], AF.Ln, [e1.b], [e1.b], bias=1.0)
                        gt = gst.next()
                        if kind == "full":
                            self.ts("dve", gt.h[:], e1.h[:], -1.0 / 16.0, ALU.mult, [e1.b], [gt.b])
                        else:
                            uu = u0 + i
                            self.ts("dve", gt.h[:], e1.h[:], self.nfl.h[:, d * NOT + uu:d * NOT + uu + 1], ALU.mult,
                                    [e1.b, self.nfl.b], [gt.b])
                        gh, gl = g_t[par][i][d]
                        self.hilo(gt.h[:], gt.b, gh.h[:], gh.b, gl.h[:], gl.b)
                        if kind == "full":
                            nm = "GGF" if d == 0 else "GGB"
                            self.store(Sc[nm + "H"][qb + i * 128:qb + (i + 1) * 128, :], gh.h[:], [gh.b])
                            self.store(Sc[nm + "L"][qb + i * 128:qb + (i + 1) * 128, :], gl.h[:], [gl.b])
                        gfb.append((gh, gl))
                    gfbs.append(gfb)
                if kind == "other":
                    Es, kes_ = [], []
                    for i in range(n):
                        uu = u0 + i
                        gfb = gfbs[i]
                        E = Et.next()
                        pt = self.prot.next()
                        for d in range(2):
                            for hd in range(4):
                                col = (hd * 2 + d) * 2
                                for hl in range(2):
                                    self.mm(pt.h[:, col:col + 2], gfb[d][hl].h[:, hd * 128:(hd + 1) * 128], self.onesb[:, 0:2], hl == 0, hl == 1,
                                            [gfb[d][hl].b, self.cstb.b], [pt.b])
                        self.act(E.h[:], pt.h[:, 0:16], AF.Exp, [pt.b], [E.b])
                        kk = []
                        for d in range(2):
                            pb = self.prot.next()
                            for hl in range(2):
                                self.mm(pb.h[:, :], self.triFb if d == 0 else self.triBb, gfb[d][hl].h[:], hl == 0, hl == 1,
                                        [self.cstb.b, gfb[d][hl].b], [pb.b])
                            em = est.next()
                            self.act(em.h[:], pb.h[:, :], AF.Exp, [pb.b], [em.b], scale=-1.0)
                            ke = ke_t[par][i][d]
                            self.stt(ke.h[:], ktms[i].h[:], self.flg.h[:, d * NOT + uu:d * NOT + uu + 1], em.h[:], ALU.mult, ALU.mult,
                                     [ktms[i].b, self.flg.b, em.b], [ke.b])
                            kk.append(ke)
                        Es.append(E)
                        kes_.append(kk)
                    for i in range(n):
                        E, gv = Es[i], gvs[i]
                        for d in range(2):
                            ke = kes_[i][d]
                            for hp in range(2):
                                pu = self.prot.next()
                                for k2 in range(2):
                                    hd = hp * 2 + k2
                                    self.mm(pu.h[:, k2 * 256:(k2 + 1) * 256], ke.h[:, hd * 128:(hd + 1) * 128],
                                            gv.h[:, hd * 256:(hd + 1) * 256], True, True, [ke.b, gv.b], [pu.b])
                                for k2 in range(2):
                                    hd = hp * 2 + k2
                                    tm = tmpS.next()
                                    self.tt("dve", tm.h[:], pu.h[:, k2 * 256:(k2 + 1) * 256], sin.h[:, hd * 2 + d, :], ALU.add,
                                            [pu.b, sin.b], [tm.b])
                                    col = (hd * 2 + d) * 2
                                    self.ts("dve", sin.h[:, hd * 2 + d, :], tm.h[:], E.h[:, col:col + 1], ALU.mult,
                                            [tm.b, E.b], [sin.b])
                if kind == "full":
                    for hd in range(4):
                        ps = fm_proj(C_QG + hd * 128, hT, N)
                        o = bfs.next()
                        self.act(o.h[:, 0:N], ps.h[:, 0:N], AF.Copy, [ps.b], [o.b], scale=128 ** -0.5)
                        self.store(Sc["GQT"][hd, :, qb:qb + N], o.h[:, 0:N], [o.b])
                        ps = fm_proj(C_KG + hd * 128, hT, N)
                        o = bfs.next()
                        self.cp("dve", o.h[:, 0:N], ps.h[:, 0:N], [ps.b], [o.b])
                        self.store(Sc["GKT"][hd, :, qb:qb + N], o.h[:, 0:N], [o.b])
            else:
                for (c0, fn, nm) in ((C_OG, AF.Silu, "OGT"), (C_GA, AF.Sigmoid, "GAT"), (C_GB, AF.Sigmoid, "GBT")):
                    for j in range(8):
                        ps = fm_proj(c0 + j * 128, hT, N)
                        o = bfs.next()
                        self.act(o.h[:, 0:N], ps.h[:, 0:N], fn, [ps.b], [o.b])
                        self.store(Sc[nm][j, :, qb:qb + N], o.h[:, 0:N], [o.b])
        if pss == 1:
            self.store(Sc["SIN"].rearrange("k p e -> p k e"), sin.h[:], [sin.b])
        self.mem.reset()

    def phaseB(self):
        nc, I, Sc, L, T, NOT = self.nc, self.I, self.Sc, self.L, self.T, self.NOT
        NKT0 = T + NOT
        self.mk_eps()
        epsT = self.epsT
        KTbig = self.tile([128, NKT0 * 128], BF16, "KTbig")
        Vbig = self.tile([128, NKT0, 128], BF16, "Vbig")
        KTsm = self.tiles(2, [128, T * 128], BF16, "KTsm")
        Vsm = self.tiles(2, [128, T, 128], BF16, "Vsm")
        QTs = self.tiles(2, [128, L], BF16, "QTs")
        Wp = self.tile([128, 1152], F32, "Wp")
        T32 = self.tile([128, 1408], F32, "T32")
        Hr = self.tile([128, 1408], BF16, "Hr")
        Lr = self.tile([128, 1408], BF16, "Lr")
        Whi = self.tiles(2, [128, 1408], BF16, "Whi")
        Wlo = self.tiles(2, [128, 1408], BF16, "Wlo")
        Pm = self.tiles(3, [128, 1024], BF16, "P")
        acc = self.tile([128, 512], F32, "acc")
        acc1 = self.tile([128, 512], F32, "acc1")
        hl2 = self.tiles(8, [128, 512], BF16, "hl2")
        fin = self.tiles(4, [128, 512], F32, "fin")
        a32 = self.tiles(2, [128, 512], F32, "a32")
        rl = self.tiles(2, [128, 1024], F32, "rl")
        o32 = self.tiles(2, [128, 512], F32, "o32")
        sq32 = self.tiles(2, [128, 512], F32, "sq32")
        ost = self.tiles(2, [128, 512], BF16, "ost")
        hls = self.tiles(2, [128, 512], BF16, "hlsB")
        Spair = Rot(self.pspair[0:2])
        L1 = self.psum[4]
        Fb = self.psum[5]
        Obank = self.psum[6:8]
        KB = [0, L + self.NO, 2 * L + self.NO]
        NG = L // 512

        def load_hs(h, s):
            nkt = NKT0 if s == 0 else T
            if s == 0:
                kt, v = KTbig, Vbig
            else:
                kt, v = KTsm.next(), Vsm.next()
            qt = QTs.next()
            t0 = KB[s] // 128
            step = 32
            for a in range(0, nkt, step):
                b = min(nkt, a + step)
                w, pw = ([kt.b], ()) if a == 0 else ((), [kt.b])
                self.load(kt.h[:, a * 128:b * 128], Sc["KT"][h, :, KB[s] + a * 128:KB[s] + b * 128], w, pw=pw)
                w, pw = ([v.b], ()) if a == 0 else ((), [v.b])
                self.load(v.h[:, a:b, :], Sc["VH"][h, :, t0 + a:t0 + b, :], w, pw=pw)
            self.load(qt.h[:, :], Sc["QT"][h, :, s * L:(s + 1) * L], [qt.b])
            return kt, v, qt

        def build_w(h):
            rb15 = self.rbb.h[:, 15 * 8 + h:15 * 8 + h + 1]
            rb31 = self.rbb.h[:, 31 * 8 + h:31 * 8 + h + 1]
            self.load(Wp.h[:], bass.AP(self.frev_t, h * 1280, [[1, 128], [1, 1152]]), [Wp.b])
            self.ts("dve", T32.h[:, 0:1152], Wp.h[:], 8.0, ALU.mult, [Wp.b], [T32.b])
            self.ts("dve", T32.h[:, 1152:1280], Wp.h[:, 640:768], rb15, ALU.subtract, [Wp.b, self.rbb.b], (), s2=8.0, op1=ALU.mult, pw=[T32.b])
            self.ts("dve", T32.h[:, 1280:1408], Wp.h[:, 384:512], rb31, ALU.subtract, [Wp.b, self.rbb.b], (), s2=8.0, op1=ALU.mult, pw=[T32.b])
            self.cp("dve", Hr.h[:], T32.h[:], [T32.b], [Hr.b])
            self.tt("dve", Lr.h[:], T32.h[:], Hr.h[:], ALU.subtract, [T32.b, Hr.b], [Lr.b])
            wh, wl = Whi.next(), Wlo.next()
            for (src, dst) in ((Hr, wh), (Lr, wl)):
                for c0 in (0, 512, 1024):
                    n = min(512, 1408 - c0)
                    ps = Spair.next()
                    self.mm(ps.h[:, 0:n], self.Jb, src.h[:, c0:c0 + n], True, True, [self.cstb.b, src.b], [ps.b])
                    w, pw = ([dst.b], ()) if c0 == 0 else ((), [dst.b])
                    self.cp("act", dst.h[:, c0:c0 + n], ps.h[:, 0:n], [ps.b], w, pw=pw)
            return wh, wl

        order = [(h, s) for h in range(8) for s in range(3)]
        nxt = load_hs(*order[0])
        wcur = None
        pending = [None]
        for oi, (h, s) in enumerate(order):
            kt, v, qt = nxt
            if s == 0:
                wcur = build_w(h)
            wh, wl = wcur
            if oi + 1 < len(order):
                nxt = load_hs(*order[oi + 1])
            nkt = NKT0 if s == 0 else T
            rb15 = self.rbb.h[:, 15 * 8 + h:15 * 8 + h + 1]
            rb31 = self.rbb.h[:, 31 * 8 + h:31 * 8 + h + 1]
            zero = self.sm.h[:, 22:23]
            steps = [(g, t) for g in range(NG) for t in range(nkt)]
            sps = {}

            def stageA(i):
                g, t = steps[i]
                band = None
                adj = None
                if t < T:
                    dt = t - 4 * g
                    if -1 <= dt <= 4:
                        band = 128 * (4 - dt)
                        bias = zero
                    elif dt < -1:
                        bias = rb15
                    else:
                        bias = rb31
                    bdep = [self.rbb.b, self.sm.b]
                else:
                    u = t - T
                    bias = self.fb.h[:, h * NOT + u:h * NOT + u + 1]
                    bdep = [self.fb.b]
                    if u == NOT - 2 and g == 0:
                        adj = (1152, 0)
                    if u == NOT - 1 and g == NG - 1:
                        adj = (1280, 384)
                sp = Spair.next()
                for m in range(2):
                    o = sp.h[:, m * 512:(m + 1) * 512]
                    last = (band is None and adj is None)
                    self.mm(o, kt.h[m * 64:(m + 1) * 64, t * 128:(t + 1) * 128],
                            qt.h[m * 64:(m + 1) * 64, g * 512:(g + 1) * 512], True, last, [kt.b, qt.b], [sp.b])
                    if band is not None:
                        self.mm(o, self.identb, wh.h[:, band:band + 512], False, False, [self.cstb.b, wh.b], [sp.b])
                        self.mm(o, self.identb, wl.h[:, band:band + 512], False, True, [self.cstb.b, wl.b], [sp.b])
                    if adj is not None:
                        wc, qc = adj
                        o2 = sp.h[:, m * 512 + qc:m * 512 + qc + 128]
                        self.mm(o2, self.identb, wh.h[:, wc:wc + 128], False, False, [self.cstb.b, wh.b], [sp.b])
                        self.mm(o2, self.identb, wl.h[:, wc:wc + 128], False, True, [self.cstb.b, wl.b], [sp.b])
                sps[i] = (sp, bias, bdep)

            def stageBC(i):
                g, t = steps[i]
                sp, bias, bdep = sps.pop(i)
                p = Pm.next()
                self.act(p.h[:], sp.h[:, :], AF.Exp, [sp.b] + bdep, [p.b], scale=0.125, bias=bias)
                if t == 0:
                    self.cp("dve", acc.h[:], p.h[:, 0:512], [p.b], [acc.b])
                else:
                    self.tt("dve", acc.h[:], acc.h[:], p.h[:, 0:512], ALU.add, [acc.b, p.b], [acc.b])
                for m in range(2):
                    self.mm(Obank[m].h[:, :], v.h[:, t, :], p.h[:, m * 512:(m + 1) * 512], t == 0, t == nkt - 1,
                            [v.b, p.b], [Obank[m].b])
                if t % 2 == 1:
                    self.mm(L1.h[:, :], self.onesb, p.h[:, 512:1024], t == 1, False, [self.cstb.b, p.b], [L1.b])
                elif t == 0:
                    self.cp("dve", acc1.h[:], p.h[:, 512:1024], [p.b], [acc1.b])
                else:
                    self.tt("dve", acc1.h[:], acc1.h[:], p.h[:, 512:1024], ALU.add, [acc1.b, p.b], [acc1.b])

            def evac(g, h=h, s=s):
                o0s, o1s = fin.next(), fin.next()
                self.cp("dve", o0s.h[:], Obank[0].h[:, :], [Obank[0].b], [o0s.b])
                self.cp("act", o1s.h[:], Obank[1].h[:, :], [Obank[1].b], [o1s.b])
                bh, bl = hl2.next(), hl2.next()
                self.hilo(acc1.h[:], acc1.b, bh.h[:], bh.b, bl.h[:], bl.b)
                self.mm(L1.h[:, :], self.onesb, bh.h[:], False, False, [self.cstb.b, bh.b], [L1.b])
                self.mm(L1.h[:, :], self.onesb, bl.h[:], False, True, [self.cstb.b, bl.b], [L1.b])
                r = rl.next()
                self.act(r.h[:, 0:512], L1.h[:, :], AF.Ln, [L1.b], [r.b])
                ah, al = hl2.next(), hl2.next()
                self.hilo(acc.h[:], acc.b, ah.h[:], ah.b, al.h[:], al.b)

                def rest():
                    self.mm(Fb.h[:, :], self.onesb, ah.h[:], True, False, [self.cstb.b, ah.b], [Fb.b])
                    self.mm(Fb.h[:, :], self.onesb, al.h[:], False, True, [self.cstb.b, al.b], [Fb.b])
                    self.act(r.h[:, 512:1024], Fb.h[:, :], AF.Ln, [Fb.b], (), pw=[r.b])
                    self.act(r.h[:], r.h[:], AF.Exp, [r.b], [r.b], scale=-1.0)
                    a0, a1 = a32.next(), a32.next()
                    self.tt("dve", a0.h[:], o0s.h[:], r.h[:, 512:1024], ALU.mult, [o0s.b, r.b], [a0.b])
                    self.tt("dve", a1.h[:], o1s.h[:], r.h[:, 0:512], ALU.mult, [o1s.b, r.b], [a1.b])
                    o = o32.next()
                    self.stt(o.h[:], a1.h[:], self.neglam, a0.h[:], ALU.mult, ALU.add, [a0.b, a1.b, self.sm.b], [o.b])
                    sq = sq32.next()
                    self.act(sq.h[:], o.h[:], AF.Square, [o.b], [sq.b])
                    sh, sl = hls.next(), hls.next()
                    self.hilo(sq.h[:], sq.b, sh.h[:], sh.b, sl.h[:], sl.b)
                    self.mm(Fb.h[:, :], self.onesb, sh.h[:], True, False, [self.cstb.b, sh.b], [Fb.b])
                    self.mm(Fb.h[:, :], self.onesb, sl.h[:], False, True, [self.cstb.b, sl.b], [Fb.b])
                    self.act(sq.h[:], Fb.h[:, :], AF.Ln, [Fb.b, epsT.b], [sq.b], scale=1.0 / 128, bias=epsT.h[:, 0:1])
                    self.act(sq.h[:], sq.h[:], AF.Exp, [sq.b], [sq.b], scale=-0.5)
                    ob = ost.next()
                    self.stt(ob.h[:], o.h[:], self.sm.h[:, 21:22], sq.h[:], ALU.mult, ALU.mult, [o.b, sq.b, self.sm.b], [ob.b])
                    self.store(Sc["OAT"][h, :, s * L + g * 512:s * L + (g + 1) * 512], ob.h[:], [ob.b])
                return rest

            for g in range(NG):
                base = g * nkt
                stageA(base)
                for i in range(nkt):
                    if i + 1 < nkt:
                        stageA(base + i + 1)
                    stageBC(base + i)
                    if i == 2 and pending[0] is not None:
                        pending[0]()
                        pending[0] = None
                assert pending[0] is None
                pending[0] = evac(g)
        if pending[0] is not None:
            pending[0]()
        self.mem.pers = self.pers_late
        self.mem.reset()

    def phaseC(self):
        nc, I, Sc, L, T, NOT = self.nc, self.I, self.Sc, self.L, self.T, self.NOT
        self.mk_eps()
        epsT = self.epsT
        mask2 = self.tile([128, 256], F32, "mask2")
        self.cp("dve", mask2.h[:, 0:128], self.triF, [self.cst.b], [mask2.b])
        self.cp("dve", mask2.h[:, 128:256], self.triB, [self.cst.b], (), pw=[mask2.b])
        sinp = self.tile([128, 8, 256], F32, "sinp")
        self.load(sinp.h[:], Sc["SIN"].rearrange("k p e -> p k e"), [sinp.b])
        qTs = self.tiles(2, [128, L], BF16, "qT")
        kTs = self.tiles(2, [128, L], BF16, "kT")
        ktms = self.tiles(1, [128, T, 128], BF16, "ktm")
        vtms = self.tiles(1, [128, T, 256], BF16, "vtm")
        gfs = [self.tiles(1, [128, 2, T, 128], BF16, "gf"), self.tiles(1, [128, 2, T, 128], BF16, "gb")]
        ogs = self.tiles(1, [128, 2, L], BF16, "og")
        ke = [self.tile([128, T, 128], BF16, "ke_f"), self.tile([128, T, 128], BF16, "ke_b")]
        qeT = [self.tile([128, L], BF16, "qeT_f"), self.tile([128, L], BF16, "qeT_b")]
        keT = [self.tile([128, L], BF16, "keT_f"), self.tile([128, L], BF16, "keT_b")]
        Ecol = [self.tile([128, T], F32, "E_f"), self.tile([128, T], F32, "E_b")]
        Sin = [[Tl(self.mem.alloc([128, 256], BF16, name="Sin"), f"Sin{d}_{c}") for c in range(T)] for d in range(2)]
        S32 = [self.tile([128, 256], F32, "S32f"), self.tile([128, 256], F32, "S32b")]
        e32 = self.tiles(3, [128, 512], F32, "e32")
        tmp = self.tiles(2, [128, 256], F32, "tmpc")
        aTm = self.tiles(3, [128, 256], BF16, "aTm")
        sq = self.tiles(2, [128, 2, 512], F32, "sqc")
        nrm = self.tiles(2, [128, 512], F32, "nrm")
        t32 = self.tiles(2, [128, 512], F32, "t32")
        ost = self.tiles(3, [128, 512], BF16, "ostc")
        tri = [self.triFb, self.triBb]
        hlsC = self.tiles(2, [128, 2, 512], BF16, "hlsC")
        NQ = T // 4

        def load_a(s, hd):
            q, k = qTs.next(), kTs.next()
            r = slice(s * L, (s + 1) * L)
            self.load(q.h[:, :], Sc["GQT"][hd, :, r], [q.b])
            self.load(k.h[:, :], Sc["GKT"][hd, :, r], [k.b])
            return q, k

        def load_b(s, hd):
            ktm, vtm, gf, gb, og = ktms.next(), vtms.next(), gfs[0].next(), gfs[1].next(), ogs.next()
            r = slice(s * L, (s + 1) * L)
            self.load(ktm.h[:], Sc["GK"][r, hd * 128:(hd + 1) * 128].rearrange("(t p) d -> p t d", p=128), [ktm.b])
            self.load(vtm.h[:], Sc["GV"][r, hd * 256:(hd + 1) * 256].rearrange("(t p) d -> p t d", p=128), [vtm.b])
            for (gt_, nm) in ((gf, "GGF"), (gb, "GGB")):
                self.load(gt_.h[:, 0, :, :], Sc[nm + "H"][r, hd * 128:(hd + 1) * 128].rearrange("(t p) d -> p t d", p=128), [gt_.b])
                self.load(gt_.h[:, 1, :, :], Sc[nm + "L"][r, hd * 128:(hd + 1) * 128].rearrange("(t p) d -> p t d", p=128), (), pw=[gt_.b])
            self.load(og.h[:], Sc["OGT"][2 * hd:2 * hd + 2, :, r].rearrange("j p t -> p j t"), [og.b])
            return ktm, vtm, gf, gb, og

        order = [(s, hd) for s in range(3) for hd in range(4)]
        nxt = load_a(*order[0])
        for oi, (s, hd) in enumerate(order):
            q, k = nxt
            ktm, vtm, gf, gb, og = load_b(s, hd)
            if oi + 1 < len(order):
                nxt = load_a(*order[oi + 1])
            gg = [gf, gb]
            for d in range(2):
                for qd in range(NQ):
                    cs = slice(qd * 512, (qd + 1) * 512)
                    ps = self.prot.next()
                    for cc in range(4):
                        c = qd * 4 + cc
                        for hl in range(2):
                            self.mm(ps.h[:, cc * 128:(cc + 1) * 128], tri[d], gg[d].h[:, hl, c, :], hl == 0, hl == 1, [self.cstb.b, gg[d].b], [ps.b])
                    em = e32.next()
                    self.act(em.h[:], ps.h[:, :], AF.Exp, [ps.b], [em.b], scale=-1.0)
                    self.tt("dve", ke[d].h[:, qd * 4:(qd + 1) * 4, :].rearrange("p t d -> p (t d)"),
                            ktm.h[:, qd * 4:(qd + 1) * 4, :].rearrange("p t d -> p (t d)"), em.h[:], ALU.mult,
                            [ktm.b, em.b], [ke[d].b])
                    ps = self.prot.next()
                    for cc in range(4):
                        c = qd * 4 + cc
                        for hl in range(2):
                            self.mm(ps.h[:, cc * 128:(cc + 1) * 128], gg[d].h[:, hl, c, :], tri[d], hl == 0, hl == 1, [self.cstb.b, gg[d].b], [ps.b])
                    ep = e32.next()
                    self.act(ep.h[:], ps.h[:, :], AF.Exp, [ps.b], [ep.b])
                    em2 = e32.next()
                    self.act(em2.h[:], ps.h[:, :], AF.Exp, [ps.b], [em2.b], scale=-1.0)
                    self.tt("dve", qeT[d].h[:, cs], q.h[:, cs], ep.h[:], ALU.mult, [q.b, ep.b], [qeT[d].b])
                    self.tt("dve", keT[d].h[:, cs], k.h[:, cs], em2.h[:], ALU.mult, [k.b, em2.b], [keT[d].b])
                    off = 127 if d == 0 else 0
                    self.cp("dve", Ecol[d].h[:, qd * 4:(qd + 1) * 4], ep.h[:].rearrange("p (c t) -> p c t", t=128)[:, :, off],
                            [ep.b], [Ecol[d].b])
            for d in range(2):
                if s == 0:
                    self.cp("dve", S32[d].h[:], sinp.h[:, hd * 2 + d, :], [sinp.b], [S32[d].b])
                else:
                    self.memset("dve", S32[d].h[:], 0.0, [S32[d].b])
                cords = list(range(T)) if d == 0 else list(range(T - 1, -1, -1))
                self.cp("act", Sin[d][cords[0]].h[:], S32[d].h[:], [S32[d].b], [Sin[d][cords[0]].b])
                for ci, c in enumerate(cords):
                    if ci == T - 1:
                        break
                    pu = self.prot.next()
                    self.mm(pu.h[:, 0:256], ke[d].h[:, c, :], vtm.h[:, c, :], True, True, [ke[d].b, vtm.b], [pu.b])
                    tm = tmp.next()
                    self.tt("dve", tm.h[:], pu.h[:, 0:256], S32[d].h[:], ALU.add, [pu.b, S32[d].b], [tm.b])
                    self.ts("dve", S32[d].h[:], tm.h[:], Ecol[d].h[:, c:c + 1], ALU.mult, [tm.b, Ecol[d].b], [S32[d].b])
                    nx = Sin[d][cords[ci + 1]]
                    self.act(nx.h[:], tm.h[:], AF.Identity, [tm.b, Ecol[d].b], [nx.b], scale=Ecol[d].h[:, c:c + 1])
            for qd in range(NQ):
                po = [self.prot.next(), self.prot.next()]
                for cc in range(4):
                    c = qd * 4 + cc
                    cs = slice(c * 128, (c + 1) * 128)
                    pa = self.prot.next()
                    for d in range(2):
                        self.mm(pa.h[:, d * 128:(d + 1) * 128], keT[d].h[:, cs], qeT[d].h[:, cs], True, True,
                                [keT[d].b, qeT[d].b], [pa.b])
                    am = aTm.next()
                    self.tt("dve", am.h[:], pa.h[:, 0:256], mask2.h[:], ALU.mult, [pa.b, mask2.b], [am.b])
                    for eb in range(2):
                        es = slice(eb * 128, (eb + 1) * 128)
                        o = po[eb].h[:, cc * 128:(cc + 1) * 128]
                        self.mm(o, vtm.h[:, c, es], am.h[:, 0:128], True, False, [vtm.b, am.b], [po[eb].b])
                        self.mm(o, vtm.h[:, c, es], am.h[:, 128:256], False, False, [vtm.b, am.b], [po[eb].b])
                        self.mm(o, Sin[0][c].h[:, es], qeT[0].h[:, cs], False, False, [Sin[0][c].b, qeT[0].b], [po[eb].b])
                        self.mm(o, Sin[1][c].h[:, es], qeT[1].h[:, cs], False, True, [Sin[1][c].b, qeT[1].b], [po[eb].b])
                sqt = sq.next()
                for eb in range(2):
                    self.act(sqt.h[:, eb, :], po[eb].h[:, :], AF.Square, [po[eb].b], [sqt.b])
                pn = self.prot.next()
                sh, sl = hlsC.next(), hlsC.next()
                self.hilo(sqt.h[:], sqt.b, sh.h[:], sh.b, sl.h[:], sl.b)
                for eb in range(2):
                    self.mm(pn.h[:, :], self.onesb, sh.h[:, eb, :], eb == 0, False, [self.cstb.b, sh.b], [pn.b])
                    self.mm(pn.h[:, :], self.onesb, sl.h[:, eb, :], False, eb == 1, [self.cstb.b, sl.b], [pn.b])
                nr = nrm.next()
                self.act(nr.h[:], pn.h[:, :], AF.Ln, [pn.b, epsT.b], [nr.b], scale=1.0 / 256, bias=epsT.h[:, 0:1])
                self.act(nr.h[:], nr.h[:], AF.Exp, [nr.b], [nr.b], scale=-0.5)
                for eb in range(2):
                    tt_ = t32.next()
                    self.stt(tt_.h[:], po[eb].h[:, :], self.sm.h[:, 19 + eb:20 + eb], nr.h[:], ALU.mult, ALU.mult,
                             [po[eb].b, nr.b, self.sm.b], [tt_.b])
                    ob = ost.next()
                    self.tt("dve", ob.h[:], tt_.h[:], og.h[:, eb, qd * 512:(qd + 1) * 512], ALU.mult, [tt_.b, og.b], [ob.b])
                    self.store(Sc["OBT"][2 * hd + eb, :, s * L + qd * 512:s * L + (qd + 1) * 512], ob.h[:], [ob.b])
        self.mem.reset()

    def phaseD1(self):
        nc, I, Sc, L, T = self.nc, self.I, self.Sc, self.L, self.T
        wba = self.tile([128, 8, D], BF16, "wba")
        wbb = self.tile([128, 8, D], BF16, "wbb")
        wout = self.tile([128, 8, D], BF16, "wout")
        self.wload(wba, I["w_ba"].rearrange("(kc p) c -> p kc c", p=128), 8, 0, D)
        self.wload(wbb, I["w_bb"].rearrange("(kc p) c -> p kc c", p=128), 8, 0, D)
        self.wload(wout, I["w_out"].rearrange("(kc p) c -> p kc c", p=128), 8, 0, D)
        self.mk_eps()
        gbc = self.tiles(1, [128, D], F32, "gbc")
        ins = [self.tiles(1, [128, 8, 512], BF16, nm) for nm in ("oaT", "obT", "gaT", "gbT")]
        xrot = self.tiles(4, [128, 1024], F32, "x")
        m32 = self.tiles(4, [128, 512], F32, "m32")
        mT = self.tiles(1, [128, 8, 512], BF16, "mT")
        x1s = self.tiles(4, [128, 1024], F32, "x1")
        t32 = self.tiles(2, [128, 512], F32, "t32")
        xnrot = self.tiles(4, [128, 1024], F32, "xn")
        strot = self.tiles(4, [128, 4], F32, "st")
        scr = self.tile([128, 1024], BF16, "sqscr")
        h2 = self.tiles(1, [128, 8, 512], BF16, "h2T")
        NG = L // 512
        groups = [(s, q) for s in range(3) for q in range(NG)]

        def loads_in(g):
            s, q = g
            r = slice(s * L + q * 512, s * L + (q + 1) * 512)
            out = []
            for rot, nm in zip(ins, ("OAT", "OBT", "GAT", "GBT")):
                t = rot.next()
                self.load(t.h[:], Sc[nm][:, :, r].rearrange("j p t -> p j t"), [t.b])
                out.append(t)
            return out

        def loads_x(g):
            s, q = g
            xs = []
            for i in range(4):
                xt = xrot.next()
                self.load(xt.h[:], I["xf"][s * L + q * 512 + i * 128:s * L + q * 512 + (i + 1) * 128, :], [xt.b])
                xs.append(xt)
            return xs

        nxt_in = loads_in(groups[0])
        nxt_x = loads_x(groups[0])
        gcur = None
        for gi, g in enumerate(groups):
            s, q = g
            (oaT, obT, gaT, gbT), xs = nxt_in, nxt_x
            if q == 0:
                gcur = gbc.next()
                self.load(gcur.h[:], Sc["GBC"][s, :, :], [gcur.b])
            m = mT.next()
            for cb in range(8):
                pa = self.prot.next()
                for kc in range(8):
                    self.mm(pa.h[:, :], wba.h[:, kc, cb * 128:(cb + 1) * 128], oaT.h[:, kc, :], kc == 0, kc == 7, [wba.b, oaT.b], [pa.b])
                pb = self.prot.next()
                for kc in range(8):
                    self.mm(pb.h[:, :], wbb.h[:, kc, cb * 128:(cb + 1) * 128], obT.h[:, kc, :], kc == 0, kc == 7, [wbb.b, obT.b], [pb.b])
                m1, m2 = m32.next(), m32.next()
                self.tt("dve", m1.h[:], pa.h[:, :], gaT.h[:, cb, :], ALU.mult, [pa.b, gaT.b], [m1.b])
                self.tt("dve", m2.h[:], pb.h[:, :], gbT.h[:, cb, :], ALU.mult, [pb.b, gbT.b], [m2.b])
                w, pw = ([m.b], ()) if cb == 0 else ((), [m.b])
                self.tt("pool", m.h[:, cb, :], m1.h[:], m2.h[:], ALU.add, [m1.b, m2.b], w, pw=pw)
            if gi + 1 < len(groups):
                nxt_in = loads_in(groups[gi + 1])
            x1l = []
            for i in range(4):
                x1 = x1s.next()
                for half in range(2):
                    hs = slice(half * 512, (half + 1) * 512)
                    pz = self.prot.next()
                    for kc in range(8):
                        self.mm(pz.h[:, :], m.h[:, kc, i * 128:(i + 1) * 128], wout.h[:, kc, hs], kc == 0, kc == 7, [m.b, wout.b], [pz.b])
                    t_ = t32.next()
                    self.tt("dve", t_.h[:], pz.h[:, :], gcur.h[:, hs], ALU.mult, [pz.b, gcur.b], [t_.b])
                    w, pw = ([x1.b], ()) if half == 0 else ((), [x1.b])
                    self.tt("pool", x1.h[:, hs], t_.h[:], xs[i].h[:, hs], ALU.add, [t_.b, xs[i].b], w, pw=pw)
                r0 = s * L + q * 512 + i * 128
                self.store(Sc["X1"][r0:r0 + 128, :], x1.h[:], [x1.b])
                x1l.append(x1)
            if gi + 1 < len(groups):
                nxt_x = loads_x(groups[gi + 1])
            hT = h2.next()
            self.norm_group(x1l, s, 2, hT, scr, xnrot, strot)
            self.store(Sc["H2T"][:, :, s * L + q * 512:s * L + (q + 1) * 512].rearrange("j p t -> p j t"), hT.h[:], [hT.b])
        self.mem.reset()

    def phaseD2(self):
        nc, I, Sc, L, T = self.nc, self.I, self.Sc, self.L, self.T
        wups = [self.tile([128, 8, 1024], BF16, "wup") for _ in range(4)]
        wdn = self.tile([128, 32, D], BF16, "wdn")
        for qi in range(4):
            self.wload(wups[qi], I["w_up"].rearrange("(kc p) c -> p kc c", p=128), 8, qi * 1024, (qi + 1) * 1024, 0)
        self.wload(wdn, I["w_dn"].rearrange("(kc p) c -> p kc c", p=128), 32, 0, D)
        gbc = self.tiles(1, [128, D], F32, "gbc2")
        h2 = self.tiles(2, [128, 8, 256], BF16, "h2")
        uT = self.tile([128, 32, 256], BF16, "uT")
        r32 = self.tiles(2, [128, 256], F32, "r32")
        x1s = self.tiles(4, [128, 1024], F32, "x1b")
        t32 = self.tiles(1, [128, 512], F32, "t32b")
        NG = L // 256
        groups = [(s, q) for s in range(3) for q in range(NG)]

        def loads(g):
            s, q = g
            r0 = s * L + q * 256
            h = h2.next()
            self.load(h.h[:], Sc["H2T"][:, :, r0:r0 + 256].rearrange("j p t -> p j t"), [h.b])
            xs = []
            for i in range(2):
                xt = x1s.next()
                self.load(xt.h[:], Sc["X1"][r0 + i * 128:r0 + (i + 1) * 128, :], [xt.b])
                xs.append(xt)
            return h, xs

        nxt = loads(groups[0])
        gcur = None
        for gi, g in enumerate(groups):
            s, q = g
            h, xs = nxt
            if q == 0:
                gcur = gbc.next()
                self.load(gcur.h[:], Sc["GBC"][3 + s, :, :], [gcur.b])
            if gi + 1 < len(groups):
                nxt = loads(groups[gi + 1])
            for fc in range(32):
                pu = self.prot.next()
                for kc in range(8):
                    wq = wups[fc // 8]
                    self.mm(pu.h[:, 0:256], wq.h[:, kc, (fc % 8) * 128:(fc % 8 + 1) * 128], h.h[:, kc, :], kc == 0, kc == 7, [wq.b, h.b], [pu.b])
                r = r32.next()
                self.act(r.h[:], pu.h[:, 0:256], AF.Relu, [pu.b], [r.b])
                w, pw = ([uT.b], ()) if fc == 0 else ((), [uT.b])
                self.tt("dve", uT.h[:, fc, :], r.h[:], r.h[:], ALU.mult, [r.b], w, pw=pw)
            for i in range(2):
                y = xs[i]
                for half in range(2):
                    hs = slice(half * 512, (half + 1) * 512)
                    pz = self.prot.next()
                    for fc in range(32):
                        self.mm(pz.h[:, :], uT.h[:, fc, i * 128:(i + 1) * 128], wdn.h[:, fc, hs], fc == 0, fc == 31, [uT.b, wdn.b], [pz.b])
                    t_ = t32.next()
                    self.tt("dve", t_.h[:], pz.h[:, :], gcur.h[:, hs], ALU.mult, [pz.b, gcur.b], [t_.b])
                    self.tt("pool", y.h[:, hs], t_.h[:], xs[i].h[:, hs], ALU.add, [t_.b, xs[i].b], [y.b])
                r0 = s * L + q * 256 + i * 128
                self.store(self.y[r0:r0 + 128, :], y.h[:], [y.b])
        self.mem.reset()


def _t5_bucket_np(rel):
    nb = 16
    max_exact = 8
    ret = (rel > 0).astype(np.int32) * nb
    n = np.abs(rel)
    nf = np.maximum(n, 1).astype(np.float32)
    large = max_exact + (np.log(nf / np.float32(max_exact)) / np.float32(math.log(128 / max_exact))
                         * np.float32(nb - max_exact)).astype(np.int32)
    large = np.minimum(large, nb - 1)
    return ret + np.where(n < max_exact, n, large)


def _consts():
    p = np.arange(128)
    ident = np.eye(128, dtype=np.float32)
    J = ident[::-1].copy()
    triF = (p[:, None] <= p[None, :]).astype(np.float32)
    triB = (p[:, None] >= p[None, :]).astype(np.float32)
    blk = ((p[:, None] // 64) == (p[None, :] // 64)).astype(np.float32)
    ones = np.ones((128, 128), np.float32)
    cst = np.concatenate([ident, J, triF, triB, blk, ones], axis=1)
    i = np.arange(1280)
    bk = _t5_bucket_np((639 - i).astype(np.int32))
    oh = (bk[None, :] == np.arange(32)[:, None]).astype(np.float32)
    return np.ascontiguousarray(cst), np.ascontiguousarray(oh)


def make_in_maps(L, inp):
    T = L // 128
    NOT = 7 * T + 1
    f = lambda a: np.ascontiguousarray(np.asarray(a, dtype=np.float32))
    xp = f(inp["x_prompt"])[0]
    xs = f(inp["x_sample"])
    cp = f(inp["c_prompt"])[0]
    cs = f(inp["c_sample"])
    cst, oh = _consts()
    fmT = lambda v: np.ascontiguousarray(v.reshape(-1, 128).T)
    shared = {
        "w_ada": f(inp["w_ada"])[0], "b_adaT": fmT(f(inp["b_ada"])[0]), "b_ada": f(inp["b_ada"])[0][None, :],
        "n1T": fmT(f(inp["norm1_g"])[0]), "n2T": fmT(f(inp["norm2_g"])[0]), "w_in": f(inp["w_in"])[0],
        "qkg": np.ascontiguousarray(np.stack([np.tile(f(inp["q_norm_g"])[0], 2), np.tile(f(inp["k_norm_g"])[0], 2)], axis=1)),
        "lamv": np.concatenate([f(inp["lam_q1"])[0], f(inp["lam_k1"])[0], f(inp["lam_q2"])[0], f(inp["lam_k2"])[0]])[None, :],
        "subg": f(inp["subln_g"])[0][:, None],
        "glng": fmT(f(inp["gla_norm_g"])[0]),
        "w_ba": f(inp["w_branch_a"])[0], "w_bb": f(inp["w_branch_b"])[0], "w_out": f(inp["w_out"])[0],
        "w_up": f(inp["w_up"])[0], "w_dn": f(inp["w_down"])[0],
        "rb": f(inp["rel_bias"]), "rbrow": f(inp["rel_bias"]).reshape(1, 256), "oh": oh, "cst": cst,
    }
    z16 = np.zeros((16, 512), np.float32)
    shared["wgf"] = np.ascontiguousarray(np.concatenate([f(inp["w_gate_f"])[0], z16, f(inp["b_gate_f"])], axis=0))
    shared["wgb"] = np.ascontiguousarray(np.concatenate([z16, f(inp["w_gate_b"])[0], f(inp["b_gate_b"])], axis=0))
    maps = []
    xpt = xp.reshape(8 * T, 128, D)
    for c in range(8):
        m = dict(shared)
        m["xf"] = np.ascontiguousarray(np.concatenate([xp[c * L:(c + 1) * L], xs[2 * c], xs[2 * c + 1]], axis=0))
        post = list(range(8 * T - 1, (c + 1) * T, -1))
        pre = list(range(0, c * T - 1))
        xo = np.zeros((NOT, 128, D), np.float32)
        flg = np.zeros((3, NOT), np.float32)
        u = 0
        for t in post:
            xo[u] = xpt[t]; flg[1, u] = 1; u += 1
        for t in pre:
            xo[u] = xpt[t]; flg[0, u] = 1; u += 1
        while u < NOT - 2:
            flg[2, u] = 1; u += 1
        if c > 0:
            xo[NOT - 2] = xpt[c * T - 1]; flg[0, NOT - 2] = 1
        else:
            flg[2, NOT - 2] = 1
        if c < 7:
            xo[NOT - 1] = xpt[(c + 1) * T]; flg[1, NOT - 1] = 1
        else:
            flg[2, NOT - 1] = 1
        m["xo"] = np.ascontiguousarray(xo.reshape(NOT * 128, D))
        m["flg"] = np.ascontiguousarray(np.broadcast_to(flg.reshape(1, 3 * NOT), (128, 3 * NOT)))
        cc = np.zeros((4, D), np.float32)
        cc[0] = cp; cc[1] = cs[2 * c]; cc[2] = cs[2 * c + 1]
        m["cT"] = np.ascontiguousarray(cc.reshape(4, 8, 128).transpose(2, 1, 0).reshape(128, 32))
        maps.append(m)
    return maps


_CACHE = {}


def run(L, inp, debug=False, nphase=99):
    key = (L, debug, nphase)
    if key not in _CACHE:
        k = K(L, debug=debug, nphase=nphase)
        nc = k.build()
        from contextlib import ExitStack
        st = ExitStack()
        k.S.emit(nc, st)
        st.close()
        _CACHE[key] = (k, nc)
    k, nc = _CACHE[key]
    maps = make_in_maps(L, inp)
    res = run_bass_kernel_spmd(nc, maps, core_ids=list(range(8)))
    return k, res


def kernel(**inputs):
    L = 2048
    k, res = run(L, inputs)
    yp = np.concatenate([res.results[c]["y"][0:L] for c in range(8)], axis=0)[None]
    ys = np.stack([res.results[c]["y"][L * (1 + j):L * (2 + j)] for c in range(8) for j in range(2)], axis=0)
    return (np.ascontiguousarray(yp, dtype=np.float32), np.ascontiguousarray(ys, dtype=np.float32))
```

```python
import math
import numpy as np
import concourse.bass as bass
import concourse.mybir as mybir
from concourse.bass_utils import run_bass_kernel_spmd

F32 = mybir.dt.float32
BF16 = mybir.dt.bfloat16
AF = mybir.ActivationFunctionType
ALU = mybir.AluOpType

D = 1024
NH = 8
HB = 4
DIN = 8224
DFF = 4096
EPS = 1e-6
C_QA, C_KA, C_VA, C_QG, C_KG, C_VG, C_OG, C_LR, C_GA, C_GB = 0, 1024, 2048, 3072, 3584, 4096, 5120, 6144, 6176, 7200
NEG = -30000.0


class Buf:
    __slots__ = ("name", "w", "r", "pw", "pr", "excl")

    def __init__(self, name, excl=False):
        self.name = name
        self.excl = excl
        self.w = []
        self.r = []
        self.pw = []
        self.pr = []


class Op:
    __slots__ = ("eng", "fn", "deps", "needed", "val", "dma", "dsem", "idx")

    def __init__(self, eng, fn, dma):
        self.eng = eng
        self.fn = fn
        self.deps = []
        self.needed = False
        self.val = 0
        self.dma = dma
        self.dsem = -1
        self.idx = 0


ENGS = ("pe", "act", "dve", "pool", "sp")
NDSEM = 6
import os
STORE_Q = os.environ.get('STORE_Q', 'pool')


class Sched:
    def __init__(self):
        self.ops = {e: [] for e in ENGS}
        self.dma_ops = []
        self.out_dmas = []
        self.last_barrier_idx = {e: 0 for e in ENGS}

    def op(self, eng, fn, r=(), w=(), dma=False, pw=()):
        o = Op(eng, fn, dma)
        deps = []
        for b in r:
            deps.extend(b.w)
            if b.excl:
                deps.extend(d for d in b.r if d.eng != eng)
        for b in w:
            deps.extend(b.w)
            deps.extend(b.r)
        for b in pw:
            deps.extend(b.pw)
            deps.extend(b.pr)
        seen = set()
        for d in deps:
            if d is o or id(d) in seen:
                continue
            seen.add(id(d))
            if d.eng == "pe" and eng == "pe" and not d.dma:
                continue
            d.needed = True
            o.deps.append(d)
        for b in r:
            b.r.append(o)
        for b in w:
            b.pw = b.w
            b.pr = b.r
            b.w = [o]
            b.r = []
        for b in pw:
            b.w.append(o)
        o.idx = len(self.ops[eng])
        self.ops[eng].append(o)
        if dma:
            self.dma_ops.append(o)
        return o

    def barrier(self):
        lasts = []
        for e in ENGS:
            for o in reversed(self.ops[e]):
                if not o.dma:
                    lasts.append(o)
                    break
        dmas = list(self.dma_ops)
        self.dma_ops = []
        for e in ENGS:
            o = Op(e, None, False)
            for d in lasts + dmas:
                if d.eng == e and not d.dma:
                    continue
                d.needed = True
                o.deps.append(d)
            o.idx = len(self.ops[e])
            self.ops[e].append(o)

    def emit(self, nc, stack):
        sems = {e: stack.enter_context(nc.semaphore("sem_" + e)) for e in ENGS}
        dsems = {}
        for e in ("sp", "pool", "act"):
            dsems[e] = [stack.enter_context(nc.semaphore(f"dsem_{e}_{i}")) for i in range(NDSEM)]
        for e in ENGS:
            cnt = 0
            dcnt = [0] * NDSEM
            k = 0
            for o in self.ops[e]:
                if o.dma:
                    o.dsem = k % NDSEM
                    dcnt[o.dsem] += 16
                    o.val = dcnt[o.dsem]
                    k += 1
                elif o.fn is not None and o.needed:
                    cnt += 1
                    o.val = cnt
        engobj = {"pe": "tensor", "act": "scalar", "dve": "vector", "pool": "gpsimd", "sp": "sync"}
        block = stack.enter_context(nc.Block())

        def run(ename):
            def body(e):
                waited = {}
                for o in self.ops[ename]:
                    if o.dma:
                        if o.val > 16:
                            key = ("d", ename, o.dsem)
                            if waited.get(key, 0) < o.val - 16:
                                e.wait_ge(dsems[ename][o.dsem], o.val - 16)
                                waited[key] = o.val - 16
                    for d in o.deps:
                        if d.dma:
                            key = ("d", d.eng, d.dsem)
                            s = dsems[d.eng][d.dsem]
                        else:
                            key = ("c", d.eng)
                            s = sems[d.eng]
                        if waited.get(key, 0) < d.val:
                            e.wait_ge(s, d.val)
                            waited[key] = d.val
                    if o.fn is None:
                        continue
                    ins = o.fn(e)
                    if o.dma:
                        ins.then_inc(dsems[ename][o.dsem], 16)
                    elif o.needed:
                        ins.then_inc(sems[ename], 1)
            return body

        block.sync(run("sp"))
        block.gpsimd(run("pool"))
        block.scalar(run("act"))
        block.vector(run("dve"))
        block.tensor(run("pe"))


class Mem:
    def __init__(self, nc, lo=16384, hi=212000):
        self.nc = nc
        self.lo = lo
        self.hi = hi
        self.cur = lo
        self.n = 0
        self.pers = lo

    def alloc(self, shape, dt, nbuf=1, name=None):
        nb = int(np.prod(shape[1:])) * (4 if dt == F32 else 2)
        nb = (nb + 63) // 64 * 64
        self.n += 1
        t = self.nc.alloc_sbuf_tensor_at(f"{name or 't'}_{self.n}", list(shape), dt, offset=self.cur)
        self.cur += nb
        assert self.cur <= self.hi, f"SBUF overflow {self.cur} {name}"
        return t

    def persist(self):
        self.pers = self.cur

    def reset(self):
        self.cur = self.pers


class Tl:
    __slots__ = ("h", "b")

    def __init__(self, h, name, excl=False):
        self.h = h
        self.b = Buf(name, excl)


class Rot:
    def __init__(self, tiles):
        self.t = tiles
        self.i = 0

    def next(self):
        t = self.t[self.i % len(self.t)]
        self.i += 1
        return t


class K:
    def __init__(self, L, debug=False, nphase=99):
        self.nphase = nphase
        assert L % 512 == 0
        self.L = L
        self.T = L // 128
        self.NOT = 7 * self.T + 1
        self.NO = self.NOT * 128
        self.NF = 3 * L
        self.NTK = self.NF + self.NO
        self.debug = debug
        self.nc = bass.Bass("TRN2", target_bir_lowering=False)
        self.S = Sched()
        self.mem = Mem(self.nc)
        self.ntl = 0

    def tile(self, shape, dt, name="t"):
        self.ntl += 1
        return Tl(self.mem.alloc(shape, dt, name=name), f"{name}{self.ntl}")

    def tiles(self, n, shape, dt, name="t"):
        return Rot([self.tile(shape, dt, name) for _ in range(n)])

    def dram_in(self, name, shape, dt=F32):
        return self.nc.dram_tensor(name, list(shape), dt, kind="ExternalInput").ap()

    def dram_scr(self, name, shape, dt):
        kind = "ExternalOutput" if self.debug else "Internal"
        if self.debug:
            self.dbg_names.append(name)
        return self.nc.dram_tensor(name, list(shape), dt, kind=kind).ap()

    def mm(self, out, lhsT, rhs, start, stop, r, w, pw=()):
        self.S.op("pe", lambda e: e.matmul(out, lhsT=lhsT, rhs=rhs, start=start, stop=stop), r, w, pw=pw)

    def tr(self, out, in_, ident, r, w, pw=()):
        self.S.op("pe", lambda e: e.transpose(out, in_, ident), r, w, pw=pw)

    def act(self, out, in_, func, r, w, scale=1.0, bias=0.0, accum=None, pw=()):
        if accum is None:
            self.S.op("act", lambda e: e.activation(out=out, in_=in_, func=func, scale=scale, bias=bias), r, w, pw=pw)
        else:
            self.S.op("act", lambda e: e.activation(out=out, in_=in_, func=func, scale=scale, bias=bias,
                                                    accum_out=accum), r, w, pw=pw)

    def ts(self, eng, out, in0, s1, op0, r, w, s2=None, op1=None, pw=()):
        if s2 is None:
            self.S.op(eng, lambda e: e.tensor_scalar(out=out, in0=in0, scalar1=s1, scalar2=None, op0=op0), r, w, pw=pw)
        else:
            self.S.op(eng, lambda e: e.tensor_scalar(out=out, in0=in0, scalar1=s1, scalar2=s2, op0=op0, op1=op1), r, w, pw=pw)

    def tt(self, eng, out, in0, in1, op, r, w, pw=()):
        self.S.op(eng, lambda e: e.tensor_tensor(out=out, in0=in0, in1=in1, op=op), r, w, pw=pw)

    def stt(self, out, in0, scalar, in1, op0, op1, r, w, pw=()):
        self.S.op("dve", lambda e: e.scalar_tensor_tensor(out=out, in0=in0, scalar=scalar, in1=in1, op0=op0, op1=op1), r, w, pw=pw)

    def cp(self, eng, out, in_, r, w, pw=()):
        if eng == "act":
            self.S.op("act", lambda e: e.copy(out=out, in_=in_), r, w, pw=pw)
        else:
            self.S.op(eng, lambda e: e.tensor_copy(out=out, in_=in_), r, w, pw=pw)

    def dma(self, q, out, in_, r, w, pw=()):
        return self.S.op(q, lambda e: e.dma_start(out=out, in_=in_), r, w, dma=True, pw=pw)

    def load(self, out, in_, w, r=(), pw=()):
        return self.dma("sp", out, in_, r, w, pw=pw)

    def store(self, out, in_, r, w=()):
        return self.dma(STORE_Q, out, in_, r, w)

    def memset(self, eng, ap, val, w):
        self.S.op(eng, lambda e: e.memset(ap, val), (), w)

    def hilo(self, src, srcb, hi, hib, lo, lob, eng="dve"):
        self.cp(eng, hi, src, [srcb], [hib])
        self.tt("dve", lo, src, hi, ALU.subtract, [srcb, hib], [lob])

    def wload(self, dst, src_v, nk, c0, c1, l0=None):
        if l0 is None:
            l0 = c0
        first = not dst.b.w
        for kc in range(nk):
            for a in range(c0, c1, 2048):
                b = min(c1, a + 2048)
                la = l0 + (a - c0)
                if first:
                    self.dma("pool", dst.h[:, kc, la:la + (b - a)], src_v[:, kc, a:b], (), [dst.b])
                    first = False
                else:
                    self.dma("pool", dst.h[:, kc, la:la + (b - a)], src_v[:, kc, a:b], (), (), pw=[dst.b])

    def build(self):
        nc, L, T, NOT, NO, NF, NTK = self.nc, self.L, self.T, self.NOT, self.NO, self.NF, self.NTK
        self.dbg_names = []
        I = {}
        I["xf"] = self.dram_in("xf", [NF, D])
        I["xo"] = self.dram_in("xo", [NO, D])
        I["cT"] = self.dram_in("cT", [128, 8 * 4])
        I["flg"] = self.dram_in("flg", [128, 3 * NOT])
        I["w_ada"] = self.dram_in("w_ada", [D, 6 * D])
        I["b_adaT"] = self.dram_in("b_adaT", [128, 48])
        I["b_ada"] = self.dram_in("b_ada", [1, 6 * D])
        I["n1T"] = self.dram_in("n1T", [128, 8])
        I["n2T"] = self.dram_in("n2T", [128, 8])
        I["w_in"] = self.dram_in("w_in", [D, DIN])
        I["qkg"] = self.dram_in("qkg", [128, 2])
        I["lamv"] = self.dram_in("lamv", [1, 256])
        I["subg"] = self.dram_in("subg", [128, 1])
        I["wgf"] = self.dram_in("wgf", [33, 512])
        I["wgb"] = self.dram_in("wgb", [33, 512])
        I["glng"] = self.dram_in("glng", [128, 2])
        I["w_ba"] = self.dram_in("w_ba", [D, D])
        I["w_bb"] = self.dram_in("w_bb", [D, D])
        I["w_out"] = self.dram_in("w_out", [D, D])
        I["w_up"] = self.dram_in("w_up", [D, DFF])
        I["w_dn"] = self.dram_in("w_dn", [DFF, D])
        I["rb"] = self.dram_in("rb", [32, 8])
        I["rbrow"] = self.dram_in("rbrow", [1, 256])
        I["oh"] = self.dram_in("oh", [32, 1280])
        I["cst"] = self.dram_in("cst", [128, 6 * 128])
        self.I = I
        self.y = nc.dram_tensor("y", [NF, D], F32, kind="ExternalOutput").ap()

        Sc = {}
        Sc["KT"] = self.dram_scr("KT", [8, 128, NTK], BF16)
        Sc["VH"] = self.dram_scr("VH", [8, 128, NTK // 128, 128], BF16)
        Sc["QT"] = self.dram_scr("QT", [8, 128, NF], BF16)
        Sc["GQT"] = self.dram_scr("GQT", [4, 128, NF], BF16)
        Sc["GKT"] = self.dram_scr("GKT", [4, 128, NF], BF16)
        Sc["GK"] = self.dram_scr("GK", [NF, 512], BF16)
        Sc["GV"] = self.dram_scr("GV", [NF, 1024], BF16)
        for nm in ("GGFH", "GGFL", "GGBH", "GGBL"):
            Sc[nm] = self.dram_scr(nm, [NF, 512], BF16)
        Sc["OGT"] = self.dram_scr("OGT", [8, 128, NF], BF16)
        Sc["GAT"] = self.dram_scr("GAT", [8, 128, NF], BF16)
        Sc["GBT"] = self.dram_scr("GBT", [8, 128, NF], BF16)
        Sc["OAT"] = self.dram_scr("OAT", [8, 128, NF], BF16)
        Sc["OBT"] = self.dram_scr("OBT", [8, 128, NF], BF16)
        Sc["X1"] = self.dram_scr("X1", [NF, D], F32)
        Sc["H2T"] = self.dram_scr("H2T", [8, 128, NF], BF16)
        Sc["GBC"] = self.dram_scr("GBC", [6, 128, D], F32)
        Sc["HT"] = self.dram_scr("HT", [8, 128, NTK], BF16)
        self.frev_t = nc.dram_tensor("FREV", [8, 1280], F32, kind="ExternalOutput" if self.debug else "Internal")
        if self.debug:
            self.dbg_names.append("FREV")
        Sc["SIN"] = self.dram_scr("SIN", [8, 128, 256], F32)
        self.Sc = Sc

        self.pspair = [Tl(nc.alloc_psum_tensor(f"pp{i}", [128, 1024], F32), f"pp{i}", excl=True) for i in range(4)]
        self.psum = [Tl(self.pspair[i // 2].h[:, (i % 2) * 512:(i % 2 + 1) * 512], f"ps{i}", excl=True) for i in range(8)]
        self.prot = Rot(self.psum)

        phases = [self.phase0, lambda: self.phase1(0), lambda: self.phase1(1), lambda: self.phase1(2),
                  self.phaseB, self.phaseC, self.phaseD1, self.phaseD2]
        for ph in phases[:self.nphase]:
            ph()
            self.S.barrier()
        return nc

    def phase0(self):
        nc, I, Sc, NOT = self.nc, self.I, self.Sc, self.NOT
        P = self
        cst = self.tile([128, 6 * 128], F32, "cst")
        self.load(cst.h[:], I["cst"][:, :], [cst.b])
        self.cst = cst
        self.ident = cst.h[:, 0:128]
        self.J32 = cst.h[:, 128:256]
        self.triF = cst.h[:, 256:384]
        self.triB = cst.h[:, 384:512]
        self.blk64 = cst.h[:, 512:640]
        self.ones32 = cst.h[:, 640:768]
        cstb = self.tile([128, 6 * 128], BF16, "cstb")
        self.cstb = cstb
        self.cp("dve", cstb.h[:], cst.h[:], [cst.b], [cstb.b])
        self.identb = cstb.h[:, 0:128]
        self.Jb = cstb.h[:, 128:256]
        self.triFb = cstb.h[:, 256:384]
        self.triBb = cstb.h[:, 384:512]
        self.blk64b = cstb.h[:, 512:640]
        self.onesb = cstb.h[:, 640:768]
        sm = self.tile([128, 64], F32, "sm")
        self.sm = sm
        self.load(sm.h[:, 0:8], I["n1T"][:, :], [sm.b])
        self.load(sm.h[:, 8:16], I["n2T"][:, :], [sm.b])
        self.load(sm.h[:, 16:18], I["qkg"][:, :], [sm.b])
        self.load(sm.h[:, 18:19], I["subg"][:, :], [sm.b])
        self.load(sm.h[:, 19:21], I["glng"][:, :], [sm.b])
        self.ts("dve", sm.h[:, 21:22], sm.h[:, 18:19], 0.8, ALU.mult, [sm.b], [sm.b])
        self.memset("dve", sm.h[:, 22:23], 0.0, [sm.b])
        amod = self.tile([128, 4 * 32], F32, "amod")
        self.amod = amod
        self.pers_late = self.mem.cur
        rbb = self.tile([128, 256], F32, "rbb")
        self.rbb = rbb
        self.load(rbb.h[:], I["rbrow"][0:1, :].partition_broadcast(128), [rbb.b])
        flg = self.tile([128, 3 * NOT], F32, "flg")
        self.flg = flg
        self.load(flg.h[:], I["flg"][:, :], [flg.b])
        nfl = self.tile([128, 2 * NOT], F32, "nfl")
        self.nfl = nfl
        self.ts("dve", nfl.h[:], flg.h[:, 0:2 * NOT], -1.0 / 16.0, ALU.mult, [flg.b], [nfl.b])
        fb = self.tile([128, 8 * NOT], F32, "fb")
        self.fb = fb
        for h in range(8):
            o = fb.h[:, h * NOT:(h + 1) * NOT]
            self.ts("dve", o, flg.h[:, 0:NOT], rbb.h[:, 15 * 8 + h:15 * 8 + h + 1], ALU.mult, [flg.b, rbb.b], [fb.b])
            self.stt(o, flg.h[:, NOT:2 * NOT], rbb.h[:, 31 * 8 + h:31 * 8 + h + 1], o, ALU.mult, ALU.add, [flg.b, rbb.b, fb.b], [fb.b])
            self.stt(o, flg.h[:, 2 * NOT:3 * NOT], NEG, o, ALU.mult, ALU.add, [flg.b, fb.b], [fb.b])
        wgh = self.tile([33, 1024], BF16, "wgh")
        wgl = self.tile([33, 1024], BF16, "wgl")
        self.wgh, self.wgl = wgh, wgl
        self.mem.persist()
        modT = self.tile([128, 6 * 8 * 4], F32, "modT")
        self.modT = modT
        lamt = self.tile([128, 256 + 8], F32, "lamt")
        self.load(lamt.h[:, 0:256], I["lamv"][0:1, :].partition_broadcast(128), [lamt.b])
        self.tt("dve", lamt.h[:, 0:64], lamt.h[:, 0:64], lamt.h[:, 64:128], ALU.mult, [lamt.b], [lamt.b])
        self.tt("dve", lamt.h[:, 128:192], lamt.h[:, 128:192], lamt.h[:, 192:256], ALU.mult, [lamt.b], [lamt.b])
        self.S.op("dve", lambda e: e.reduce_sum(out=lamt.h[:, 256:257], in_=lamt.h[:, 0:64], axis=mybir.AxisListType.X), [lamt.b], [lamt.b])
        self.S.op("dve", lambda e: e.reduce_sum(out=lamt.h[:, 257:258], in_=lamt.h[:, 128:192], axis=mybir.AxisListType.X), [lamt.b], [lamt.b])
        self.act(lamt.h[:, 258:260], lamt.h[:, 256:258], AF.Exp, [lamt.b], [lamt.b])
        self.tt("dve", lamt.h[:, 260:261], lamt.h[:, 259:260], lamt.h[:, 258:259], ALU.subtract, [lamt.b], [lamt.b])
        self.ts("dve", sm.h[:, 23:24], lamt.h[:, 260:261], -0.2, ALU.add, [lamt.b], [sm.b])
        self.neglam = sm.h[:, 23:24]

        wg = self.tile([33, 1024], F32, "wg")
        self.load(wg.h[:, 0:512], I["wgf"][:, :], [wg.b])
        self.load(wg.h[:, 512:1024], I["wgb"][:, :], [wg.b])
        self.hilo(wg.h[:], wg.b, wgh.h[:], wgh.b, wgl.h[:], wgl.b)
        cT = self.tile([128, 32], F32, "cT")
        self.load(cT.h[:], I["cT"][:, :], [cT.b])
        scT = self.tile([128, 32], F32, "scT")
        self.act(scT.h[:], cT.h[:], AF.Silu, [cT.b], [scT.b])
        badT = self.tile([128, 48], F32, "badT")
        self.load(badT.h[:], I["b_adaT"][:, :], [badT.b])
        screp = self.tile([128, 3 * 8 * 128], F32, "screp")
        for s in range(3):
            for kc in range(8):
                self.cp("dve", screp.h[:, (s * 8 + kc) * 128:(s * 8 + kc + 1) * 128],
                        scT.h[:, kc * 4 + s:kc * 4 + s + 1].to_broadcast([128, 128]), [scT.b], [screp.b])
        wa = self.tiles(2, [128, 8, 1024], F32, "wa")
        w_ada_v = I["w_ada"].rearrange("(kc p) c -> p kc c", p=128)
        bbc = self.tile([128, 1024], F32, "bbc")
        gst = self.tiles(2, [128, 1024], F32, "gst")
        for blk in range(6):
            w = wa.next()
            for kc in range(8):
                self.load(w.h[:, kc, :], w_ada_v[:, kc, blk * 1024:(blk + 1) * 1024], [w.b])
            if blk in (2, 5):
                self.load(bbc.h[:], I["b_ada"][0:1, blk * 1024:(blk + 1) * 1024].partition_broadcast(128), [bbc.b])
                for s in range(3):
                    g = gst.next()
                    for half in range(2):
                        ps = self.prot.next()
                        for kc in range(8):
                            self.mm(ps.h[:, :], screp.h[:, (s * 8 + kc) * 128:(s * 8 + kc + 1) * 128],
                                    w.h[:, kc, half * 512:(half + 1) * 512], kc == 0, kc == 7, [screp.b, w.b], [ps.b])
                        self.tt("dve", g.h[:, half * 512:(half + 1) * 512], ps.h[:, :], bbc.h[:, half * 512:(half + 1) * 512],
                                ALU.add, [ps.b, bbc.b], [g.b])
                    self.store(Sc["GBC"][(0 if blk == 2 else 3) + s, :, :], g.h[:], [g.b])
            else:
                for j in range(8):
                    ps = self.prot.next()
                    for kc in range(8):
                        self.mm(ps.h[:, 0:4], w.h[:, kc, j * 128:(j + 1) * 128], scT.h[:, kc * 4:kc * 4 + 4],
                                kc == 0, kc == 7, [w.b, scT.b], [ps.b])
                    self.ts("dve", modT.h[:, (blk * 8 + j) * 4:(blk * 8 + j) * 4 + 4], ps.h[:, 0:4],
                            badT.h[:, blk * 8 + j:blk * 8 + j + 1], ALU.add, [ps.b, badT.b], [modT.b])
        for j in range(8):
            for (dst, nrm, sblk, hblk) in ((0, 0, 1, 0), (2, 8, 4, 3)):
                self.ts("dve", amod.h[:, (dst * 8 + j) * 4:(dst * 8 + j) * 4 + 4], modT.h[:, (sblk * 8 + j) * 4:(sblk * 8 + j) * 4 + 4],
                        1.0, ALU.add, [modT.b], [amod.b], s2=self.sm.h[:, nrm + j:nrm + j + 1], op1=ALU.mult)
                self.cp("dve", amod.h[:, ((dst + 1) * 8 + j) * 4:((dst + 1) * 8 + j) * 4 + 4],
                        modT.h[:, (hblk * 8 + j) * 4:(hblk * 8 + j) * 4 + 4], [modT.b], [amod.b])
        rb = self.tile([32, 8], F32, "rb")
        self.load(rb.h[:], I["rb"][:, :], [rb.b])
        oh = self.tile([32, 1280], F32, "oh")
        self.load(oh.h[:], I["oh"][:, :], [oh.b])
        fr = self.tile([8, 1280], F32, "fr")
        for c0 in (0, 512, 1024):
            n = min(512, 1280 - c0)
            ps = self.prot.next()
            self.mm(ps.h[0:8, 0:n], rb.h[:, :], oh.h[:, c0:c0 + n], True, True, [rb.b, oh.b], [ps.b])
            self.cp("dve", fr.h[:, c0:c0 + n], ps.h[0:8, 0:n], [ps.b], [fr.b])
        self.store(self.frev_t.ap()[:, :], fr.h[:], [fr.b])
        self.mem.reset()

    def amod_ap(self, which, j, s):
        c = (which * 8 + j) * 4 + s
        return self.amod.h[:, c:c + 1]

    def norm_group(self, xts, seq, which, hT, scr, xn_rot, stat_rot):
        n = len(xts)
        xns = []
        for i, xt in enumerate(xts):
            st = stat_rot.next()
            self.act(scr.h[:], xt.h[:], AF.Square, [xt.b], [scr.b, st.b], accum=st.h[:, 0:1])
            self.act(st.h[:, 1:2], st.h[:, 0:1], AF.Ln, [st.b, self.epsT.b], [st.b], scale=1.0 / D, bias=self.eps_ap)
            self.act(st.h[:, 2:3], st.h[:, 1:2], AF.Exp, [st.b], [st.b], scale=-0.5)
            xn = xn_rot.next()
            self.ts("dve", xn.h[:], xt.h[:], st.h[:, 2:3], ALU.mult, [xt.b, st.b], [xn.b])
            xns.append(xn)
        for j in range(8):
            ps = self.prot.next()
            for i, xn in enumerate(xns):
                self.tr(ps.h[:, i * 128:(i + 1) * 128], xn.h[:, j * 128:(j + 1) * 128], self.ident, [xn.b, self.cst.b], [ps.b])
            w, pw = ([hT.b], ()) if j == 0 else ((), [hT.b])
            if j % 2 == 0:
                self.ts("dve", hT.h[:, j, 0:n * 128], ps.h[:, 0:n * 128], self.amod_ap(which, j, seq), ALU.mult,
                        [ps.b, self.amod.b], w, s2=self.amod_ap(which + 1, j, seq), op1=ALU.add, pw=pw)
            else:
                self.act(hT.h[:, j, 0:n * 128], ps.h[:, 0:n * 128], AF.Identity, [ps.b, self.amod.b], w,
                         scale=self.amod_ap(which, j, seq), bias=self.amod_ap(which + 1, j, seq), pw=pw)

    def mk_eps(self):
        epsT = self.tile([128, 1], F32, "eps")
        self.memset("dve", epsT.h[:], EPS, [epsT.b])
        self.eps_ap = epsT.h[:, 0:1]
        self.epsT = epsT

    def phase1(self, pss):
        nc, I, Sc, L, T, NOT = self.nc, self.I, self.Sc, self.L, self.T, self.NOT
        w_in_v = I["w_in"].rearrange("(kc p) c -> p kc c", p=128)
        if pss == 0:
            rr = [(C_KA, 1024), (C_QA, 1024), (C_VA, 1024)]
        elif pss == 1:
            rr = [(C_LR, 32), (C_KG, 512), (C_VG, 1024), (C_QG, 512)]
        else:
            rr = [(C_OG, 1024), (C_GA, 1024), (C_GB, 1024)]
        ranges = []
        for (a, w) in rr:
            wt = self.tile([128, 8, w], BF16, "win")
            self.wload(wt, w_in_v, 8, a, a + w, 0)
            ranges.append((a, a + w, wt))
        ncol = sum(b - a for a, b, _ in ranges)
        win = ranges[0][2]

        def lc(c):
            for (a, b, wt) in ranges:
                if a <= c < b:
                    return wt, c - a
            raise AssertionError(c)

        if os.environ.get("P1_WLOAD_ONLY"):
            o = self.tile([128, 1024], BF16, "dbgo")
            self.cp("dve", o.h[:], win.h[:, 7, 0:1024], [win.b], [o.b])
            self.store(Sc["GV"][0:128, :], o.h[:], [o.b])
            self.mem.reset()
            return
        self.mk_eps()
        if pss == 0:
            xrot = self.tiles(8, [128, 1024], F32, "x")
            xnrot = self.tiles(4, [128, 1024], F32, "xn")
            strot = self.tiles(4, [128, 4], F32, "st")
            scr = self.tile([128, 1024], BF16, "sqscr")
        hTs = self.tiles(2 if pss == 0 else 3, [128, 8, 512], BF16, "hT")
        f32s = self.tiles(4, [128, 512], F32, "f32s")
        bfs = self.tiles(4, [128, 512], BF16, "bfs")
        hls = self.tiles(4, [128, 512], BF16, "hls")
        if pss == 0:
            vst = self.tiles(3, [128, 1024], BF16, "vst")
        if pss == 1:
            kt_t = [[self.tile([128, 512], BF16, "ktm") for _ in range(4)] for _ in range(2)]
            gv_t = [[self.tile([128, 1024], BF16, "gvt") for _ in range(4)] for _ in range(2)]
            g_t = [[[(self.tile([128, 512], BF16, "gh"), self.tile([128, 512], BF16, "gl")) for _ in range(2)]
                    for _ in range(4)] for _ in range(2)]
            ke_t = [[[self.tile([128, 512], BF16, "ke") for _ in range(2)] for _ in range(4)] for _ in range(2)]
            lrT = self.tile([33, 512], BF16, "lrT")
            self.memset("dve", lrT.h[:], 1.0, [lrT.b])
            gst = self.tiles(4, [128, 512], F32, "gst")
            est = self.tiles(3, [128, 512], F32, "est")
            Et = self.tiles(8, [128, 16], F32, "Et")
            tmpS = self.tiles(2, [128, 256], F32, "tmpS")
            sin = self.tile([128, 8, 256], F32, "sin")
            self.memset("dve", sin.h[:], 0.0, [sin.b])

        groups = []
        KB = [0, L + self.NO, 2 * L + self.NO]
        for s in range(3):
            for q in range(L // 512):
                groups.append(("full", s, self.I["xf"], s * L + q * 512, 4, KB[s] + q * 512, s * L + q * 512, None))
        if pss < 2:
            u = 0
            while u < NOT:
                n = min(4, NOT - u)
                groups.append(("other", 0, self.I["xo"], u * 128, n, L + u * 128, None, u))
                u += n

        def load_x(g):
            kind, s, src, r0, n, kb, qb, u0 = g
            xts = []
            for i in range(n):
                xt = xrot.next()
                self.load(xt.h[:], src[r0 + i * 128:r0 + (i + 1) * 128, :], [xt.b])
                xts.append(xt)
            return xts

        def fm_proj(c0, hT, n, m=128):
            ps = self.prot.next()
            wt, l = lc(c0)
            for kc in range(8):
                self.mm(ps.h[0:m, 0:n], wt.h[:, kc, l:l + m], hT.h[:, kc, 0:n], kc == 0, kc == 7, [wt.b, hT.b], [ps.b])
            return ps

        def tm_proj(c0, hT, i):
            ps = self.prot.next()
            wt, l = lc(c0)
            for kc in range(8):
                self.mm(ps.h[:, :], hT.h[:, kc, i * 128:(i + 1) * 128], wt.h[:, kc, l:l + 512], kc == 0, kc == 7,
                        [wt.b, hT.b], [ps.b])
            return ps

        def qk_norm(ps, n, gcol, dst):
            if os.environ.get("QKN") == "0":
                o = bfs.next()
                self.cp("dve", o.h[:, 0:n], ps.h[:, 0:n], [ps.b], [o.b])
                self.store(dst, o.h[:, 0:n], [o.b])
                return
            sh = hls.next()
            self.act(sh.h[:, 0:n], ps.h[:, 0:n], AF.Square, [ps.b], [sh.b])
            raw = f32s.next()
            self.cp("dve", raw.h[:, 0:n], ps.h[:, 0:n], [ps.b], [raw.b])
            p2 = self.prot.next()
            self.mm(p2.h[:, 0:n], self.blk64b, sh.h[:, 0:n], True, True, [self.cstb.b, sh.b], [p2.b])
            ln = f32s.next()
            self.act(ln.h[:, 0:n], p2.h[:, 0:n], AF.Ln, [p2.b, self.epsT.b], [ln.b], scale=1.0 / 64, bias=self.eps_ap)
            self.act(ln.h[:, 0:n], ln.h[:, 0:n], AF.Exp, [ln.b], [ln.b], scale=-0.5)
            o = bfs.next()
            self.stt(o.h[:, 0:n], raw.h[:, 0:n], self.sm.h[:, gcol:gcol + 1], ln.h[:, 0:n], ALU.mult, ALU.mult,
                     [raw.b, ln.b, self.sm.b], [o.b])
            self.store(dst, o.h[:, 0:n], [o.b])

        def load_h(g):
            kind, s, src, r0, n, kb, qb, u0 = g
            hT_ = hTs.next()
            self.load(hT_.h[:, :, 0:n * 128], Sc["HT"][:, :, kb:kb + n * 128].rearrange("j p t -> p j t"), [hT_.b])
            return hT_

        nxt = load_x(groups[0]) if pss == 0 else load_h(groups[0])
        for gi, g in enumerate(groups):
            kind, s, src, r0, n, kb, qb, u0 = g
            N = n * 128
            if pss == 0:
                xts = nxt
                if gi + 1 < len(groups):
                    nxt = load_x(groups[gi + 1])
                hT = hTs.next()
                self.norm_group(xts, s, 0, hT, scr, xnrot, strot)
                self.store(Sc["HT"][:, :, kb:kb + N].rearrange("j p t -> p j t"), hT.h[:, :, 0:N], [hT.b])
            else:
                hT = nxt
                if gi + 1 < len(groups):
                    nxt = load_h(groups[gi + 1])
            if pss == 0:
                blocks = [(C_KA + h * 128, 17, Sc["KT"][h, :, kb:kb + N]) for h in range(8)]
                if kind == "full":
                    blocks += [(C_QA + h * 128, 16, Sc["QT"][h, :, qb:qb + N]) for h in range(8)]
                ps_n = fm_proj(blocks[0][0], hT, N)
                for bi, (c0, gcol, dst) in enumerate(blocks):
                    ps = ps_n
                    if bi + 1 < len(blocks):
                        ps_n = fm_proj(blocks[bi + 1][0], hT, N)
                    qk_norm(ps, N, gcol, dst)
                for i in range(n):
                    v = vst.next()
                    for half in range(2):
                        ps = tm_proj(C_VA + half * 512, hT, i)
                        if half == 0:
                            self.cp("dve", v.h[:, 0:512], ps.h[:, :], [ps.b], [v.b])
                        else:
                            self.cp("act", v.h[:, 512:1024], ps.h[:, :], [ps.b], (), pw=[v.b])
                    kt = (kb + i * 128) // 128
                    self.store(Sc["VH"][:, :, kt, :].rearrange("h p d -> p h d"),
                               v.h[:].rearrange("p (h d) -> p h d", h=8), [v.b])
            elif pss == 1:
                par = gi % 2
                ps = fm_proj(C_LR, hT, N, m=32)
                self.cp("dve", lrT.h[0:32, 0:N], ps.h[0:32, 0:N], [ps.b], [lrT.b])
                ktms, gvs = [], []
                for i in range(n):
                    ps = tm_proj(C_KG, hT, i)
                    ktm = kt_t[par][i]
                    self.cp("act", ktm.h[:], ps.h[:, :], [ps.b], [ktm.b])
                    gv = gv_t[par][i]
                    for half in range(2):
                        ps = tm_proj(C_VG + half * 512, hT, i)
                        if half == 0:
                            self.cp("dve", gv.h[:, 0:512], ps.h[:, :], [ps.b], [gv.b])
                        else:
                            self.cp("act", gv.h[:, 512:1024], ps.h[:, :], [ps.b], (), pw=[gv.b])
                    ktms.append(ktm)
                    gvs.append(gv)
                    if kind == "full":
                        self.store(Sc["GK"][qb + i * 128:qb + (i + 1) * 128, :], ktm.h[:], [ktm.b])
                        self.store(Sc["GV"][qb + i * 128:qb + (i + 1) * 128, :], gv.h[:], [gv.b])
                gfbs = []
                for i in range(n):
                    gfb = []
                    for d in range(2):
                        pz = self.prot.next()
                        self.mm(pz.h[:, :], lrT.h[0:33, i * 128:(i + 1) * 128], self.wgh.h[0:33, d * 512:(d + 1) * 512], True, False,
                                [lrT.b, self.wgh.b], [pz.b])
                        self.mm(pz.h[:, :], lrT.h[0:33, i * 128:(i + 1) * 128], self.wgl.h[0:33, d * 512:(d + 1) * 512], False, True,
                                [lrT.b, self.wgl.b], [pz.b])
                        e1 = est.next()
                        self.act(e1.h[:], pz.h[:, :], AF.Exp, [pz.b], [e1.b], scale=-1.0)
                        self.act(e1.h[:], e1.h[:], AF.Ln, [e1.b], [e1.b], bias=1.0)
                        gt = gst.next()
                        if kind == "full":
                            self.ts("dve", gt.h[:], e1.h[:], -1.0 / 16.0, ALU.mult, [e1.b], [gt.b])
                        else:
                            uu = u0 + i
                            self.ts("dve", gt.h[:], e1.h[:], self.nfl.h[:, d * NOT + uu:d * NOT + uu + 1], ALU.mult,
                                    [e1.b, self.nfl.b], [gt.b])
                        gh, gl = g_t[par][i][d]
                        self.hilo(gt.h[:], gt.b, gh.h[:], gh.b, gl.h[:], gl.b)
                        if kind == "full":
                            nm = "GGF" if d == 0 else "GGB"
                            self.store(Sc[nm + "H"][qb + i * 128:qb + (i + 1) * 128, :], gh.h[:], [gh.b])
                            self.store(Sc[nm + "L"][qb + i * 128:qb + (i + 1) * 128, :], gl.h[:], [gl.b])
                        gfb.append((gh, gl))
                    gfbs.append(gfb)
                if kind == "other":
                    Es, kes_ = [], []
                    for i in range(n):
                        uu = u0 + i
                        gfb = gfbs[i]
                        E = Et.next()
                        pt = self.prot.next()
                        for d in range(2):
                            for hd in range(4):
                                col = (hd * 2 + d) * 2
                                for hl in range(2):
                                    self.mm(pt.h[:, col:col + 2], gfb[d][hl].h[:, hd * 128:(hd + 1) * 128], self.onesb[:, 0:2], hl == 0, hl == 1,
                                            [gfb[d][hl].b, self.cstb.b], [pt.b])
                        self.act(E.h[:], pt.h[:, 0:16], AF.Exp, [pt.b], [E.b])
                        kk = []
                        for d in range(2):
                            pb = self.prot.next()
                            for hl in range(2):
                                self.mm(pb.h[:, :], self.triFb if d == 0 else self.triBb, gfb[d][hl].h[:], hl == 0, hl == 1,
                                        [self.cstb.b, gfb[d][hl].b], [pb.b])
                            em = est.next()
                            self.act(em.h[:], pb.h[:, :], AF.Exp, [pb.b], [em.b], scale=-1.0)
                            ke = ke_t[par][i][d]
                            self.stt(ke.h[:], ktms[i].h[:], self.flg.h[:, d * NOT + uu:d * NOT + uu + 1], em.h[:], ALU.mult, ALU.mult,
                                     [ktms[i].b, self.flg.b, em.b], [ke.b])
                            kk.append(ke)
                        Es.append(E)
                        kes_.append(kk)
                    for i in range(n):
                        E, gv = Es[i], gvs[i]
                        for d in range(2):
                            ke = kes_[i][d]
                            for hp in range(2):
                                pu = self.prot.next()
                                for k2 in range(2):
                                    hd = hp * 2 + k2
                                    self.mm(pu.h[:, k2 * 256:(k2 + 1) * 256], ke.h[:, hd * 128:(hd + 1) * 128],
                                            gv.h[:, hd * 256:(hd + 1) * 256], True, True, [ke.b, gv.b], [pu.b])
                                for k2 in range(2):
                                    hd = hp * 2 + k2
                                    tm = tmpS.next()
                                    self.tt("dve", tm.h[:], pu.h[:, k2 * 256:(k2 + 1) * 256], sin.h[:, hd * 2 + d, :], ALU.add,
                                            [pu.b, sin.b], [tm.b])
                                    col = (hd * 2 + d) * 2
                                    self.ts("dve", sin.h[:, hd * 2 + d, :], tm.h[:], E.h[:, col:col + 1], ALU.mult,
                                            [tm.b, E.b], [sin.b])
                if kind == "full":
                    for hd in range(4):
                        ps = fm_proj(C_QG + hd * 128, hT, N)
                        o = bfs.next()
                        self.act(o.h[:, 0:N], ps.h[:, 0:N], AF.Copy, [ps.b], [o.b], scale=128 ** -0.5)
                        self.store(Sc["GQT"][hd, :, qb:qb + N], o.h[:, 0:N], [o.b])
                        ps = fm_proj(C_KG + hd * 128, hT, N)
                        o = bfs.next()
                        self.cp("dve", o.h[:, 0:N], ps.h[:, 0:N], [ps.b], [o.b])
                        self.store(Sc["GKT"][hd, :, qb:qb + N], o.h[:, 0:N], [o.b])
            else:
                for (c0, fn, nm) in ((C_OG, AF.Silu, "OGT"), (C_GA, AF.Sigmoid, "GAT"), (C_GB, AF.Sigmoid, "GBT")):
                    for j in range(8):
                        ps = fm_proj(c0 + j * 128, hT, N)
                        o = bfs.next()
                        self.act(o.h[:, 0:N], ps.h[:, 0:N], fn, [ps.b], [o.b])
                        self.store(Sc[nm][j, :, qb:qb + N], o.h[:, 0:N], [o.b])
        if pss == 1:
            self.store(Sc["SIN"].rearrange("k p e -> p k e"), sin.h[:], [sin.b])
        self.mem.reset()

    def phaseB(self):
        nc, I, Sc, L, T, NOT = self.nc, self.I, self.Sc, self.L, self.T, self.NOT
        NKT0 = T + NOT
        self.mk_eps()
        epsT = self.epsT
        KTbig = self.tile([128, NKT0 * 128], BF16, "KTbig")
        Vbig = self.tile([128, NKT0, 128], BF16, "Vbig")
        KTsm = self.tiles(2, [128, T * 128], BF16, "KTsm")
        Vsm = self.tiles(2, [128, T, 128], BF16, "Vsm")
        QTs = self.tiles(2, [128, L], BF16, "QTs")
        Wp = self.tile([128, 1152], F32, "Wp")
        T32 = self.tile([128, 1408], F32, "T32")
        Hr = self.tile([128, 1408], BF16, "Hr")
        Lr = self.tile([128, 1408], BF16, "Lr")
        Whi = self.tiles(2, [128, 1408], BF16, "Whi")
        Wlo = self.tiles(2, [128, 1408], BF16, "Wlo")
        Pm = self.tiles(3, [128, 1024], BF16, "P")
        acc = self.tile([128, 512], F32, "acc")
        acc1 = self.tile([128, 512], F32, "acc1")
        hl2 = self.tiles(8, [128, 512], BF16, "hl2")
        fin = self.tiles(4, [128, 512], F32, "fin")
        a32 = self.tiles(2, [128, 512], F32, "a32")
        rl = self.tiles(2, [128, 1024], F32, "rl")
        o32 = self.tiles(2, [128, 512], F32, "o32")
        sq32 = self.tiles(2, [128, 512], F32, "sq32")
        ost = self.tiles(2, [128, 512], BF16, "ost")
        hls = self.tiles(2, [128, 512], BF16, "hlsB")
        Spair = Rot(self.pspair[0:2])
        L1 = self.psum[4]
        Fb = self.psum[5]
        Obank = self.psum[6:8]
        KB = [0, L + self.NO, 2 * L + self.NO]
        NG = L // 512

        def load_hs(h, s):
            nkt = NKT0 if s == 0 else T
            if s == 0:
                kt, v = KTbig, Vbig
            else:
                kt, v = KTsm.next(), Vsm.next()
            qt = QTs.next()
            t0 = KB[s] // 128
            step = 32
            for a in range(0, nkt, step):
                b = min(nkt, a + step)
                w, pw = ([kt.b], ()) if a == 0 else ((), [kt.b])
                self.load(kt.h[:, a * 128:b * 128], Sc["KT"][h, :, KB[s] + a * 128:KB[s] + b * 128], w, pw=pw)
                w, pw = ([v.b], ()) if a == 0 else ((), [v.b])
                self.load(v.h[:, a:b, :], Sc["VH"][h, :, t0 + a:t0 + b, :], w, pw=pw)
            self.load(qt.h[:, :], Sc["QT"][h, :, s * L:(s + 1) * L], [qt.b])
            return kt, v, qt

        def build_w(h):
            rb15 = self.rbb.h[:, 15 * 8 + h:15 * 8 + h + 1]
            rb31 = self.rbb.h[:, 31 * 8 + h:31 * 8 + h + 1]
            self.load(Wp.h[:], bass.AP(self.frev_t, h * 1280, [[1, 128], [1, 1152]]), [Wp.b])
            self.ts("dve", T32.h[:, 0:1152], Wp.h[:], 8.0, ALU.mult, [Wp.b], [T32.b])
            self.ts("dve", T32.h[:, 1152:1280], Wp.h[:, 640:768], rb15, ALU.subtract, [Wp.b, self.rbb.b], (), s2=8.0, op1=ALU.mult, pw=[T32.b])
            self.ts("dve", T32.h[:, 1280:1408], Wp.h[:, 384:512], rb31, ALU.subtract, [Wp.b, self.rbb.b], (), s2=8.0, op1=ALU.mult, pw=[T32.b])
            self.cp("dve", Hr.h[:], T32.h[:], [T32.b], [Hr.b])
            self.tt("dve", Lr.h[:], T32.h[:], Hr.h[:], ALU.subtract, [T32.b, Hr.b], [Lr.b])
            wh, wl = Whi.next(), Wlo.next()
            for (src, dst) in ((Hr, wh), (Lr, wl)):
                for c0 in (0, 512, 1024):
                    n = min(512, 1408 - c0)
                    ps = Spair.next()
                    self.mm(ps.h[:, 0:n], self.Jb, src.h[:, c0:c0 + n], True, True, [self.cstb.b, src.b], [ps.b])
                    w, pw = ([dst.b], ()) if c0 == 0 else ((), [dst.b])
                    self.cp("act", dst.h[:, c0:c0 + n], ps.h[:, 0:n], [ps.b], w, pw=pw)
            return wh, wl

        order = [(h, s) for h in range(8) for s in range(3)]
        nxt = load_hs(*order[0])
        wcur = None
        pending = [None]
        for oi, (h, s) in enumerate(order):
            kt, v, qt = nxt
            if s == 0:
                wcur = build_w(h)
            wh, wl = wcur
            if oi + 1 < len(order):
                nxt = load_hs(*order[oi + 1])
            nkt = NKT0 if s == 0 else T
            rb15 = self.rbb.h[:, 15 * 8 + h:15 * 8 + h + 1]
            rb31 = self.rbb.h[:, 31 * 8 + h:31 * 8 + h + 1]
            zero = self.sm.h[:, 22:23]
            steps = [(g, t) for g in range(NG) for t in range(nkt)]
            sps = {}

            def stageA(i):
                g, t = steps[i]
                band = None
                adj = None
                if t < T:
                    dt = t - 4 * g
                    if -1 <= dt <= 4:
                        band = 128 * (4 - dt)
                        bias = zero
                    elif dt < -1:
                        bias = rb15
                    else:
                        bias = rb31
                    bdep = [self.rbb.b, self.sm.b]
                else:
                    u = t - T
                    bias = self.fb.h[:, h * NOT + u:h * NOT + u + 1]
                    bdep = [self.fb.b]
                    if u == NOT - 2 and g == 0:
                        adj = (1152, 0)
                    if u == NOT - 1 and g == NG - 1:
                        adj = (1280, 384)
                sp = Spair.next()
                for m in range(2):
                    o = sp.h[:, m * 512:(m + 1) * 512]
                    last = (band is None and adj is None)
                    self.mm(o, kt.h[m * 64:(m + 1) * 64, t * 128:(t + 1) * 128],
                            qt.h[m * 64:(m + 1) * 64, g * 512:(g + 1) * 512], True, last, [kt.b, qt.b], [sp.b])
                    if band is not None:
                        self.mm(o, self.identb, wh.h[:, band:band + 512], False, False, [self.cstb.b, wh.b], [sp.b])
                        self.mm(o, self.identb, wl.h[:, band:band + 512], False, True, [self.cstb.b, wl.b], [sp.b])
                    if adj is not None:
                        wc, qc = adj
                        o2 = sp.h[:, m * 512 + qc:m * 512 + qc + 128]
                        self.mm(o2, self.identb, wh.h[:, wc:wc + 128], False, False, [self.cstb.b, wh.b], [sp.b])
                        self.mm(o2, self.identb, wl.h[:, wc:wc + 128], False, True, [self.cstb.b, wl.b], [sp.b])
                sps[i] = (sp, bias, bdep)

            def stageBC(i):
                g, t = steps[i]
                sp, bias, bdep = sps.pop(i)
                p = Pm.next()
                self.act(p.h[:], sp.h[:, :], AF.Exp, [sp.b] + bdep, [p.b], scale=0.125, bias=bias)
                if t == 0:
                    self.cp("dve", acc.h[:], p.h[:, 0:512], [p.b], [acc.b])
                else:
                    self.tt("dve", acc.h[:], acc.h[:], p.h[:, 0:512], ALU.add, [acc.b, p.b], [acc.b])
                for m in range(2):
                    self.mm(Obank[m].h[:, :], v.h[:, t, :], p.h[:, m * 512:(m + 1) * 512], t == 0, t == nkt - 1,
                            [v.b, p.b], [Obank[m].b])
                self.mm(L1.h[:, :], self.onesb, p.h[:, 512:1024], t == 0, t == nkt - 1, [self.cstb.b, p.b], [L1.b])

            def evac(g, h=h, s=s):
                o0s, o1s = fin.next(), fin.next()
                self.cp("dve", o0s.h[:], Obank[0].h[:, :], [Obank[0].b], [o0s.b])
                self.cp("act", o1s.h[:], Obank[1].h[:, :], [Obank[1].b], [o1s.b])
                r = rl.next()
                self.act(r.h[:, 0:512], L1.h[:, :], AF.Ln, [L1.b], [r.b])
                ah, al = hl2.next(), hl2.next()
                self.hilo(acc.h[:], acc.b, ah.h[:], ah.b, al.h[:], al.b)

                def rest():
                    self.mm(Fb.h[:, :], self.onesb, ah.h[:], True, False, [self.cstb.b, ah.b], [Fb.b])
                    self.mm(Fb.h[:, :], self.onesb, al.h[:], False, True, [self.cstb.b, al.b], [Fb.b])
                    self.act(r.h[:, 512:1024], Fb.h[:, :], AF.Ln, [Fb.b], (), pw=[r.b])
                    self.act(r.h[:], r.h[:], AF.Exp, [r.b], [r.b], scale=-1.0)
                    a0, a1 = a32.next(), a32.next()
                    self.tt("dve", a0.h[:], o0s.h[:], r.h[:, 512:1024], ALU.mult, [o0s.b, r.b], [a0.b])
                    self.tt("dve", a1.h[:], o1s.h[:], r.h[:, 0:512], ALU.mult, [o1s.b, r.b], [a1.b])
                    o = o32.next()
                    self.stt(o.h[:], a1.h[:], self.neglam, a0.h[:], ALU.mult, ALU.add, [a0.b, a1.b, self.sm.b], [o.b])
                    sq = sq32.next()
                    self.act(sq.h[:], o.h[:], AF.Square, [o.b], [sq.b])
                    sh, sl = hls.next(), hls.next()
                    self.hilo(sq.h[:], sq.b, sh.h[:], sh.b, sl.h[:], sl.b)
                    self.mm(Fb.h[:, :], self.onesb, sh.h[:], True, False, [self.cstb.b, sh.b], [Fb.b])
                    self.mm(Fb.h[:, :], self.onesb, sl.h[:], False, True, [self.cstb.b, sl.b], [Fb.b])
                    self.act(sq.h[:], Fb.h[:, :], AF.Ln, [Fb.b, epsT.b], [sq.b], scale=1.0 / 128, bias=epsT.h[:, 0:1])
                    self.act(sq.h[:], sq.h[:], AF.Exp, [sq.b], [sq.b], scale=-0.5)
                    ob = ost.next()
                    self.stt(ob.h[:], o.h[:], self.sm.h[:, 21:22], sq.h[:], ALU.mult, ALU.mult, [o.b, sq.b, self.sm.b], [ob.b])
                    self.store(Sc["OAT"][h, :, s * L + g * 512:s * L + (g + 1) * 512], ob.h[:], [ob.b])
                return rest

            for g in range(NG):
                base = g * nkt
                stageA(base)
                for i in range(nkt):
                    if i + 1 < nkt:
                        stageA(base + i + 1)
                    stageBC(base + i)
                    if i == 2 and pending[0] is not None:
                        pending[0]()
                        pending[0] = None
                assert pending[0] is None
                pending[0] = evac(g)
        if pending[0] is not None:
            pending[0]()
        self.mem.pers = self.pers_late
        self.mem.reset()

    def phaseC(self):
        nc, I, Sc, L, T, NOT = self.nc, self.I, self.Sc, self.L, self.T, self.NOT
        self.mk_eps()
        epsT = self.epsT
        mask2 = self.tile([128, 256], F32, "mask2")
        self.cp("dve", mask2.h[:, 0:128], self.triF, [self.cst.b], [mask2.b])
        self.cp("dve", mask2.h[:, 128:256], self.triB, [self.cst.b], (), pw=[mask2.b])
        sinp = self.tile([128, 8, 256], F32, "sinp")
        self.load(sinp.h[:], Sc["SIN"].rearrange("k p e -> p k e"), [sinp.b])
        qTs = self.tiles(2, [128, L], BF16, "qT")
        kTs = self.tiles(2, [128, L], BF16, "kT")
        ktms = self.tiles(1, [128, T, 128], BF16, "ktm")
        vtms = self.tiles(1, [128, T, 256], BF16, "vtm")
        gfs = [self.tiles(1, [128, 2, T, 128], BF16, "gf"), self.tiles(1, [128, 2, T, 128], BF16, "gb")]
        ogs = self.tiles(1, [128, 2, L], BF16, "og")
        ke = [self.tile([128, T, 128], BF16, "ke_f"), self.tile([128, T, 128], BF16, "ke_b")]
        qeT = [self.tile([128, L], BF16, "qeT_f"), self.tile([128, L], BF16, "qeT_b")]
        keT = [self.tile([128, L], BF16, "keT_f"), self.tile([128, L], BF16, "keT_b")]
        Ecol = [self.tile([128, T], F32, "E_f"), self.tile([128, T], F32, "E_b")]
        Sin = [[Tl(self.mem.alloc([128, 256], BF16, name="Sin"), f"Sin{d}_{c}") for c in range(T)] for d in range(2)]
        S32 = [self.tile([128, 256], F32, "S32f"), self.tile([128, 256], F32, "S32b")]
        e32 = self.tiles(3, [128, 512], F32, "e32")
        tmp = self.tiles(2, [128, 256], F32, "tmpc")
        aTm = self.tiles(3, [128, 256], BF16, "aTm")
        sq = self.tiles(2, [128, 2, 512], F32, "sqc")
        nrm = self.tiles(2, [128, 512], F32, "nrm")
        t32 = self.tiles(2, [128, 512], F32, "t32")
        ost = self.tiles(3, [128, 512], BF16, "ostc")
        tri = [self.triFb, self.triBb]
        hlsC = self.tiles(2, [128, 2, 512], BF16, "hlsC")
        NQ = T // 4

        def load_a(s, hd):
            q, k = qTs.next(), kTs.next()
            r = slice(s * L, (s + 1) * L)
            self.load(q.h[:, :], Sc["GQT"][hd, :, r], [q.b])
            self.load(k.h[:, :], Sc["GKT"][hd, :, r], [k.b])
            return q, k

        def load_b(s, hd):
            ktm, vtm, gf, gb, og = ktms.next(), vtms.next(), gfs[0].next(), gfs[1].next(), ogs.next()
            r = slice(s * L, (s + 1) * L)
            self.load(ktm.h[:], Sc["GK"][r, hd * 128:(hd + 1) * 128].rearrange("(t p) d -> p t d", p=128), [ktm.b])
            self.load(vtm.h[:], Sc["GV"][r, hd * 256:(hd + 1) * 256].rearrange("(t p) d -> p t d", p=128), [vtm.b])
            for (gt_, nm) in ((gf, "GGF"), (gb, "GGB")):
                self.load(gt_.h[:, 0, :, :], Sc[nm + "H"][r, hd * 128:(hd + 1) * 128].rearrange("(t p) d -> p t d", p=128), [gt_.b])
                self.load(gt_.h[:, 1, :, :], Sc[nm + "L"][r, hd * 128:(hd + 1) * 128].rearrange("(t p) d -> p t d", p=128), (), pw=[gt_.b])
            self.load(og.h[:], Sc["OGT"][2 * hd:2 * hd + 2, :, r].rearrange("j p t -> p j t"), [og.b])
            return ktm, vtm, gf, gb, og

        order = [(s, hd) for s in range(3) for hd in range(4)]
        nxt = load_a(*order[0])
        for oi, (s, hd) in enumerate(order):
            q, k = nxt
            ktm, vtm, gf, gb, og = load_b(s, hd)
            if oi + 1 < len(order):
                nxt = load_a(*order[oi + 1])
            gg = [gf, gb]
            for d in range(2):
                for qd in range(NQ):
                    cs = slice(qd * 512, (qd + 1) * 512)
                    ps = self.prot.next()
                    for cc in range(4):
                        c = qd * 4 + cc
                        for hl in range(2):
                            self.mm(ps.h[:, cc * 128:(cc + 1) * 128], tri[d], gg[d].h[:, hl, c, :], hl == 0, hl == 1, [self.cstb.b, gg[d].b], [ps.b])
                    em = e32.next()
                    self.act(em.h[:], ps.h[:, :], AF.Exp, [ps.b], [em.b], scale=-1.0)
                    self.tt("dve", ke[d].h[:, qd * 4:(qd + 1) * 4, :].rearrange("p t d -> p (t d)"),
                            ktm.h[:, qd * 4:(qd + 1) * 4, :].rearrange("p t d -> p (t d)"), em.h[:], ALU.mult,
                            [ktm.b, em.b], [ke[d].b])
                    ps = self.prot.next()
                    for cc in range(4):
                        c = qd * 4 + cc
                        for hl in range(2):
                            self.mm(ps.h[:, cc * 128:(cc + 1) * 128], gg[d].h[:, hl, c, :], tri[d], hl == 0, hl == 1, [self.cstb.b, gg[d].b], [ps.b])
                    ep = e32.next()
                    self.act(ep.h[:], ps.h[:, :], AF.Exp, [ps.b], [ep.b])
                    em2 = e32.next()
                    self.act(em2.h[:], ps.h[:, :], AF.Exp, [ps.b], [em2.b], scale=-1.0)
                    self.tt("dve", qeT[d].h[:, cs], q.h[:, cs], ep.h[:], ALU.mult, [q.b, ep.b], [qeT[d].b])
                    self.tt("dve", keT[d].h[:, cs], k.h[:, cs], em2.h[:], ALU.mult, [k.b, em2.b], [keT[d].b])
                    off = 127 if d == 0 else 0
                    self.cp("dve", Ecol[d].h[:, qd * 4:(qd + 1) * 4], ep.h[:].rearrange("p (c t) -> p c t", t=128)[:, :, off],
                            [ep.b], [Ecol[d].b])
            for d in range(2):
                if s == 0:
                    self.cp("dve", S32[d].h[:], sinp.h[:, hd * 2 + d, :], [sinp.b], [S32[d].b])
                else:
                    self.memset("dve", S32[d].h[:], 0.0, [S32[d].b])
                cords = list(range(T)) if d == 0 else list(range(T - 1, -1, -1))
                self.cp("act", Sin[d][cords[0]].h[:], S32[d].h[:], [S32[d].b], [Sin[d][cords[0]].b])
                for ci, c in enumerate(cords):
                    if ci == T - 1:
                        break
                    pu = self.prot.next()
                    self.mm(pu.h[:, 0:256], ke[d].h[:, c, :], vtm.h[:, c, :], True, True, [ke[d].b, vtm.b], [pu.b])
                    tm = tmp.next()
                    self.tt("dve", tm.h[:], pu.h[:, 0:256], S32[d].h[:], ALU.add, [pu.b, S32[d].b], [tm.b])
                    self.ts("dve", S32[d].h[:], tm.h[:], Ecol[d].h[:, c:c + 1], ALU.mult, [tm.b, Ecol[d].b], [S32[d].b])
                    nx = Sin[d][cords[ci + 1]]
                    self.act(nx.h[:], tm.h[:], AF.Identity, [tm.b, Ecol[d].b], [nx.b], scale=Ecol[d].h[:, c:c + 1])
            for qd in range(NQ):
                po = [self.prot.next(), self.prot.next()]
                for cc in range(4):
                    c = qd * 4 + cc
                    cs = slice(c * 128, (c + 1) * 128)
                    pa = self.prot.next()
                    for d in range(2):
                        self.mm(pa.h[:, d * 128:(d + 1) * 128], keT[d].h[:, cs], qeT[d].h[:, cs], True, True,
                                [keT[d].b, qeT[d].b], [pa.b])
                    am = aTm.next()
                    self.tt("dve", am.h[:], pa.h[:, 0:256], mask2.h[:], ALU.mult, [pa.b, mask2.b], [am.b])
                    for eb in range(2):
                        es = slice(eb * 128, (eb + 1) * 128)
                        o = po[eb].h[:, cc * 128:(cc + 1) * 128]
                        self.mm(o, vtm.h[:, c, es], am.h[:, 0:128], True, False, [vtm.b, am.b], [po[eb].b])
                        self.mm(o, vtm.h[:, c, es], am.h[:, 128:256], False, False, [vtm.b, am.b], [po[eb].b])
                        self.mm(o, Sin[0][c].h[:, es], qeT[0].h[:, cs], False, False, [Sin[0][c].b, qeT[0].b], [po[eb].b])
                        self.mm(o, Sin[1][c].h[:, es], qeT[1].h[:, cs], False, True, [Sin[1][c].b, qeT[1].b], [po[eb].b])
                sqt = sq.next()
                for eb in range(2):
                    self.act(sqt.h[:, eb, :], po[eb].h[:, :], AF.Square, [po[eb].b], [sqt.b])
                pn = self.prot.next()
                sh, sl = hlsC.next(), hlsC.next()
                self.hilo(sqt.h[:], sqt.b, sh.h[:], sh.b, sl.h[:], sl.b)
                for eb in range(2):
                    self.mm(pn.h[:, :], self.onesb, sh.h[:, eb, :], eb == 0, False, [self.cstb.b, sh.b], [pn.b])
                    self.mm(pn.h[:, :], self.onesb, sl.h[:, eb, :], False, eb == 1, [self.cstb.b, sl.b], [pn.b])
                nr = nrm.next()
                self.act(nr.h[:], pn.h[:, :], AF.Ln, [pn.b, epsT.b], [nr.b], scale=1.0 / 256, bias=epsT.h[:, 0:1])
                self.act(nr.h[:], nr.h[:], AF.Exp, [nr.b], [nr.b], scale=-0.5)
                for eb in range(2):
                    tt_ = t32.next()
                    self.stt(tt_.h[:], po[eb].h[:, :], self.sm.h[:, 19 + eb:20 + eb], nr.h[:], ALU.mult, ALU.mult,
                             [po[eb].b, nr.b, self.sm.b], [tt_.b])
                    ob = ost.next()
                    self.tt("dve", ob.h[:], tt_.h[:], og.h[:, eb, qd * 512:(qd + 1) * 512], ALU.mult, [tt_.b, og.b], [ob.b])
                    self.store(Sc["OBT"][2 * hd + eb, :, s * L + qd * 512:s * L + (qd + 1) * 512], ob.h[:], [ob.b])
        self.mem.reset()

    def phaseD1(self):
        nc, I, Sc, L, T = self.nc, self.I, self.Sc, self.L, self.T
        wba = self.tile([128, 8, D], BF16, "wba")
        wbb = self.tile([128, 8, D], BF16, "wbb")
        wout = self.tile([128, 8, D], BF16, "wout")
        self.wload(wba, I["w_ba"].rearrange("(kc p) c -> p kc c", p=128), 8, 0, D)
        self.wload(wbb, I["w_bb"].rearrange("(kc p) c -> p kc c", p=128), 8, 0, D)
        self.wload(wout, I["w_out"].rearrange("(kc p) c -> p kc c", p=128), 8, 0, D)
        self.mk_eps()
        gbc = self.tiles(1, [128, D], F32, "gbc")
        ins = [self.tiles(1, [128, 8, 512], BF16, nm) for nm in ("oaT", "obT", "gaT", "gbT")]
        xrot = self.tiles(4, [128, 1024], F32, "x")
        m32 = self.tiles(4, [128, 512], F32, "m32")
        mT = self.tiles(1, [128, 8, 512], BF16, "mT")
        x1s = self.tiles(4, [128, 1024], F32, "x1")
        t32 = self.tiles(2, [128, 512], F32, "t32")
        xnrot = self.tiles(4, [128, 1024], F32, "xn")
        strot = self.tiles(4, [128, 4], F32, "st")
        scr = self.tile([128, 1024], BF16, "sqscr")
        h2 = self.tiles(1, [128, 8, 512], BF16, "h2T")
        NG = L // 512
        groups = [(s, q) for s in range(3) for q in range(NG)]

        def loads_in(g):
            s, q = g
            r = slice(s * L + q * 512, s * L + (q + 1) * 512)
            out = []
            for rot, nm in zip(ins, ("OAT", "OBT", "GAT", "GBT")):
                t = rot.next()
                self.load(t.h[:], Sc[nm][:, :, r].rearrange("j p t -> p j t"), [t.b])
                out.append(t)
            return out

        def loads_x(g):
            s, q = g
            xs = []
            for i in range(4):
                xt = xrot.next()
                self.load(xt.h[:], I["xf"][s * L + q * 512 + i * 128:s * L + q * 512 + (i + 1) * 128, :], [xt.b])
                xs.append(xt)
            return xs

        nxt_in = loads_in(groups[0])
        nxt_x = loads_x(groups[0])
        gcur = None
        for gi, g in enumerate(groups):
            s, q = g
            (oaT, obT, gaT, gbT), xs = nxt_in, nxt_x
            if q == 0:
                gcur = gbc.next()
                self.load(gcur.h[:], Sc["GBC"][s, :, :], [gcur.b])
            m = mT.next()
            for cb in range(8):
                pa = self.prot.next()
                for kc in range(8):
                    self.mm(pa.h[:, :], wba.h[:, kc, cb * 128:(cb + 1) * 128], oaT.h[:, kc, :], kc == 0, kc == 7, [wba.b, oaT.b], [pa.b])
                pb = self.prot.next()
                for kc in range(8):
                    self.mm(pb.h[:, :], wbb.h[:, kc, cb * 128:(cb + 1) * 128], obT.h[:, kc, :], kc == 0, kc == 7, [wbb.b, obT.b], [pb.b])
                m1, m2 = m32.next(), m32.next()
                self.tt("dve", m1.h[:], pa.h[:, :], gaT.h[:, cb, :], ALU.mult, [pa.b, gaT.b], [m1.b])
                self.tt("dve", m2.h[:], pb.h[:, :], gbT.h[:, cb, :], ALU.mult, [pb.b, gbT.b], [m2.b])
                w, pw = ([m.b], ()) if cb == 0 else ((), [m.b])
                self.tt("pool", m.h[:, cb, :], m1.h[:], m2.h[:], ALU.add, [m1.b, m2.b], w, pw=pw)
            if gi + 1 < len(groups):
                nxt_in = loads_in(groups[gi + 1])
            x1l = []
            for i in range(4):
                x1 = x1s.next()
                for half in range(2):
                    hs = slice(half * 512, (half + 1) * 512)
                    pz = self.prot.next()
                    for kc in range(8):
                        self.mm(pz.h[:, :], m.h[:, kc, i * 128:(i + 1) * 128], wout.h[:, kc, hs], kc == 0, kc == 7, [m.b, wout.b], [pz.b])
                    t_ = t32.next()
                    self.tt("dve", t_.h[:], pz.h[:, :], gcur.h[:, hs], ALU.mult, [pz.b, gcur.b], [t_.b])
                    w, pw = ([x1.b], ()) if half == 0 else ((), [x1.b])
                    self.tt("pool", x1.h[:, hs], t_.h[:], xs[i].h[:, hs], ALU.add, [t_.b, xs[i].b], w, pw=pw)
                r0 = s * L + q * 512 + i * 128
                self.store(Sc["X1"][r0:r0 + 128, :], x1.h[:], [x1.b])
                x1l.append(x1)
            if gi + 1 < len(groups):
                nxt_x = loads_x(groups[gi + 1])
            hT = h2.next()
            self.norm_group(x1l, s, 2, hT, scr, xnrot, strot)
            self.store(Sc["H2T"][:, :, s * L + q * 512:s * L + (q + 1) * 512].rearrange("j p t -> p j t"), hT.h[:], [hT.b])
        self.mem.reset()

    def phaseD2(self):
        nc, I, Sc, L, T = self.nc, self.I, self.Sc, self.L, self.T
        wups = [self.tile([128, 8, 1024], BF16, "wup") for _ in range(4)]
        wdn = self.tile([128, 32, D], BF16, "wdn")
        for qi in range(4):
            self.wload(wups[qi], I["w_up"].rearrange("(kc p) c -> p kc c", p=128), 8, qi * 1024, (qi + 1) * 1024, 0)
        self.wload(wdn, I["w_dn"].rearrange("(kc p) c -> p kc c", p=128), 32, 0, D)
        gbc = self.tiles(1, [128, D], F32, "gbc2")
        h2 = self.tiles(2, [128, 8, 256], BF16, "h2")
        uT = self.tile([128, 32, 256], BF16, "uT")
        r32 = self.tiles(2, [128, 256], F32, "r32")
        x1s = self.tiles(4, [128, 1024], F32, "x1b")
        t32 = self.tiles(1, [128, 512], F32, "t32b")
        NG = L // 256
        groups = [(s, q) for s in range(3) for q in range(NG)]

        def loads(g):
            s, q = g
            r0 = s * L + q * 256
            h = h2.next()
            self.load(h.h[:], Sc["H2T"][:, :, r0:r0 + 256].rearrange("j p t -> p j t"), [h.b])
            xs = []
            for i in range(2):
                xt = x1s.next()
                self.load(xt.h[:], Sc["X1"][r0 + i * 128:r0 + (i + 1) * 128, :], [xt.b])
                xs.append(xt)
            return h, xs

        nxt = loads(groups[0])
        gcur = None
        for gi, g in enumerate(groups):
            s, q = g
            h, xs = nxt
            if q == 0:
                gcur = gbc.next()
                self.load(gcur.h[:], Sc["GBC"][3 + s, :, :], [gcur.b])
            if gi + 1 < len(groups):
                nxt = loads(groups[gi + 1])
            for fc in range(32):
                pu = self.prot.next()
                for kc in range(8):
                    wq = wups[fc // 8]
                    self.mm(pu.h[:, 0:256], wq.h[:, kc, (fc % 8) * 128:(fc % 8 + 1) * 128], h.h[:, kc, :], kc == 0, kc == 7, [wq.b, h.b], [pu.b])
                r = r32.next()
                self.act(r.h[:], pu.h[:, 0:256], AF.Relu, [pu.b], [r.b])
                w, pw = ([uT.b], ()) if fc == 0 else ((), [uT.b])
                self.tt("dve", uT.h[:, fc, :], r.h[:], r.h[:], ALU.mult, [r.b], w, pw=pw)
            for i in range(2):
                y = xs[i]
                for half in range(2):
                    hs = slice(half * 512, (half + 1) * 512)
                    pz = self.prot.next()
                    for fc in range(32):
                        self.mm(pz.h[:, :], uT.h[:, fc, i * 128:(i + 1) * 128], wdn.h[:, fc, hs], fc == 0, fc == 31, [uT.b, wdn.b], [pz.b])
                    t_ = t32.next()
                    self.tt("dve", t_.h[:], pz.h[:, :], gcur.h[:, hs], ALU.mult, [pz.b, gcur.b], [t_.b])
                    self.tt("pool", y.h[:, hs], t_.h[:], xs[i].h[:, hs], ALU.add, [t_.b, xs[i].b], [y.b])
                r0 = s * L + q * 256 + i * 128
                self.store(self.y[r0:r0 + 128, :], y.h[:], [y.b])
        self.mem.reset()


def _t5_bucket_np(rel):
    nb = 16
    max_exact = 8
    ret = (rel > 0).astype(np.int32) * nb
    n = np.abs(rel)
    nf = np.maximum(n, 1).astype(np.float32)
    large = max_exact + (np.log(nf / np.float32(max_exact)) / np.float32(math.log(128 / max_exact))
                         * np.float32(nb - max_exact)).astype(np.int32)
    large = np.minimum(large, nb - 1)
    return ret + np.where(n < max_exact, n, large)


def _consts():
    p = np.arange(128)
    ident = np.eye(128, dtype=np.float32)
    J = ident[::-1].copy()
    triF = (p[:, None] <= p[None, :]).astype(np.float32)
    triB = (p[:, None] >= p[None, :]).astype(np.float32)
    blk = ((p[:, None] // 64) == (p[None, :] // 64)).astype(np.float32)
    ones = np.ones((128, 128), np.float32)
    cst = np.concatenate([ident, J, triF, triB, blk, ones], axis=1)
    i = np.arange(1280)
    bk = _t5_bucket_np((639 - i).astype(np.int32))
    oh = (bk[None, :] == np.arange(32)[:, None]).astype(np.float32)
    return np.ascontiguousarray(cst), np.ascontiguousarray(oh)


def make_in_maps(L, inp):
    T = L // 128
    NOT = 7 * T + 1
    f = lambda a: np.ascontiguousarray(np.asarray(a, dtype=np.float32))
    xp = f(inp["x_prompt"])[0]
    xs = f(inp["x_sample"])
    cp = f(inp["c_prompt"])[0]
    cs = f(inp["c_sample"])
    cst, oh = _consts()
    fmT = lambda v: np.ascontiguousarray(v.reshape(-1, 128).T)
    shared = {
        "w_ada": f(inp["w_ada"])[0], "b_adaT": fmT(f(inp["b_ada"])[0]), "b_ada": f(inp["b_ada"])[0][None, :],
        "n1T": fmT(f(inp["norm1_g"])[0]), "n2T": fmT(f(inp["norm2_g"])[0]), "w_in": f(inp["w_in"])[0],
        "qkg": np.ascontiguousarray(np.stack([np.tile(f(inp["q_norm_g"])[0], 2), np.tile(f(inp["k_norm_g"])[0], 2)], axis=1)),
        "lamv": np.concatenate([f(inp["lam_q1"])[0], f(inp["lam_k1"])[0], f(inp["lam_q2"])[0], f(inp["lam_k2"])[0]])[None, :],
        "subg": f(inp["subln_g"])[0][:, None],
        "glng": fmT(f(inp["gla_norm_g"])[0]),
        "w_ba": f(inp["w_branch_a"])[0], "w_bb": f(inp["w_branch_b"])[0], "w_out": f(inp["w_out"])[0],
        "w_up": f(inp["w_up"])[0], "w_dn": f(inp["w_down"])[0],
        "rb": f(inp["rel_bias"]), "rbrow": f(inp["rel_bias"]).reshape(1, 256), "oh": oh, "cst": cst,
    }
    z16 = np.zeros((16, 512), np.float32)
    shared["wgf"] = np.ascontiguousarray(np.concatenate([f(inp["w_gate_f"])[0], z16, f(inp["b_gate_f"])], axis=0))
    shared["wgb"] = np.ascontiguousarray(np.concatenate([z16, f(inp["w_gate_b"])[0], f(inp["b_gate_b"])], axis=0))
    maps = []
    xpt = xp.reshape(8 * T, 128, D)
    for c in range(8):
        m = dict(shared)
        m["xf"] = np.ascontiguousarray(np.concatenate([xp[c * L:(c + 1) * L], xs[2 * c], xs[2 * c + 1]], axis=0))
        post = list(range(8 * T - 1, (c + 1) * T, -1))
        pre = list(range(0, c * T - 1))
        xo = np.zeros((NOT, 128, D), np.float32)
        flg = np.zeros((3, NOT), np.float32)
        u = 0
        for t in post:
            xo[u] = xpt[t]; flg[1, u] = 1; u += 1
        for t in pre:
            xo[u] = xpt[t]; flg[0, u] = 1; u += 1
        while u < NOT - 2:
            flg[2, u] = 1; u += 1
        if c > 0:
            xo[NOT - 2] = xpt[c * T - 1]; flg[0, NOT - 2] = 1
        else:
            flg[2, NOT - 2] = 1
        if c < 7:
            xo[NOT - 1] = xpt[(c + 1) * T]; flg[1, NOT - 1] = 1
        else:
            flg[2, NOT - 1] = 1
        m["xo"] = np.ascontiguousarray(xo.reshape(NOT * 128, D))
        m["flg"] = np.ascontiguousarray(np.broadcast_to(flg.reshape(1, 3 * NOT), (128, 3 * NOT)))
        cc = np.zeros((4, D), np.float32)
        cc[0] = cp; cc[1] = cs[2 * c]; cc[2] = cs[2 * c + 1]
        m["cT"] = np.ascontiguousarray(cc.reshape(4, 8, 128).transpose(2, 1, 0).reshape(128, 32))
        maps.append(m)
    return maps


_CACHE = {}


def run(L, inp, debug=False, nphase=99):
    key = (L, debug, nphase)
    if key not in _CACHE:
        k = K(L, debug=debug, nphase=nphase)
        nc = k.build()
        from contextlib import ExitStack
        st = ExitStack()
        k.S.emit(nc, st)
        st.close()
        _CACHE[key] = (k, nc)
    k, nc = _CACHE[key]
    maps = make_in_maps(L, inp)
    res = run_bass_kernel_spmd(nc, maps, core_ids=list(range(8)))
    return k, res


def kernel(**inputs):
    L = 2048
    k, res = run(L, inputs)
    yp = np.concatenate([res.results[c]["y"][0:L] for c in range(8)], axis=0)[None]
    ys = np.stack([res.results[c]["y"][L * (1 + j):L * (2 + j)] for c in range(8) for j in range(2)], axis=0)
    return (np.ascontiguousarray(yp, dtype=np.float32), np.ascontiguousarray(ys, dtype=np.float32))
```

```python
import math
import numpy as np
import concourse.bass as bass
import concourse.mybir as mybir
from concourse.bass_utils import run_bass_kernel_spmd

F32 = mybir.dt.float32
BF16 = mybir.dt.bfloat16
AF = mybir.ActivationFunctionType
ALU = mybir.AluOpType

D = 1024
NH = 8
HB = 4
DIN = 8224
DFF = 4096
EPS = 1e-6
C_QA, C_KA, C_VA, C_QG, C_KG, C_VG, C_OG, C_LR, C_GA, C_GB = 0, 1024, 2048, 3072, 3584, 4096, 5120, 6144, 6176, 7200
NEG = -30000.0


class Buf:
    __slots__ = ("name", "w", "r", "pw", "pr", "excl")

    def __init__(self, name, excl=False):
        self.name = name
        self.excl = excl
        self.w = []
        self.r = []
        self.pw = []
        self.pr = []


class Op:
    __slots__ = ("eng", "fn", "deps", "needed", "val", "dma", "dsem", "idx")

    def __init__(self, eng, fn, dma):
        self.eng = eng
        self.fn = fn
        self.deps = []
        self.needed = False
        self.val = 0
        self.dma = dma
        self.dsem = -1
        self.idx = 0


ENGS = ("pe", "act", "dve", "pool", "sp")
NDSEM = 6
import os
STORE_Q = os.environ.get('STORE_Q', 'pool')


class Sched:
    def __init__(self):
        self.ops = {e: [] for e in ENGS}
        self.dma_ops = []
        self.out_dmas = []
        self.last_barrier_idx = {e: 0 for e in ENGS}

    def op(self, eng, fn, r=(), w=(), dma=False, pw=()):
        o = Op(eng, fn, dma)
        deps = []
        for b in r:
            deps.extend(b.w)
            if b.excl:
                deps.extend(d for d in b.r if d.eng != eng)
        for b in w:
            deps.extend(b.w)
            deps.extend(b.r)
        for b in pw:
            deps.extend(b.pw)
            deps.extend(b.pr)
        seen = set()
        for d in deps:
            if d is o or id(d) in seen:
                continue
            seen.add(id(d))
            if d.eng == "pe" and eng == "pe" and not d.dma:
                continue
            d.needed = True
            o.deps.append(d)
        for b in r:
            b.r.append(o)
        for b in w:
            b.pw = b.w
            b.pr = b.r
            b.w = [o]
            b.r = []
        for b in pw:
            b.w.append(o)
        o.idx = len(self.ops[eng])
        self.ops[eng].append(o)
        if dma:
            self.dma_ops.append(o)
        return o

    def barrier(self):
        lasts = []
        for e in ENGS:
            for o in reversed(self.ops[e]):
                if not o.dma:
                    lasts.append(o)
                    break
        dmas = list(self.dma_ops)
        self.dma_ops = []
        for e in ENGS:
            o = Op(e, None, False)
            for d in lasts + dmas:
                if d.eng == e and not d.dma:
                    continue
                d.needed = True
                o.deps.append(d)
            o.idx = len(self.ops[e])
            self.ops[e].append(o)

    def emit(self, nc, stack):
        sems = {e: stack.enter_context(nc.semaphore("sem_" + e)) for e in ENGS}
        dsems = {}
        for e in ("sp", "pool", "act"):
            dsems[e] = [stack.enter_context(nc.semaphore(f"dsem_{e}_{i}")) for i in range(NDSEM)]
        for e in ENGS:
            cnt = 0
            dcnt = [0] * NDSEM
            k = 0
            for o in self.ops[e]:
                if o.dma:
                    o.dsem = k % NDSEM
                    dcnt[o.dsem] += 16
                    o.val = dcnt[o.dsem]
                    k += 1
                elif o.fn is not None and o.needed:
                    cnt += 1
                    o.val = cnt
        engobj = {"pe": "tensor", "act": "scalar", "dve": "vector", "pool": "gpsimd", "sp": "sync"}
        block = stack.enter_context(nc.Block())

        def run(ename):
            def body(e):
                waited = {}
                for o in self.ops[ename]:
                    if o.dma:
                        if o.val > 16:
                            key = ("d", ename, o.dsem)
                            if waited.get(key, 0) < o.val - 16:
                                e.wait_ge(dsems[ename][o.dsem], o.val - 16)
                                waited[key] = o.val - 16
                    for d in o.deps:
                        if d.dma:
                            key = ("d", d.eng, d.dsem)
                            s = dsems[d.eng][d.dsem]
                        else:
                            key = ("c", d.eng)
                            s = sems[d.eng]
                        if waited.get(key, 0) < d.val:
                            e.wait_ge(s, d.val)
                            waited[key] = d.val
                    if o.fn is None:
                        continue
                    ins = o.fn(e)
                    if o.dma:
                        ins.then_inc(dsems[ename][o.dsem], 16)
                    elif o.needed:
                        ins.then_inc(sems[ename], 1)
            return body

        block.sync(run("sp"))
        block.gpsimd(run("pool"))
        block.scalar(run("act"))
        block.vector(run("dve"))
        block.tensor(run("pe"))


class Mem:
    def __init__(self, nc, lo=16384, hi=212000):
        self.nc = nc
        self.lo = lo
        self.hi = hi
        self.cur = lo
        self.n = 0
        self.pers = lo

    def alloc(self, shape, dt, nbuf=1, name=None):
        nb = int(np.prod(shape[1:])) * (4 if dt == F32 else 2)
        nb = (nb + 63) // 64 * 64
        self.n += 1
        t = self.nc.alloc_sbuf_tensor_at(f"{name or 't'}_{self.n}", list(shape), dt, offset=self.cur)
        self.cur += nb
        assert self.cur <= self.hi, f"SBUF overflow {self.cur} {name}"
        return t

    def persist(self):
        self.pers = self.cur

    def reset(self):
        self.cur = self.pers


class Tl:
    __slots__ = ("h", "b")

    def __init__(self, h, name, excl=False):
        self.h = h
        self.b = Buf(name, excl)


class Rot:
    def __init__(self, tiles):
        self.t = tiles
        self.i = 0

    def next(self):
        t = self.t[self.i % len(self.t)]
        self.i += 1
        return t


class K:
    def __init__(self, L, debug=False, nphase=99):
        self.nphase = nphase
        assert L % 512 == 0
        self.L = L
        self.T = L // 128
        self.NOT = 7 * self.T + 1
        self.NO = self.NOT * 128
        self.NF = 3 * L
        self.NTK = self.NF + self.NO
        self.debug = debug
        self.nc = bass.Bass("TRN2", target_bir_lowering=False)
        self.S = Sched()
        self.mem = Mem(self.nc)
        self.ntl = 0

    def tile(self, shape, dt, name="t"):
        self.ntl += 1
        return Tl(self.mem.alloc(shape, dt, name=name), f"{name}{self.ntl}")

    def tiles(self, n, shape, dt, name="t"):
        return Rot([self.tile(shape, dt, name) for _ in range(n)])

    def dram_in(self, name, shape, dt=F32):
        return self.nc.dram_tensor(name, list(shape), dt, kind="ExternalInput").ap()

    def dram_scr(self, name, shape, dt):
        kind = "ExternalOutput" if self.debug else "Internal"
        if self.debug:
            self.dbg_names.append(name)
        return self.nc.dram_tensor(name, list(shape), dt, kind=kind).ap()

    def mm(self, out, lhsT, rhs, start, stop, r, w, pw=()):
        self.S.op("pe", lambda e: e.matmul(out, lhsT=lhsT, rhs=rhs, start=start, stop=stop), r, w, pw=pw)

    def tr(self, out, in_, ident, r, w, pw=()):
        self.S.op("pe", lambda e: e.transpose(out, in_, ident), r, w, pw=pw)

    def act(self, out, in_, func, r, w, scale=1.0, bias=0.0, accum=None, pw=()):
        if accum is None:
            self.S.op("act", lambda e: e.activation(out=out, in_=in_, func=func, scale=scale, bias=bias), r, w, pw=pw)
        else:
            self.S.op("act", lambda e: e.activation(out=out, in_=in_, func=func, scale=scale, bias=bias,
                                                    accum_out=accum), r, w, pw=pw)

    def ts(self, eng, out, in0, s1, op0, r, w, s2=None, op1=None, pw=()):
        if s2 is None:
            self.S.op(eng, lambda e: e.tensor_scalar(out=out, in0=in0, scalar1=s1, scalar2=None, op0=op0), r, w, pw=pw)
        else:
            self.S.op(eng, lambda e: e.tensor_scalar(out=out, in0=in0, scalar1=s1, scalar2=s2, op0=op0, op1=op1), r, w, pw=pw)

    def tt(self, eng, out, in0, in1, op, r, w, pw=()):
        self.S.op(eng, lambda e: e.tensor_tensor(out=out, in0=in0, in1=in1, op=op), r, w, pw=pw)

    def stt(self, out, in0, scalar, in1, op0, op1, r, w, pw=()):
        self.S.op("dve", lambda e: e.scalar_tensor_tensor(out=out, in0=in0, scalar=scalar, in1=in1, op0=op0, op1=op1), r, w, pw=pw)

    def cp(self, eng, out, in_, r, w, pw=()):
        if eng == "act":
            self.S.op("act", lambda e: e.copy(out=out, in_=in_), r, w, pw=pw)
        else:
            self.S.op(eng, lambda e: e.tensor_copy(out=out, in_=in_), r, w, pw=pw)

    def dma(self, q, out, in_, r, w, pw=()):
        return self.S.op(q, lambda e: e.dma_start(out=out, in_=in_), r, w, dma=True, pw=pw)

    def load(self, out, in_, w, r=(), pw=()):
        return self.dma("sp", out, in_, r, w, pw=pw)

    def store(self, out, in_, r, w=()):
        return self.dma(STORE_Q, out, in_, r, w)

    def memset(self, eng, ap, val, w):
        self.S.op(eng, lambda e: e.memset(ap, val), (), w)

    def hilo(self, src, srcb, hi, hib, lo, lob, eng="dve"):
        self.cp(eng, hi, src, [srcb], [hib])
        self.tt("dve", lo, src, hi, ALU.subtract, [srcb, hib], [lob])

    def wload(self, dst, src_v, nk, c0, c1, l0=None):
        if l0 is None:
            l0 = c0
        first = not dst.b.w
        for kc in range(nk):
            for a in range(c0, c1, 2048):
                b = min(c1, a + 2048)
                la = l0 + (a - c0)
                if first:
                    self.dma("pool", dst.h[:, kc, la:la + (b - a)], src_v[:, kc, a:b], (), [dst.b])
                    first = False
                else:
                    self.dma("pool", dst.h[:, kc, la:la + (b - a)], src_v[:, kc, a:b], (), (), pw=[dst.b])

    def build(self):
        nc, L, T, NOT, NO, NF, NTK = self.nc, self.L, self.T, self.NOT, self.NO, self.NF, self.NTK
        self.dbg_names = []
        I = {}
        I["xf"] = self.dram_in("xf", [NF, D])
        I["xo"] = self.dram_in("xo", [NO, D])
        I["cT"] = self.dram_in("cT", [128, 8 * 4])
        I["flg"] = self.dram_in("flg", [128, 3 * NOT])
        I["w_ada"] = self.dram_in("w_ada", [D, 6 * D])
        I["b_adaT"] = self.dram_in("b_adaT", [128, 48])
        I["b_ada"] = self.dram_in("b_ada", [1, 6 * D])
        I["n1T"] = self.dram_in("n1T", [128, 8])
        I["n2T"] = self.dram_in("n2T", [128, 8])
        I["w_in"] = self.dram_in("w_in", [D, DIN])
        I["qkg"] = self.dram_in("qkg", [128, 2])
        I["lamv"] = self.dram_in("lamv", [1, 256])
        I["subg"] = self.dram_in("subg", [128, 1])
        I["wgf"] = self.dram_in("wgf", [33, 512])
        I["wgb"] = self.dram_in("wgb", [33, 512])
        I["glng"] = self.dram_in("glng", [128, 2])
        I["w_ba"] = self.dram_in("w_ba", [D, D])
        I["w_bb"] = self.dram_in("w_bb", [D, D])
        I["w_out"] = self.dram_in("w_out", [D, D])
        I["w_up"] = self.dram_in("w_up", [D, DFF])
        I["w_dn"] = self.dram_in("w_dn", [DFF, D])
        I["rb"] = self.dram_in("rb", [32, 8])
        I["rbrow"] = self.dram_in("rbrow", [1, 256])
        I["oh"] = self.dram_in("oh", [32, 1280])
        I["cst"] = self.dram_in("cst", [128, 6 * 128])
        self.I = I
        self.y = nc.dram_tensor("y", [NF, D], F32, kind="ExternalOutput").ap()

        Sc = {}
        Sc["KT"] = self.dram_scr("KT", [8, 128, NTK], BF16)
        Sc["VH"] = self.dram_scr("VH", [8, 128, NTK // 128, 128], BF16)
        Sc["QT"] = self.dram_scr("QT", [8, 128, NF], BF16)
        Sc["GQT"] = self.dram_scr("GQT", [4, 128, NF], BF16)
        Sc["GKT"] = self.dram_scr("GKT", [4, 128, NF], BF16)
        Sc["GK"] = self.dram_scr("GK", [NF, 512], BF16)
        Sc["GV"] = self.dram_scr("GV", [NF, 1024], BF16)
        for nm in ("GGFH", "GGFL", "GGBH", "GGBL"):
            Sc[nm] = self.dram_scr(nm, [NF, 512], BF16)
        Sc["OGT"] = self.dram_scr("OGT", [8, 128, NF], BF16)
        Sc["GAT"] = self.dram_scr("GAT", [8, 128, NF], BF16)
        Sc["GBT"] = self.dram_scr("GBT", [8, 128, NF], BF16)
        Sc["OAT"] = self.dram_scr("OAT", [8, 128, NF], BF16)
        Sc["OBT"] = self.dram_scr("OBT", [8, 128, NF], BF16)
        Sc["X1"] = self.dram_scr("X1", [NF, D], F32)
        Sc["H2T"] = self.dram_scr("H2T", [8, 128, NF], BF16)
        Sc["GBC"] = self.dram_scr("GBC", [6, 128, D], F32)
        Sc["HT"] = self.dram_scr("HT", [8, 128, NTK], BF16)
        self.frev_t = nc.dram_tensor("FREV", [8, 1280], F32, kind="ExternalOutput" if self.debug else "Internal")
        if self.debug:
            self.dbg_names.append("FREV")
        Sc["SIN"] = self.dram_scr("SIN", [8, 128, 256], F32)
        self.Sc = Sc

        self.pspair = [Tl(nc.alloc_psum_tensor(f"pp{i}", [128, 1024], F32), f"pp{i}", excl=True) for i in range(4)]
        self.psum = [Tl(self.pspair[i // 2].h[:, (i % 2) * 512:(i % 2 + 1) * 512], f"ps{i}", excl=True) for i in range(8)]
        self.prot = Rot(self.psum)

        phases = [self.phase0, lambda: self.phase1(0), lambda: self.phase1(1), lambda: self.phase1(2),
                  self.phaseB, self.phaseC, self.phaseD1, self.phaseD2]
        for ph in phases[:self.nphase]:
            ph()
            self.S.barrier()
        return nc

    def phase0(self):
        nc, I, Sc, NOT = self.nc, self.I, self.Sc, self.NOT
        P = self
        cst = self.tile([128, 6 * 128], F32, "cst")
        self.load(cst.h[:], I["cst"][:, :], [cst.b])
        self.cst = cst
        self.ident = cst.h[:, 0:128]
        self.J32 = cst.h[:, 128:256]
        self.triF = cst.h[:, 256:384]
        self.triB = cst.h[:, 384:512]
        self.blk64 = cst.h[:, 512:640]
        self.ones32 = cst.h[:, 640:768]
        cstb = self.tile([128, 6 * 128], BF16, "cstb")
        self.cstb = cstb
        self.cp("dve", cstb.h[:], cst.h[:], [cst.b], [cstb.b])
        self.identb = cstb.h[:, 0:128]
        self.Jb = cstb.h[:, 128:256]
        self.triFb = cstb.h[:, 256:384]
        self.triBb = cstb.h[:, 384:512]
        self.blk64b = cstb.h[:, 512:640]
        self.onesb = cstb.h[:, 640:768]
        sm = self.tile([128, 64], F32, "sm")
        self.sm = sm
        self.load(sm.h[:, 0:8], I["n1T"][:, :], [sm.b])
        self.load(sm.h[:, 8:16], I["n2T"][:, :], [sm.b])
        self.load(sm.h[:, 16:18], I["qkg"][:, :], [sm.b])
        self.load(sm.h[:, 18:19], I["subg"][:, :], [sm.b])
        self.load(sm.h[:, 19:21], I["glng"][:, :], [sm.b])
        self.ts("dve", sm.h[:, 21:22], sm.h[:, 18:19], 0.8, ALU.mult, [sm.b], [sm.b])
        self.memset("dve", sm.h[:, 22:23], 0.0, [sm.b])
        amod = self.tile([128, 4 * 32], F32, "amod")
        self.amod = amod
        self.pers_late = self.mem.cur
        rbb = self.tile([128, 256], F32, "rbb")
        self.rbb = rbb
        self.load(rbb.h[:], I["rbrow"][0:1, :].partition_broadcast(128), [rbb.b])
        flg = self.tile([128, 3 * NOT], F32, "flg")
        self.flg = flg
        self.load(flg.h[:], I["flg"][:, :], [flg.b])
        nfl = self.tile([128, 2 * NOT], F32, "nfl")
        self.nfl = nfl
        self.ts("dve", nfl.h[:], flg.h[:, 0:2 * NOT], -1.0 / 16.0, ALU.mult, [flg.b], [nfl.b])
        fb = self.tile([128, 8 * NOT], F32, "fb")
        self.fb = fb
        for h in range(8):
            o = fb.h[:, h * NOT:(h + 1) * NOT]
            self.ts("dve", o, flg.h[:, 0:NOT], rbb.h[:, 15 * 8 + h:15 * 8 + h + 1], ALU.mult, [flg.b, rbb.b], [fb.b])
            self.stt(o, flg.h[:, NOT:2 * NOT], rbb.h[:, 31 * 8 + h:31 * 8 + h + 1], o, ALU.mult, ALU.add, [flg.b, rbb.b, fb.b], [fb.b])
            self.stt(o, flg.h[:, 2 * NOT:3 * NOT], NEG, o, ALU.mult, ALU.add, [flg.b, fb.b], [fb.b])
        wgh = self.tile([33, 1024], BF16, "wgh")
        wgl = self.tile([33, 1024], BF16, "wgl")
        self.wgh, self.wgl = wgh, wgl
        self.mem.persist()
        modT = self.tile([128, 6 * 8 * 4], F32, "modT")
        self.modT = modT
        lamt = self.tile([128, 256 + 8], F32, "lamt")
        self.load(lamt.h[:, 0:256], I["lamv"][0:1, :].partition_broadcast(128), [lamt.b])
        self.tt("dve", lamt.h[:, 0:64], lamt.h[:, 0:64], lamt.h[:, 64:128], ALU.mult, [lamt.b], [lamt.b])
        self.tt("dve", lamt.h[:, 128:192], lamt.h[:, 128:192], lamt.h[:, 192:256], ALU.mult, [lamt.b], [lamt.b])
        self.S.op("dve", lambda e: e.reduce_sum(out=lamt.h[:, 256:257], in_=lamt.h[:, 0:64], axis=mybir.AxisListType.X), [lamt.b], [lamt.b])
        self.S.op("dve", lambda e: e.reduce_sum(out=lamt.h[:, 257:258], in_=lamt.h[:, 128:192], axis=mybir.AxisListType.X), [lamt.b], [lamt.b])
        self.act(lamt.h[:, 258:260], lamt.h[:, 256:258], AF.Exp, [lamt.b], [lamt.b])
        self.tt("dve", lamt.h[:, 260:261], lamt.h[:, 259:260], lamt.h[:, 258:259], ALU.subtract, [lamt.b], [lamt.b])
        self.ts("dve", sm.h[:, 23:24], lamt.h[:, 260:261], -0.2, ALU.add, [lamt.b], [sm.b])
        self.neglam = sm.h[:, 23:24]

        wg = self.tile([33, 1024], F32, "wg")
        self.load(wg.h[:, 0:512], I["wgf"][:, :], [wg.b])
        self.load(wg.h[:, 512:1024], I["wgb"][:, :], [wg.b])
        self.hilo(wg.h[:], wg.b, wgh.h[:], wgh.b, wgl.h[:], wgl.b)
        cT = self.tile([128, 32], F32, "cT")
        self.load(cT.h[:], I["cT"][:, :], [cT.b])
        scT = self.tile([128, 32], F32, "scT")
        self.act(scT.h[:], cT.h[:], AF.Silu, [cT.b], [scT.b])
        badT = self.tile([128, 48], F32, "badT")
        self.load(badT.h[:], I["b_adaT"][:, :], [badT.b])
        screp = self.tile([128, 3 * 8 * 128], F32, "screp")
        for s in range(3):
            for kc in range(8):
                self.cp("dve", screp.h[:, (s * 8 + kc) * 128:(s * 8 + kc + 1) * 128],
                        scT.h[:, kc * 4 + s:kc * 4 + s + 1].to_broadcast([128, 128]), [scT.b], [screp.b])
        wa = self.tiles(2, [128, 8, 1024], F32, "wa")
        w_ada_v = I["w_ada"].rearrange("(kc p) c -> p kc c", p=128)
        bbc = self.tile([128, 1024], F32, "bbc")
        gst = self.tiles(2, [128, 1024], F32, "gst")
        for blk in range(6):
            w = wa.next()
            for kc in range(8):
                self.load(w.h[:, kc, :], w_ada_v[:, kc, blk * 1024:(blk + 1) * 1024], [w.b])
            if blk in (2, 5):
                self.load(bbc.h[:], I["b_ada"][0:1, blk * 1024:(blk + 1) * 1024].partition_broadcast(128), [bbc.b])
                for s in range(3):
                    g = gst.next()
                    for half in range(2):
                        ps = self.prot.next()
                        for kc in range(8):
                            self.mm(ps.h[:, :], screp.h[:, (s * 8 + kc) * 128:(s * 8 + kc + 1) * 128],
                                    w.h[:, kc, half * 512:(half + 1) * 512], kc == 0, kc == 7, [screp.b, w.b], [ps.b])
                        self.tt("dve", g.h[:, half * 512:(half + 1) * 512], ps.h[:, :], bbc.h[:, half * 512:(half + 1) * 512],
                                ALU.add, [ps.b, bbc.b], [g.b])
                    self.store(Sc["GBC"][(0 if blk == 2 else 3) + s, :, :], g.h[:], [g.b])
            else:
                for j in range(8):
                    ps = self.prot.next()
                    for kc in range(8):
                        self.mm(ps.h[:, 0:4], w.h[:, kc, j * 128:(j + 1) * 128], scT.h[:, kc * 4:kc * 4 + 4],
                                kc == 0, kc == 7, [w.b, scT.b], [ps.b])
                    self.ts("dve", modT.h[:, (blk * 8 + j) * 4:(blk * 8 + j) * 4 + 4], ps.h[:, 0:4],
                            badT.h[:, blk * 8 + j:blk * 8 + j + 1], ALU.add, [ps.b, badT.b], [modT.b])
        for j in range(8):
            for (dst, nrm, sblk, hblk) in ((0, 0, 1, 0), (2, 8, 4, 3)):
                self.ts("dve", amod.h[:, (dst * 8 + j) * 4:(dst * 8 + j) * 4 + 4], modT.h[:, (sblk * 8 + j) * 4:(sblk * 8 + j) * 4 + 4],
                        1.0, ALU.add, [modT.b], [amod.b], s2=self.sm.h[:, nrm + j:nrm + j + 1], op1=ALU.mult)
                self.cp("dve", amod.h[:, ((dst + 1) * 8 + j) * 4:((dst + 1) * 8 + j) * 4 + 4],
                        modT.h[:, (hblk * 8 + j) * 4:(hblk * 8 + j) * 4 + 4], [modT.b], [amod.b])
        rb = self.tile([32, 8], F32, "rb")
        self.load(rb.h[:], I["rb"][:, :], [rb.b])
        oh = self.tile([32, 1280], F32, "oh")
        self.load(oh.h[:], I["oh"][:, :], [oh.b])
        fr = self.tile([8, 1280], F32, "fr")
        for c0 in (0, 512, 1024):
            n = min(512, 1280 - c0)
            ps = self.prot.next()
            self.mm(ps.h[0:8, 0:n], rb.h[:, :], oh.h[:, c0:c0 + n], True, True, [rb.b, oh.b], [ps.b])
            self.cp("dve", fr.h[:, c0:c0 + n], ps.h[0:8, 0:n], [ps.b], [fr.b])
        self.store(self.frev_t.ap()[:, :], fr.h[:], [fr.b])
        self.mem.reset()

    def amod_ap(self, which, j, s):
        c = (which * 8 + j) * 4 + s
        return self.amod.h[:, c:c + 1]

    def norm_group(self, xts, seq, which, hT, scr, xn_rot, stat_rot):
        n = len(xts)
        xns = []
        for i, xt in enumerate(xts):
            st = stat_rot.next()
            self.act(scr.h[:], xt.h[:], AF.Square, [xt.b], [scr.b, st.b], accum=st.h[:, 0:1])
            self.act(st.h[:, 1:2], st.h[:, 0:1], AF.Ln, [st.b, self.epsT.b], [st.b], scale=1.0 / D, bias=self.eps_ap)
            self.act(st.h[:, 2:3], st.h[:, 1:2], AF.Exp, [st.b], [st.b], scale=-0.5)
            xn = xn_rot.next()
            self.ts("dve", xn.h[:], xt.h[:], st.h[:, 2:3], ALU.mult, [xt.b, st.b], [xn.b])
            xns.append(xn)
        for j in range(8):
            ps = self.prot.next()
            for i, xn in enumerate(xns):
                self.tr(ps.h[:, i * 128:(i + 1) * 128], xn.h[:, j * 128:(j + 1) * 128], self.ident, [xn.b, self.cst.b], [ps.b])
            w, pw = ([hT.b], ()) if j == 0 else ((), [hT.b])
            if j % 2 == 0:
                self.ts("dve", hT.h[:, j, 0:n * 128], ps.h[:, 0:n * 128], self.amod_ap(which, j, seq), ALU.mult,
                        [ps.b, self.amod.b], w, s2=self.amod_ap(which + 1, j, seq), op1=ALU.add, pw=pw)
            else:
                self.act(hT.h[:, j, 0:n * 128], ps.h[:, 0:n * 128], AF.Identity, [ps.b, self.amod.b], w,
                         scale=self.amod_ap(which, j, seq), bias=self.amod_ap(which + 1, j, seq), pw=pw)

    def mk_eps(self):
        epsT = self.tile([128, 1], F32, "eps")
        self.memset("dve", epsT.h[:], EPS, [epsT.b])
        self.eps_ap = epsT.h[:, 0:1]
        self.epsT = epsT

    def phase1(self, pss):
        nc, I, Sc, L, T, NOT = self.nc, self.I, self.Sc, self.L, self.T, self.NOT
        w_in_v = I["w_in"].rearrange("(kc p) c -> p kc c", p=128)
        if pss == 0:
            rr = [(C_KA, 1024), (C_QA, 1024), (C_VA, 1024)]
        elif pss == 1:
            rr = [(C_LR, 32), (C_KG, 512), (C_VG, 1024), (C_QG, 512)]
        else:
            rr = [(C_OG, 1024), (C_GA, 1024), (C_GB, 1024)]
        ranges = []
        for (a, w) in rr:
            wt = self.tile([128, 8, w], BF16, "win")
            self.wload(wt, w_in_v, 8, a, a + w, 0)
            ranges.append((a, a + w, wt))
        ncol = sum(b - a for a, b, _ in ranges)
        win = ranges[0][2]

        def lc(c):
            for (a, b, wt) in ranges:
                if a <= c < b:
                    return wt, c - a
            raise AssertionError(c)

        if os.environ.get("P1_WLOAD_ONLY"):
            o = self.tile([128, 1024], BF16, "dbgo")
            self.cp("dve", o.h[:], win.h[:, 7, 0:1024], [win.b], [o.b])
            self.store(Sc["GV"][0:128, :], o.h[:], [o.b])
            self.mem.reset()
            return
        self.mk_eps()
        if pss == 0:
            xrot = self.tiles(8, [128, 1024], F32, "x")
            xnrot = self.tiles(4, [128, 1024], F32, "xn")
            strot = self.tiles(4, [128, 4], F32, "st")
            scr = self.tile([128, 1024], BF16, "sqscr")
        hTs = self.tiles(2 if pss == 0 else 3, [128, 8, 512], BF16, "hT")
        f32s = self.tiles(4, [128, 512], F32, "f32s")
        bfs = self.tiles(4, [128, 512], BF16, "bfs")
        hls = self.tiles(4, [128, 512], BF16, "hls")
        if pss == 0:
            vst = self.tiles(3, [128, 1024], BF16, "vst")
        if pss == 1:
            kt_t = [[self.tile([128, 512], BF16, "ktm") for _ in range(4)] for _ in range(2)]
            gv_t = [[self.tile([128, 1024], BF16, "gvt") for _ in range(4)] for _ in range(2)]
            g_t = [[[(self.tile([128, 512], BF16, "gh"), self.tile([128, 512], BF16, "gl")) for _ in range(2)]
                    for _ in range(4)] for _ in range(2)]
            ke_t = [[[self.tile([128, 512], BF16, "ke") for _ in range(2)] for _ in range(4)] for _ in range(2)]
            lrT = self.tile([33, 512], BF16, "lrT")
            self.memset("dve", lrT.h[:], 1.0, [lrT.b])
            gst = self.tiles(4, [128, 512], F32, "gst")
            est = self.tiles(3, [128, 512], F32, "est")
            Et = self.tiles(8, [128, 16], F32, "Et")
            tmpS = self.tiles(2, [128, 256], F32, "tmpS")
            sin = self.tile([128, 8, 256], F32, "sin")
            self.memset("dve", sin.h[:], 0.0, [sin.b])

        groups = []
        KB = [0, L + self.NO, 2 * L + self.NO]
        for s in range(3):
            for q in range(L // 512):
                groups.append(("full", s, self.I["xf"], s * L + q * 512, 4, KB[s] + q * 512, s * L + q * 512, None))
        if pss < 2:
            u = 0
            while u < NOT:
                n = min(4, NOT - u)
                groups.append(("other", 0, self.I["xo"], u * 128, n, L + u * 128, None, u))
                u += n

        def load_x(g):
            kind, s, src, r0, n, kb, qb, u0 = g
            xts = []
            for i in range(n):
                xt = xrot.next()
                self.load(xt.h[:], src[r0 + i * 128:r0 + (i + 1) * 128, :], [xt.b])
                xts.append(xt)
            return xts

        def fm_proj(c0, hT, n, m=128):
            ps = self.prot.next()
            wt, l = lc(c0)
            for kc in range(8):
                self.mm(ps.h[0:m, 0:n], wt.h[:, kc, l:l + m], hT.h[:, kc, 0:n], kc == 0, kc == 7, [wt.b, hT.b], [ps.b])
            return ps

        def tm_proj(c0, hT, i):
            ps = self.prot.next()
            wt, l = lc(c0)
            for kc in range(8):
                self.mm(ps.h[:, :], hT.h[:, kc, i * 128:(i + 1) * 128], wt.h[:, kc, l:l + 512], kc == 0, kc == 7,
                        [wt.b, hT.b], [ps.b])
            return ps

        def qk_norm(ps, n, gcol, dst):
            if os.environ.get("QKN") == "0":
                o = bfs.next()
                self.cp("dve", o.h[:, 0:n], ps.h[:, 0:n], [ps.b], [o.b])
                self.store(dst, o.h[:, 0:n], [o.b])
                return
            sh = hls.next()
            self.act(sh.h[:, 0:n], ps.h[:, 0:n], AF.Square, [ps.b], [sh.b])
            raw = f32s.next()
            self.cp("dve", raw.h[:, 0:n], ps.h[:, 0:n], [ps.b], [raw.b])
            p2 = self.prot.next()
            self.mm(p2.h[:, 0:n], self.blk64b, sh.h[:, 0:n], True, True, [self.cstb.b, sh.b], [p2.b])
            ln = f32s.next()
            self.act(ln.h[:, 0:n], p2.h[:, 0:n], AF.Ln, [p2.b, self.epsT.b], [ln.b], scale=1.0 / 64, bias=self.eps_ap)
            self.act(ln.h[:, 0:n], ln.h[:, 0:n], AF.Exp, [ln.b], [ln.b], scale=-0.5)
            o = bfs.next()
            self.stt(o.h[:, 0:n], raw.h[:, 0:n], self.sm.h[:, gcol:gcol + 1], ln.h[:, 0:n], ALU.mult, ALU.mult,
                     [raw.b, ln.b, self.sm.b], [o.b])
            self.store(dst, o.h[:, 0:n], [o.b])

        def load_h(g):
            kind, s, src, r0, n, kb, qb, u0 = g
            hT_ = hTs.next()
            self.load(hT_.h[:, :, 0:n * 128], Sc["HT"][:, :, kb:kb + n * 128].rearrange("j p t -> p j t"), [hT_.b])
            return hT_

        nxt = load_x(groups[0]) if pss == 0 else load_h(groups[0])
        for gi, g in enumerate(groups):
            kind, s, src, r0, n, kb, qb, u0 = g
            N = n * 128
            if pss == 0:
                xts = nxt
                if gi + 1 < len(groups):
                    nxt = load_x(groups[gi + 1])
                hT = hTs.next()
                self.norm_group(xts, s, 0, hT, scr, xnrot, strot)
                self.store(Sc["HT"][:, :, kb:kb + N].rearrange("j p t -> p j t"), hT.h[:, :, 0:N], [hT.b])
            else:
                hT = nxt
                if gi + 1 < len(groups):
                    nxt = load_h(groups[gi + 1])
            if pss == 0:
                blocks = [(C_KA + h * 128, 17, Sc["KT"][h, :, kb:kb + N]) for h in range(8)]
                if kind == "full":
                    blocks += [(C_QA + h * 128, 16, Sc["QT"][h, :, qb:qb + N]) for h in range(8)]
                ps_n = fm_proj(blocks[0][0], hT, N)
                for bi, (c0, gcol, dst) in enumerate(blocks):
                    ps = ps_n
                    if bi + 1 < len(blocks):
                        ps_n = fm_proj(blocks[bi + 1][0], hT, N)
                    qk_norm(ps, N, gcol, dst)
                for i in range(n):
                    v = vst.next()
                    for half in range(2):
                        ps = tm_proj(C_VA + half * 512, hT, i)
                        if half == 0:
                            self.cp("dve", v.h[:, 0:512], ps.h[:, :], [ps.b], [v.b])
                        else:
                            self.cp("act", v.h[:, 512:1024], ps.h[:, :], [ps.b], (), pw=[v.b])
                    kt = (kb + i * 128) // 128
                    self.store(Sc["VH"][:, :, kt, :].rearrange("h p d -> p h d"),
                               v.h[:].rearrange("p (h d) -> p h d", h=8), [v.b])
            elif pss == 1:
                par = gi % 2
                ps = fm_proj(C_LR, hT, N, m=32)
                self.cp("dve", lrT.h[0:32, 0:N], ps.h[0:32, 0:N], [ps.b], [lrT.b])
                ktms, gvs = [], []
                for i in range(n):
                    ps = tm_proj(C_KG, hT, i)
                    ktm = kt_t[par][i]
                    self.cp("act", ktm.h[:], ps.h[:, :], [ps.b], [ktm.b])
                    gv = gv_t[par][i]
                    for half in range(2):
                        ps = tm_proj(C_VG + half * 512, hT, i)
                        if half == 0:
                            self.cp("dve", gv.h[:, 0:512], ps.h[:, :], [ps.b], [gv.b])
                        else:
                            self.cp("act", gv.h[:, 512:1024], ps.h[:, :], [ps.b], (), pw=[gv.b])
                    ktms.append(ktm)
                    gvs.append(gv)
                    if kind == "full":
                        self.store(Sc["GK"][qb + i * 128:qb + (i + 1) * 128, :], ktm.h[:], [ktm.b])
                        self.store(Sc["GV"][qb + i * 128:qb + (i + 1) * 128, :], gv.h[:], [gv.b])
                gfbs = []
                for i in range(n):
                    gfb = []
                    for d in range(2):
                        pz = self.prot.next()
                        self.mm(pz.h[:, :], lrT.h[0:33, i * 128:(i + 1) * 128], self.wgh.h[0:33, d * 512:(d + 1) * 512], True, False,
                                [lrT.b, self.wgh.b], [pz.b])
                        self.mm(pz.h[:, :], lrT.h[0:33, i * 128:(i + 1) * 128], self.wgl.h[0:33, d * 512:(d + 1) * 512], False, True,
                                [lrT.b, self.wgl.b], [pz.b])
                        e1 = est.next()
                        self.act(e1.h[:], pz.h[:, :], AF.Exp, [pz.b], [e1.b], scale=-1.0)
                        self.act(e1.h[:], e1.h[:], AF.Ln, [e1.b], [e1.b], bias=1.0)
                        gt = gst.next()
                        if kind == "full":
                            self.ts("dve", gt.h[:], e1.h[:], -1.0 / 16.0, ALU.mult, [e1.b], [gt.b])
                        else:
                            uu = u0 + i
                            self.ts("dve", gt.h[:], e1.h[:], self.nfl.h[:, d * NOT + uu:d * NOT + uu + 1], ALU.mult,
                                    [e1.b, self.nfl.b], [gt.b])
                        gh, gl = g_t[par][i][d]
                        self.hilo(gt.h[:], gt.b, gh.h[:], gh.b, gl.h[:], gl.b)
                        if kind == "full":
                            nm = "GGF" if d == 0 else "GGB"
                            self.store(Sc[nm + "H"][qb + i * 128:qb + (i + 1) * 128, :], gh.h[:], [gh.b])
                            self.store(Sc[nm + "L"][qb + i * 128:qb + (i + 1) * 128, :], gl.h[:], [gl.b])
                        gfb.append((gh, gl))
                    gfbs.append(gfb)
                if kind == "other":
                    Es, kes_ = [], []
                    for i in range(n):
                        uu = u0 + i
                        gfb = gfbs[i]
                        E = Et.next()
                        pt = self.prot.next()
                        for d in range(2):
                            for hd in range(4):
                                col = (hd * 2 + d) * 2
                                for hl in range(2):
                                    self.mm(pt.h[:, col:col + 2], gfb[d][hl].h[:, hd * 128:(hd + 1) * 128], self.onesb[:, 0:2], hl == 0, hl == 1,
                                            [gfb[d][hl].b, self.cstb.b], [pt.b])
                        self.act(E.h[:], pt.h[:, 0:16], AF.Exp, [pt.b], [E.b])
                        kk = []
                        for d in range(2):
                            pb = self.prot.next()
                            for hl in range(2):
                                self.mm(pb.h[:, :], self.triFb if d == 0 else self.triBb, gfb[d][hl].h[:], hl == 0, hl == 1,
                                        [self.cstb.b, gfb[d][hl].b], [pb.b])
                            em = est.next()
                            self.act(em.h[:], pb.h[:, :], AF.Exp, [pb.b], [em.b], scale=-1.0)
                            ke = ke_t[par][i][d]
                            self.stt(ke.h[:], ktms[i].h[:], self.flg.h[:, d * NOT + uu:d * NOT + uu + 1], em.h[:], ALU.mult, ALU.mult,
                                     [ktms[i].b, self.flg.b, em.b], [ke.b])
                            kk.append(ke)
                        Es.append(E)
                        kes_.append(kk)
                    for i in range(n):
                        E, gv = Es[i], gvs[i]
                        for d in range(2):
                            ke = kes_[i][d]
                            for hp in range(2):
                                pu = self.prot.next()
                                for k2 in range(2):
                                    hd = hp * 2 + k2
                                    self.mm(pu.h[:, k2 * 256:(k2 + 1) * 256], ke.h[:, hd * 128:(hd + 1) * 128],
                                            gv.h[:, hd * 256:(hd + 1) * 256], True, True, [ke.b, gv.b], [pu.b])
                                for k2 in range(2):
                                    hd = hp * 2 + k2
                                    tm = tmpS.next()
                                    self.tt("dve", tm.h[:], pu.h[:, k2 * 256:(k2 + 1) * 256], sin.h[:, hd * 2 + d, :], ALU.add,
                                            [pu.b, sin.b], [tm.b])
                                    col = (hd * 2 + d) * 2
                                    self.ts("dve", sin.h[:, hd * 2 + d, :], tm.h[:], E.h[:, col:col + 1], ALU.mult,
                                            [tm.b, E.b], [sin.b])
                if kind == "full":
                    for hd in range(4):
                        ps = fm_proj(C_QG + hd * 128, hT, N)
                        o = bfs.next()
                        self.act(o.h[:, 0:N], ps.h[:, 0:N], AF.Copy, [ps.b], [o.b], scale=128 ** -0.5)
                        self.store(Sc["GQT"][hd, :, qb:qb + N], o.h[:, 0:N], [o.b])
                        ps = fm_proj(C_KG + hd * 128, hT, N)
                        o = bfs.next()
                        self.cp("dve", o.h[:, 0:N], ps.h[:, 0:N], [ps.b], [o.b])
                        self.store(Sc["GKT"][hd, :, qb:qb + N], o.h[:, 0:N], [o.b])
            else:
                for (c0, fn, nm) in ((C_OG, AF.Silu, "OGT"), (C_GA, AF.Sigmoid, "GAT"), (C_GB, AF.Sigmoid, "GBT")):
                    for j in range(8):
                        ps = fm_proj(c0 + j * 128, hT, N)
                        o = bfs.next()
                        self.act(o.h[:, 0:N], ps.h[:, 0:N], fn, [ps.b], [o.b])
                        self.store(Sc[nm][j, :, qb:qb + N], o.h[:, 0:N], [o.b])
        if pss == 1:
            self.store(Sc["SIN"].rearrange("k p e -> p k e"), sin.h[:], [sin.b])
        self.mem.reset()

    def phaseB(self):
        nc, I, Sc, L, T, NOT = self.nc, self.I, self.Sc, self.L, self.T, self.NOT
        NKT0 = T + NOT
        self.mk_eps()
        epsT = self.epsT
        KTbig = self.tile([128, NKT0 * 128], BF16, "KTbig")
        Vbig = self.tile([128, NKT0, 128], BF16, "Vbig")
        KTsm = self.tiles(2, [128, T * 128], BF16, "KTsm")
        Vsm = self.tiles(2, [128, T, 128], BF16, "Vsm")
        Q0s = self.tiles(2, [128, L], BF16, "Q0s")
        Q1s = self.tiles(2, [128, L], BF16, "Q1s")
        for q0, q1 in zip(Q0s.t, Q1s.t):
            self.memset("pool", q0.h[64:128, :], 0.0, [q0.b])
            self.memset("pool", q1.h[0:64, :], 0.0, [q1.b])
        Wp = self.tile([128, 1152], F32, "Wp")
        T32 = self.tile([128, 1408], F32, "T32")
        Hr = self.tile([128, 1408], BF16, "Hr")
        Lr = self.tile([128, 1408], BF16, "Lr")
        Whi = self.tiles(2, [128, 1408], BF16, "Whi")
        Wlo = self.tiles(2, [128, 1408], BF16, "Wlo")
        Pm = self.tiles(3, [128, 1024], BF16, "P")
        acc = self.tile([128, 512], F32, "acc")
        hl2 = self.tiles(4, [128, 512], BF16, "hl2")
        fin = self.tiles(4, [128, 512], F32, "fin")
        a32 = self.tiles(2, [128, 512], F32, "a32")
        rl = self.tiles(2, [128, 1024], F32, "rl")
        o32 = self.tiles(2, [128, 512], F32, "o32")
        sq32 = self.tiles(1, [128, 512], F32, "sq32")
        ost = self.tiles(2, [128, 512], BF16, "ost")
        hls = self.tiles(2, [128, 512], BF16, "hlsB")
        Spair = Rot(self.pspair[0:2])
        L1 = self.psum[4]
        Fb = self.psum[5]
        Obank = self.psum[6:8]
        KB = [0, L + self.NO, 2 * L + self.NO]
        NG = L // 512

        def load_hs(h, s):
            nkt = NKT0 if s == 0 else T
            if s == 0:
                kt, v = KTbig, Vbig
            else:
                kt, v = KTsm.next(), Vsm.next()
            qt = (Q0s.next(), Q1s.next())
            t0 = KB[s] // 128
            step = 32
            for a in range(0, nkt, step):
                b = min(nkt, a + step)
                w, pw = ([kt.b], ()) if a == 0 else ((), [kt.b])
                self.load(kt.h[:, a * 128:b * 128], Sc["KT"][h, :, KB[s] + a * 128:KB[s] + b * 128], w, pw=pw)
                w, pw = ([v.b], ()) if a == 0 else ((), [v.b])
                self.load(v.h[:, a:b, :], Sc["VH"][h, :, t0 + a:t0 + b, :], w, pw=pw)
            self.load(qt[0].h[0:64, :], Sc["QT"][h, 0:64, s * L:(s + 1) * L], [qt[0].b])
            self.load(qt[1].h[64:128, :], Sc["QT"][h, 64:128, s * L:(s + 1) * L], [qt[1].b])
            return kt, v, qt

        def build_w(h):
            rb15 = self.rbb.h[:, 15 * 8 + h:15 * 8 + h + 1]
            rb31 = self.rbb.h[:, 31 * 8 + h:31 * 8 + h + 1]
            self.load(Wp.h[:], bass.AP(self.frev_t, h * 1280, [[1, 128], [1, 1152]]), [Wp.b])
            self.ts("dve", T32.h[:, 0:1152], Wp.h[:], 8.0, ALU.mult, [Wp.b], [T32.b])
            self.ts("dve", T32.h[:, 1152:1280], Wp.h[:, 640:768], rb15, ALU.subtract, [Wp.b, self.rbb.b], (), s2=8.0, op1=ALU.mult, pw=[T32.b])
            self.ts("dve", T32.h[:, 1280:1408], Wp.h[:, 384:512], rb31, ALU.subtract, [Wp.b, self.rbb.b], (), s2=8.0, op1=ALU.mult, pw=[T32.b])
            self.cp("dve", Hr.h[:], T32.h[:], [T32.b], [Hr.b])
            self.tt("dve", Lr.h[:], T32.h[:], Hr.h[:], ALU.subtract, [T32.b, Hr.b], [Lr.b])
            wh, wl = Whi.next(), Wlo.next()
            for (src, dst) in ((Hr, wh), (Lr, wl)):
                for c0 in (0, 512, 1024):
                    n = min(512, 1408 - c0)
                    ps = Spair.next()
                    self.mm(ps.h[:, 0:n], self.Jb, src.h[:, c0:c0 + n], True, True, [self.cstb.b, src.b], [ps.b])
                    w, pw = ([dst.b], ()) if c0 == 0 else ((), [dst.b])
                    self.cp("act", dst.h[:, c0:c0 + n], ps.h[:, 0:n], [ps.b], w, pw=pw)
            return wh, wl

        order = [(h, s) for h in range(8) for s in range(3)]
        nxt = load_hs(*order[0])
        wcur = None
        pending = [None]
        for oi, (h, s) in enumerate(order):
            kt, v, qt = nxt
            if s == 0:
                wcur = build_w(h)
            wh, wl = wcur
            if oi + 1 < len(order):
                nxt = load_hs(*order[oi + 1])
            nkt = NKT0 if s == 0 else T
            rb15 = self.rbb.h[:, 15 * 8 + h:15 * 8 + h + 1]
            rb31 = self.rbb.h[:, 31 * 8 + h:31 * 8 + h + 1]
            zero = self.sm.h[:, 22:23]
            steps = [(g, t) for g in range(NG) for t in range(nkt)]
            sps = {}

            def stageA(i):
                g, t = steps[i]
                band = None
                adj = None
                if t < T:
                    dt = t - 4 * g
                    if -1 <= dt <= 4:
                        band = 128 * (4 - dt)
                        bias = zero
                    elif dt < -1:
                        bias = rb15
                    else:
                        bias = rb31
                    bdep = [self.rbb.b, self.sm.b]
                else:
                    u = t - T
                    bias = self.fb.h[:, h * NOT + u:h * NOT + u + 1]
                    bdep = [self.fb.b]
                    if u == NOT - 2 and g == 0:
                        adj = (1152, 0)
                    if u == NOT - 1 and g == NG - 1:
                        adj = (1280, 384)
                sp = Spair.next()
                for m in range(2):
                    o = sp.h[:, m * 512:(m + 1) * 512]
                    last = (band is None and adj is None)
                    self.mm(o, kt.h[:, t * 128:(t + 1) * 128],
                            qt[m].h[:, g * 512:(g + 1) * 512], True, last, [kt.b, qt[m].b], [sp.b])
                    if band is not None:
                        self.mm(o, self.identb, wh.h[:, band:band + 512], False, False, [self.cstb.b, wh.b], [sp.b])
                        self.mm(o, self.identb, wl.h[:, band:band + 512], False, True, [self.cstb.b, wl.b], [sp.b])
                    if adj is not None:
                        wc, qc = adj
                        o2 = sp.h[:, m * 512 + qc:m * 512 + qc + 128]
                        self.mm(o2, self.identb, wh.h[:, wc:wc + 128], False, False, [self.cstb.b, wh.b], [sp.b])
                        self.mm(o2, self.identb, wl.h[:, wc:wc + 128], False, True, [self.cstb.b, wl.b], [sp.b])
                sps[i] = (sp, bias, bdep)

            def stageBC(i):
                g, t = steps[i]
                sp, bias, bdep = sps.pop(i)
                p = Pm.next()
                self.act(p.h[:], sp.h[:, :], AF.Exp, [sp.b] + bdep, [p.b], scale=0.125, bias=bias)
                if t == 0:
                    self.cp("dve", acc.h[:], p.h[:, 0:512], [p.b], [acc.b])
                else:
                    self.tt("dve", acc.h[:], acc.h[:], p.h[:, 0:512], ALU.add, [acc.b, p.b], [acc.b])
                for m in range(2):
                    self.mm(Obank[m].h[:, :], v.h[:, t, :], p.h[:, m * 512:(m + 1) * 512], t == 0, t == nkt - 1,
                            [v.b, p.b], [Obank[m].b])
                self.mm(L1.h[:, :], self.onesb, p.h[:, 512:1024], t == 0, t == nkt - 1, [self.cstb.b, p.b], [L1.b])

            def evac(g, h=h, s=s):
                o0s, o1s = fin.next(), fin.next()
                self.cp("dve", o0s.h[:], Obank[0].h[:, :], [Obank[0].b], [o0s.b])
                self.cp("act", o1s.h[:], Obank[1].h[:, :], [Obank[1].b], [o1s.b])
                r = rl.next()
                self.act(r.h[:, 0:512], L1.h[:, :], AF.Ln, [L1.b], [r.b])
                ah, al = hl2.next(), hl2.next()
                self.hilo(acc.h[:], acc.b, ah.h[:], ah.b, al.h[:], al.b)

                def rest():
                    self.mm(Fb.h[:, :], self.onesb, ah.h[:], True, False, [self.cstb.b, ah.b], [Fb.b])
                    self.mm(Fb.h[:, :], self.onesb, al.h[:], False, True, [self.cstb.b, al.b], [Fb.b])
                    self.act(r.h[:, 512:1024], Fb.h[:, :], AF.Ln, [Fb.b], (), pw=[r.b])
                    self.act(r.h[:], r.h[:], AF.Exp, [r.b], [r.b], scale=-1.0)
                    a0, a1 = a32.next(), a32.next()
                    self.tt("dve", a0.h[:], o0s.h[:], r.h[:, 512:1024], ALU.mult, [o0s.b, r.b], [a0.b])
                    self.tt("dve", a1.h[:], o1s.h[:], r.h[:, 0:512], ALU.mult, [o1s.b, r.b], [a1.b])
                    o = o32.next()
                    self.stt(o.h[:], a1.h[:], self.neglam, a0.h[:], ALU.mult, ALU.add, [a0.b, a1.b, self.sm.b], [o.b])
                    sq = sq32.next()
                    self.act(sq.h[:], o.h[:], AF.Square, [o.b], [sq.b])
                    sh, sl = hls.next(), hls.next()
                    self.hilo(sq.h[:], sq.b, sh.h[:], sh.b, sl.h[:], sl.b)
                    self.mm(Fb.h[:, :], self.onesb, sh.h[:], True, False, [self.cstb.b, sh.b], [Fb.b])
                    self.mm(Fb.h[:, :], self.onesb, sl.h[:], False, True, [self.cstb.b, sl.b], [Fb.b])
                    self.act(sq.h[:], Fb.h[:, :], AF.Ln, [Fb.b, epsT.b], [sq.b], scale=1.0 / 128, bias=epsT.h[:, 0:1])
                    self.act(sq.h[:], sq.h[:], AF.Exp, [sq.b], [sq.b], scale=-0.5)
                    ob = ost.next()
                    self.stt(ob.h[:], o.h[:], self.sm.h[:, 21:22], sq.h[:], ALU.mult, ALU.mult, [o.b, sq.b, self.sm.b], [ob.b])
                    self.store(Sc["OAT"][h, :, s * L + g * 512:s * L + (g + 1) * 512], ob.h[:], [ob.b])
                return rest

            for g in range(NG):
                base = g * nkt
                stageA(base)
                for i in range(nkt):
                    if i + 1 < nkt:
                        stageA(base + i + 1)
                    stageBC(base + i)
                    if i == 2 and pending[0] is not None:
                        pending[0]()
                        pending[0] = None
                assert pending[0] is None
                pending[0] = evac(g)
        if pending[0] is not None:
            pending[0]()
        self.mem.pers = self.pers_late
        self.mem.reset()

    def phaseC(self):
        nc, I, Sc, L, T, NOT = self.nc, self.I, self.Sc, self.L, self.T, self.NOT
        self.mk_eps()
        epsT = self.epsT
        mask2 = self.tile([128, 256], F32, "mask2")
        self.cp("dve", mask2.h[:, 0:128], self.triF, [self.cst.b], [mask2.b])
        self.cp("dve", mask2.h[:, 128:256], self.triB, [self.cst.b], (), pw=[mask2.b])
        sinp = self.tile([128, 8, 256], F32, "sinp")
        self.load(sinp.h[:], Sc["SIN"].rearrange("k p e -> p k e"), [sinp.b])
        qTs = self.tiles(2, [128, L], BF16, "qT")
        kTs = self.tiles(2, [128, L], BF16, "kT")
        ktms = self.tiles(1, [128, T, 128], BF16, "ktm")
        vtms = self.tiles(1, [128, T, 256], BF16, "vtm")
        gfs = [self.tiles(1, [128, 2, T, 128], BF16, "gf"), self.tiles(1, [128, 2, T, 128], BF16, "gb")]
        ogs = self.tiles(1, [128, 2, L], BF16, "og")
        ke = [self.tile([128, T, 128], BF16, "ke_f"), self.tile([128, T, 128], BF16, "ke_b")]
        qeT = [self.tile([128, L], BF16, "qeT_f"), self.tile([128, L], BF16, "qeT_b")]
        keT = [self.tile([128, L], BF16, "keT_f"), self.tile([128, L], BF16, "keT_b")]
        Ecol = [self.tile([128, T], F32, "E_f"), self.tile([128, T], F32, "E_b")]
        Sin = [[Tl(self.mem.alloc([128, 256], BF16, name="Sin"), f"Sin{d}_{c}") for c in range(T)] for d in range(2)]
        S32 = [self.tile([128, 256], F32, "S32f"), self.tile([128, 256], F32, "S32b")]
        e32 = self.tiles(3, [128, 512], F32, "e32")
        tmp = self.tiles(2, [128, 256], F32, "tmpc")
        aTm = self.tiles(3, [128, 256], BF16, "aTm")
        sq = self.tiles(2, [128, 2, 512], F32, "sqc")
        nrm = self.tiles(2, [128, 512], F32, "nrm")
        t32 = self.tiles(2, [128, 512], F32, "t32")
        ost = self.tiles(3, [128, 512], BF16, "ostc")
        tri = [self.triFb, self.triBb]
        hlsC = self.tiles(2, [128, 2, 512], BF16, "hlsC")
        NQ = T // 4

        def load_a(s, hd):
            q, k = qTs.next(), kTs.next()
            r = slice(s * L, (s + 1) * L)
            self.load(q.h[:, :], Sc["GQT"][hd, :, r], [q.b])
            self.load(k.h[:, :], Sc["GKT"][hd, :, r], [k.b])
            return q, k

        def load_b(s, hd):
            ktm, vtm, gf, gb, og = ktms.next(), vtms.next(), gfs[0].next(), gfs[1].next(), ogs.next()
            r = slice(s * L, (s + 1) * L)
            self.load(ktm.h[:], Sc["GK"][r, hd * 128:(hd + 1) * 128].rearrange("(t p) d -> p t d", p=128), [ktm.b])
            self.load(vtm.h[:], Sc["GV"][r, hd * 256:(hd + 1) * 256].rearrange("(t p) d -> p t d", p=128), [vtm.b])
            for (gt_, nm) in ((gf, "GGF"), (gb, "GGB")):
                self.load(gt_.h[:, 0, :, :], Sc[nm + "H"][r, hd * 128:(hd + 1) * 128].rearrange("(t p) d -> p t d", p=128), [gt_.b])
                self.load(gt_.h[:, 1, :, :], Sc[nm + "L"][r, hd * 128:(hd + 1) * 128].rearrange("(t p) d -> p t d", p=128), (), pw=[gt_.b])
            self.load(og.h[:], Sc["OGT"][2 * hd:2 * hd + 2, :, r].rearrange("j p t -> p j t"), [og.b])
            return ktm, vtm, gf, gb, og

        order = [(s, hd) for s in range(3) for hd in range(4)]
        nxt = load_a(*order[0])
        for oi, (s, hd) in enumerate(order):
            q, k = nxt
            ktm, vtm, gf, gb, og = load_b(s, hd)
            if oi + 1 < len(order):
                nxt = load_a(*order[oi + 1])
            gg = [gf, gb]
            for d in range(2):
                for qd in range(NQ):
                    cs = slice(qd * 512, (qd + 1) * 512)
                    ps = self.prot.next()
                    for cc in range(4):
                        c = qd * 4 + cc
                        for hl in range(2):
                            self.mm(ps.h[:, cc * 128:(cc + 1) * 128], tri[d], gg[d].h[:, hl, c, :], hl == 0, hl == 1, [self.cstb.b, gg[d].b], [ps.b])
                    em = e32.next()
                    self.act(em.h[:], ps.h[:, :], AF.Exp, [ps.b], [em.b], scale=-1.0)
                    self.tt("dve", ke[d].h[:, qd * 4:(qd + 1) * 4, :].rearrange("p t d -> p (t d)"),
                            ktm.h[:, qd * 4:(qd + 1) * 4, :].rearrange("p t d -> p (t d)"), em.h[:], ALU.mult,
                            [ktm.b, em.b], [ke[d].b])
                    ps = self.prot.next()
                    for cc in range(4):
                        c = qd * 4 + cc
                        for hl in range(2):
                            self.mm(ps.h[:, cc * 128:(cc + 1) * 128], gg[d].h[:, hl, c, :], tri[d], hl == 0, hl == 1, [self.cstb.b, gg[d].b], [ps.b])
                    ep = e32.next()
                    self.act(ep.h[:], ps.h[:, :], AF.Exp, [ps.b], [ep.b])
                    em2 = e32.next()
                    self.act(em2.h[:], ps.h[:, :], AF.Exp, [ps.b], [em2.b], scale=-1.0)
                    self.tt("dve", qeT[d].h[:, cs], q.h[:, cs], ep.h[:], ALU.mult, [q.b, ep.b], [qeT[d].b])
                    self.tt("dve", keT[d].h[:, cs], k.h[:, cs], em2.h[:], ALU.mult, [k.b, em2.b], [keT[d].b])
                    off = 127 if d == 0 else 0
                    self.cp("dve", Ecol[d].h[:, qd * 4:(qd + 1) * 4], ep.h[:].rearrange("p (c t) -> p c t", t=128)[:, :, off],
                            [ep.b], [Ecol[d].b])
            for d in range(2):
                if s == 0:
                    self.cp("dve", S32[d].h[:], sinp.h[:, hd * 2 + d, :], [sinp.b], [S32[d].b])
                else:
                    self.memset("dve", S32[d].h[:], 0.0, [S32[d].b])
                cords = list(range(T)) if d == 0 else list(range(T - 1, -1, -1))
                self.cp("act", Sin[d][cords[0]].h[:], S32[d].h[:], [S32[d].b], [Sin[d][cords[0]].b])
                for ci, c in enumerate(cords):
                    if ci == T - 1:
                        break
                    pu = self.prot.next()
                    self.mm(pu.h[:, 0:256], ke[d].h[:, c, :], vtm.h[:, c, :], True, True, [ke[d].b, vtm.b], [pu.b])
                    tm = tmp.next()
                    self.tt("dve", tm.h[:], pu.h[:, 0:256], S32[d].h[:], ALU.add, [pu.b, S32[d].b], [tm.b])
                    self.ts("dve", S32[d].h[:], tm.h[:], Ecol[d].h[:, c:c + 1], ALU.mult, [tm.b, Ecol[d].b], [S32[d].b])
                    nx = Sin[d][cords[ci + 1]]
                    self.act(nx.h[:], tm.h[:], AF.Identity, [tm.b, Ecol[d].b], [nx.b], scale=Ecol[d].h[:, c:c + 1])
            for qd in range(NQ):
                po = [self.prot.next(), self.prot.next()]
                for cc in range(4):
                    c = qd * 4 + cc
                    cs = slice(c * 128, (c + 1) * 128)
                    pa = self.prot.next()
                    for d in range(2):
                        self.mm(pa.h[:, d * 128:(d + 1) * 128], keT[d].h[:, cs], qeT[d].h[:, cs], True, True,
                                [keT[d].b, qeT[d].b], [pa.b])
                    am = aTm.next()
                    self.tt("dve", am.h[:], pa.h[:, 0:256], mask2.h[:], ALU.mult, [pa.b, mask2.b], [am.b])
                    for eb in range(2):
                        es = slice(eb * 128, (eb + 1) * 128)
                        o = po[eb].h[:, cc * 128:(cc + 1) * 128]
                        self.mm(o, vtm.h[:, c, es], am.h[:, 0:128], True, False, [vtm.b, am.b], [po[eb].b])
                        self.mm(o, vtm.h[:, c, es], am.h[:, 128:256], False, False, [vtm.b, am.b], [po[eb].b])
                        self.mm(o, Sin[0][c].h[:, es], qeT[0].h[:, cs], False, False, [Sin[0][c].b, qeT[0].b], [po[eb].b])
                        self.mm(o, Sin[1][c].h[:, es], qeT[1].h[:, cs], False, True, [Sin[1][c].b, qeT[1].b], [po[eb].b])
                sqt = sq.next()
                for eb in range(2):
                    self.act(sqt.h[:, eb, :], po[eb].h[:, :], AF.Square, [po[eb].b], [sqt.b])
                pn = self.prot.next()
                sh, sl = hlsC.next(), hlsC.next()
                self.hilo(sqt.h[:], sqt.b, sh.h[:], sh.b, sl.h[:], sl.b)
                for eb in range(2):
                    self.mm(pn.h[:, :], self.onesb, sh.h[:, eb, :], eb == 0, False, [self.cstb.b, sh.b], [pn.b])
                    self.mm(pn.h[:, :], self.onesb, sl.h[:, eb, :], False, eb == 1, [self.cstb.b, sl.b], [pn.b])
                nr = nrm.next()
                self.act(nr.h[:], pn.h[:, :], AF.Ln, [pn.b, epsT.b], [nr.b], scale=1.0 / 256, bias=epsT.h[:, 0:1])
                self.act(nr.h[:], nr.h[:], AF.Exp, [nr.b], [nr.b], scale=-0.5)
                for eb in range(2):
                    tt_ = t32.next()
                    self.stt(tt_.h[:], po[eb].h[:, :], self.sm.h[:, 19 + eb:20 + eb], nr.h[:], ALU.mult, ALU.mult,
                             [po[eb].b, nr.b, self.sm.b], [tt_.b])
                    ob = ost.next()
                    self.tt("dve", ob.h[:], tt_.h[:], og.h[:, eb, qd * 512:(qd + 1) * 512], ALU.mult, [tt_.b, og.b], [ob.b])
                    self.store(Sc["OBT"][2 * hd + eb, :, s * L + qd * 512:s * L + (qd + 1) * 512], ob.h[:], [ob.b])
        self.mem.reset()

    def phaseD1(self):
        nc, I, Sc, L, T = self.nc, self.I, self.Sc, self.L, self.T
        wba = self.tile([128, 8, D], BF16, "wba")
        wbb = self.tile([128, 8, D], BF16, "wbb")
        wout = self.tile([128, 8, D], BF16, "wout")
        self.wload(wba, I["w_ba"].rearrange("(kc p) c -> p kc c", p=128), 8, 0, D)
        self.wload(wbb, I["w_bb"].rearrange("(kc p) c -> p kc c", p=128), 8, 0, D)
        self.wload(wout, I["w_out"].rearrange("(kc p) c -> p kc c", p=128), 8, 0, D)
        self.mk_eps()
        gbc = self.tiles(1, [128, D], F32, "gbc")
        ins = [self.tiles(1, [128, 8, 512], BF16, nm) for nm in ("oaT", "obT", "gaT", "gbT")]
        xrot = self.tiles(4, [128, 1024], F32, "x")
        m32 = self.tiles(4, [128, 512], F32, "m32")
        mT = self.tiles(1, [128, 8, 512], BF16, "mT")
        x1s = self.tiles(4, [128, 1024], F32, "x1")
        t32 = self.tiles(2, [128, 512], F32, "t32")
        xnrot = self.tiles(4, [128, 1024], F32, "xn")
        strot = self.tiles(4, [128, 4], F32, "st")
        scr = self.tile([128, 1024], BF16, "sqscr")
        h2 = self.tiles(1, [128, 8, 512], BF16, "h2T")
        NG = L // 512
        groups = [(s, q) for s in range(3) for q in range(NG)]

        def loads_in(g):
            s, q = g
            r = slice(s * L + q * 512, s * L + (q + 1) * 512)
            out = []
            for rot, nm in zip(ins, ("OAT", "OBT", "GAT", "GBT")):
                t = rot.next()
                self.load(t.h[:], Sc[nm][:, :, r].rearrange("j p t -> p j t"), [t.b])
                out.append(t)
            return out

        def loads_x(g):
            s, q = g
            xs = []
            for i in range(4):
                xt = xrot.next()
                self.load(xt.h[:], I["xf"][s * L + q * 512 + i * 128:s * L + q * 512 + (i + 1) * 128, :], [xt.b])
                xs.append(xt)
            return xs

        nxt_in = loads_in(groups[0])
        nxt_x = loads_x(groups[0])
        gcur = None
        for gi, g in enumerate(groups):
            s, q = g
            (oaT, obT, gaT, gbT), xs = nxt_in, nxt_x
            if q == 0:
                gcur = gbc.next()
                self.load(gcur.h[:], Sc["GBC"][s, :, :], [gcur.b])
            m = mT.next()
            for cb in range(8):
                pa = self.prot.next()
                for kc in range(8):
                    self.mm(pa.h[:, :], wba.h[:, kc, cb * 128:(cb + 1) * 128], oaT.h[:, kc, :], kc == 0, kc == 7, [wba.b, oaT.b], [pa.b])
                pb = self.prot.next()
                for kc in range(8):
                    self.mm(pb.h[:, :], wbb.h[:, kc, cb * 128:(cb + 1) * 128], obT.h[:, kc, :], kc == 0, kc == 7, [wbb.b, obT.b], [pb.b])
                m1, m2 = m32.next(), m32.next()
                self.tt("dve", m1.h[:], pa.h[:, :], gaT.h[:, cb, :], ALU.mult, [pa.b, gaT.b], [m1.b])
                self.tt("dve", m2.h[:], pb.h[:, :], gbT.h[:, cb, :], ALU.mult, [pb.b, gbT.b], [m2.b])
                w, pw = ([m.b], ()) if cb == 0 else ((), [m.b])
                self.tt("pool", m.h[:, cb, :], m1.h[:], m2.h[:], ALU.add, [m1.b, m2.b], w, pw=pw)
            if gi + 1 < len(groups):
                nxt_in = loads_in(groups[gi + 1])
            x1l = []
            for i in range(4):
                x1 = x1s.next()
                for half in range(2):
                    hs = slice(half * 512, (half + 1) * 512)
                    pz = self.prot.next()
                    for kc in range(8):
                        self.mm(pz.h[:, :], m.h[:, kc, i * 128:(i + 1) * 128], wout.h[:, kc, hs], kc == 0, kc == 7, [m.b, wout.b], [pz.b])
                    t_ = t32.next()
                    self.tt("dve", t_.h[:], pz.h[:, :], gcur.h[:, hs], ALU.mult, [pz.b, gcur.b], [t_.b])
                    w, pw = ([x1.b], ()) if half == 0 else ((), [x1.b])
                    self.tt("pool", x1.h[:, hs], t_.h[:], xs[i].h[:, hs], ALU.add, [t_.b, xs[i].b], w, pw=pw)
                r0 = s * L + q * 512 + i * 128
                self.store(Sc["X1"][r0:r0 + 128, :], x1.h[:], [x1.b])
                x1l.append(x1)
            if gi + 1 < len(groups):
                nxt_x = loads_x(groups[gi + 1])
            hT = h2.next()
            self.norm_group(x1l, s, 2, hT, scr, xnrot, strot)
            self.store(Sc["H2T"][:, :, s * L + q * 512:s * L + (q + 1) * 512].rearrange("j p t -> p j t"), hT.h[:], [hT.b])
        self.mem.reset()

    def phaseD2(self):
        nc, I, Sc, L, T = self.nc, self.I, self.Sc, self.L, self.T
        wups = [self.tile([128, 8, 1024], BF16, "wup") for _ in range(4)]
        wdn = self.tile([128, 32, D], BF16, "wdn")
        for qi in range(4):
            self.wload(wups[qi], I["w_up"].rearrange("(kc p) c -> p kc c", p=128), 8, qi * 1024, (qi + 1) * 1024, 0)
        self.wload(wdn, I["w_dn"].rearrange("(kc p) c -> p kc c", p=128), 32, 0, D)
        gbc = self.tiles(1, [128, D], F32, "gbc2")
        h2 = self.tiles(2, [128, 8, 256], BF16, "h2")
        uT = self.tile([128, 32, 256], BF16, "uT")
        r32 = self.tiles(2, [128, 256], F32, "r32")
        x1s = self.tiles(4, [128, 1024], F32, "x1b")
        t32 = self.tiles(1, [128, 512], F32, "t32b")
        NG = L // 256
        groups = [(s, q) for s in range(3) for q in range(NG)]

        def loads(g):
            s, q = g
            r0 = s * L + q * 256
            h = h2.next()
            self.load(h.h[:], Sc["H2T"][:, :, r0:r0 + 256].rearrange("j p t -> p j t"), [h.b])
            xs = []
            for i in range(2):
                xt = x1s.next()
                self.load(xt.h[:], Sc["X1"][r0 + i * 128:r0 + (i + 1) * 128, :], [xt.b])
                xs.append(xt)
            return h, xs

        nxt = loads(groups[0])
        gcur = None
        for gi, g in enumerate(groups):
            s, q = g
            h, xs = nxt
            if q == 0:
                gcur = gbc.next()
                self.load(gcur.h[:], Sc["GBC"][3 + s, :, :], [gcur.b])
            if gi + 1 < len(groups):
                nxt = loads(groups[gi + 1])
            for fc in range(32):
                pu = self.prot.next()
                for kc in range(8):
                    wq = wups[fc // 8]
                    self.mm(pu.h[:, 0:256], wq.h[:, kc, (fc % 8) * 128:(fc % 8 + 1) * 128], h.h[:, kc, :], kc == 0, kc == 7, [wq.b, h.b], [pu.b])
                r = r32.next()
                self.act(r.h[:], pu.h[:, 0:256], AF.Relu, [pu.b], [r.b])
                w, pw = ([uT.b], ()) if fc == 0 else ((), [uT.b])
                self.tt("dve", uT.h[:, fc, :], r.h[:], r.h[:], ALU.mult, [r.b], w, pw=pw)
            for i in range(2):
                y = xs[i]
                for half in range(2):
                    hs = slice(half * 512, (half + 1) * 512)
                    pz = self.prot.next()
                    for fc in range(32):
                        self.mm(pz.h[:, :], uT.h[:, fc, i * 128:(i + 1) * 128], wdn.h[:, fc, hs], fc == 0, fc == 31, [uT.b, wdn.b], [pz.b])
                    t_ = t32.next()
                    self.tt("dve", t_.h[:], pz.h[:, :], gcur.h[:, hs], ALU.mult, [pz.b, gcur.b], [t_.b])
                    self.tt("pool", y.h[:, hs], t_.h[:], xs[i].h[:, hs], ALU.add, [t_.b, xs[i].b], [y.b])
                r0 = s * L + q * 256 + i * 128
                self.store(self.y[r0:r0 + 128, :], y.h[:], [y.b])
        self.mem.reset()


def _t5_bucket_np(rel):
    nb = 16
    max_exact = 8
    ret = (rel > 0).astype(np.int32) * nb
    n = np.abs(rel)
    nf = np.maximum(n, 1).astype(np.float32)
    large = max_exact + (np.log(nf / np.float32(max_exact)) / np.float32(math.log(128 / max_exact))
                         * np.float32(nb - max_exact)).astype(np.int32)
    large = np.minimum(large, nb - 1)
    return ret + np.where(n < max_exact, n, large)


def _consts():
    p = np.arange(128)
    ident = np.eye(128, dtype=np.float32)
    J = ident[::-1].copy()
    triF = (p[:, None] <= p[None, :]).astype(np.float32)
    triB = (p[:, None] >= p[None, :]).astype(np.float32)
    blk = ((p[:, None] // 64) == (p[None, :] // 64)).astype(np.float32)
    ones = np.ones((128, 128), np.float32)
    cst = np.concatenate([ident, J, triF, triB, blk, ones], axis=1)
    i = np.arange(1280)
    bk = _t5_bucket_np((639 - i).astype(np.int32))
    oh = (bk[None, :] == np.arange(32)[:, None]).astype(np.float32)
    return np.ascontiguousarray(cst), np.ascontiguousarray(oh)


def make_in_maps(L, inp):
    T = L // 128
    NOT = 7 * T + 1
    f = lambda a: np.ascontiguousarray(np.asarray(a, dtype=np.float32))
    xp = f(inp["x_prompt"])[0]
    xs = f(inp["x_sample"])
    cp = f(inp["c_prompt"])[0]
    cs = f(inp["c_sample"])
    cst, oh = _consts()
    fmT = lambda v: np.ascontiguousarray(v.reshape(-1, 128).T)
    shared = {
        "w_ada": f(inp["w_ada"])[0], "b_adaT": fmT(f(inp["b_ada"])[0]), "b_ada": f(inp["b_ada"])[0][None, :],
        "n1T": fmT(f(inp["norm1_g"])[0]), "n2T": fmT(f(inp["norm2_g"])[0]), "w_in": f(inp["w_in"])[0],
        "qkg": np.ascontiguousarray(np.stack([np.tile(f(inp["q_norm_g"])[0], 2), np.tile(f(inp["k_norm_g"])[0], 2)], axis=1)),
        "lamv": np.concatenate([f(inp["lam_q1"])[0], f(inp["lam_k1"])[0], f(inp["lam_q2"])[0], f(inp["lam_k2"])[0]])[None, :],
        "subg": f(inp["subln_g"])[0][:, None],
        "glng": fmT(f(inp["gla_norm_g"])[0]),
        "w_ba": f(inp["w_branch_a"])[0], "w_bb": f(inp["w_branch_b"])[0], "w_out": f(inp["w_out"])[0],
        "w_up": f(inp["w_up"])[0], "w_dn": f(inp["w_down"])[0],
        "rb": f(inp["rel_bias"]), "rbrow": f(inp["rel_bias"]).reshape(1, 256), "oh": oh, "cst": cst,
    }
    z16 = np.zeros((16, 512), np.float32)
    shared["wgf"] = np.ascontiguousarray(np.concatenate([f(inp["w_gate_f"])[0], z16, f(inp["b_gate_f"])], axis=0))
    shared["wgb"] = np.ascontiguousarray(np.concatenate([z16, f(inp["w_gate_b"])[0], f(inp["b_gate_b"])], axis=0))
    maps = []
    xpt = xp.reshape(8 * T, 128, D)
    for c in range(8):
        m = dict(shared)
        m["xf"] = np.ascontiguousarray(np.concatenate([xp[c * L:(c + 1) * L], xs[2 * c], xs[2 * c + 1]], axis=0))
        post = list(range(8 * T - 1, (c + 1) * T, -1))
        pre = list(range(0, c * T - 1))
        xo = np.zeros((NOT, 128, D), np.float32)
        flg = np.zeros((3, NOT), np.float32)
        u = 0
        for t in post:
            xo[u] = xpt[t]; flg[1, u] = 1; u += 1
        for t in pre:
            xo[u] = xpt[t]; flg[0, u] = 1; u += 1
        while u < NOT - 2:
            flg[2, u] = 1; u += 1
        if c > 0:
            xo[NOT - 2] = xpt[c * T - 1]; flg[0, NOT - 2] = 1
        else:
            flg[2, NOT - 2] = 1
        if c < 7:
            xo[NOT - 1] = xpt[(c + 1) * T]; flg[1, NOT - 1] = 1
        else:
            flg[2, NOT - 1] = 1
        m["xo"] = np.ascontiguousarray(xo.reshape(NOT * 128, D))
        m["flg"] = np.ascontiguousarray(np.broadcast_to(flg.reshape(1, 3 * NOT), (128, 3 * NOT)))
        cc = np.zeros((4, D), np.float32)
        cc[0] = cp; cc[1] = cs[2 * c]; cc[2] = cs[2 * c + 1]
        m["cT"] = np.ascontiguousarray(cc.reshape(4, 8, 128).transpose(2, 1, 0).reshape(128, 32))
        maps.append(m)
    return maps


_CACHE = {}


def run(L, inp, debug=False, nphase=99):
    key = (L, debug, nphase)
    if key not in _CACHE:
        k = K(L, debug=debug, nphase=nphase)
        nc = k.build()
        from contextlib import ExitStack
        st = ExitStack()
        k.S.emit(nc, st)
        st.close()
        _CACHE[key] = (k, nc)
    k, nc = _CACHE[key]
    maps = make_in_maps(L, inp)
    res = run_bass_kernel_spmd(nc, maps, core_ids=list(range(8)))
    return k, res


def kernel(**inputs):
    L = 2048
    k, res = run(L, inputs)
    yp = np.concatenate([res.results[c]["y"][0:L] for c in range(8)], axis=0)[None]
    ys = np.stack([res.results[c]["y"][L * (1 + j):L * (2 + j)] for c in range(8) for j in range(2)], axis=0)
    return (np.ascontiguousarray(yp, dtype=np.float32), np.ascontiguousarray(ys, dtype=np.float32))
```
